# Optimizing a Trainium2 kernel written in Bass

```python
import jax, jax.numpy as jnp
from jax import lax
import numpy as np

D_MODEL = 2048
BATCH = 4
SEQ = 2048
DEPTH = 1
DEC_BATCH = 128
DEC_SEQ = 1
PAST_LEN = 16384
PAGE_SIZE = 128

N_HEADS = 32
N_KV_HEADS = 8
HEAD_DIM = D_MODEL // N_HEADS
GQA_GROUP = N_HEADS // N_KV_HEADS
WINDOW = 128
ROPE_THETA = 10000.0
ATTN_WIDTH = N_HEADS * HEAD_DIM
KV_WIDTH = N_KV_HEADS * HEAD_DIM
SSM_EXPAND = 2
D_INNER = SSM_EXPAND * D_MODEL
SSM_HEAD_DIM = 64
N_SSM_HEADS = D_INNER // SSM_HEAD_DIM
D_STATE = 128
N_SSM_GROUPS = 8
HEADS_PER_GROUP = N_SSM_HEADS // N_SSM_GROUPS
CONV_W = 4
CONV_DIM = D_INNER + 2 * N_SSM_GROUPS * D_STATE
CHUNK = 128
N_BRANCH = 2
D_FF = 256 * (-(-8 * D_MODEL // (3 * 256)))
EPS = 1e-6
SPLITS = (ATTN_WIDTH,
          ATTN_WIDTH + KV_WIDTH,
          ATTN_WIDTH + 2 * KV_WIDTH,
          ATTN_WIDTH + 2 * KV_WIDTH + D_INNER,
          ATTN_WIDTH + 2 * KV_WIDTH + D_INNER + CONV_DIM,
          ATTN_WIDTH + 2 * KV_WIDTH + D_INNER + CONV_DIM + N_SSM_HEADS)
IN_PROJ_DIM = ATTN_WIDTH + 2 * KV_WIDTH + D_INNER + CONV_DIM + N_SSM_HEADS + N_BRANCH * D_MODEL

kernel_name = 'hybrid_ssd_swa_sink_step'


def rms_norm(x, w):
    xf = x.astype(jnp.float32)
    y = xf * lax.rsqrt(jnp.mean(xf * xf, axis=-1, keepdims=True) + EPS)
    return (y * w.astype(jnp.float32)).astype(x.dtype)


def gated_group_rms_norm(y, z, w):
    g = (y.astype(jnp.float32) * jax.nn.silu(z.astype(jnp.float32)))
    g = g.reshape(g.shape[:-1] + (N_SSM_GROUPS, D_INNER // N_SSM_GROUPS))
    g = g * lax.rsqrt(jnp.mean(g * g, axis=-1, keepdims=True) + EPS)
    return (g.reshape(y.shape) * w.astype(jnp.float32)).astype(y.dtype)


def rope(x, pos):
    half = HEAD_DIM // 2
    inv = ROPE_THETA ** (-jnp.arange(half, dtype=jnp.float32) / half)
    ang = pos.astype(jnp.float32)[:, None] * inv[None, :]
    cos = jnp.cos(ang)[None, :, None, :]
    sin = jnp.sin(ang)[None, :, None, :]
    xf = x.astype(jnp.float32)
    x1, x2 = xf[..., :half], xf[..., half:]
    return jnp.concatenate([x1 * cos - x2 * sin, x2 * cos + x1 * sin], axis=-1).astype(x.dtype)


def sink_softmax(s, mask, sink):
    s = jnp.where(mask, s, -jnp.inf)
    sk = jnp.broadcast_to(sink.astype(jnp.float32).reshape(N_KV_HEADS, GQA_GROUP, 1, 1), s.shape[:-1] + (1,))
    p = jax.nn.softmax(jnp.concatenate([s, sk], axis=-1), axis=-1)
    return p[..., :-1]


def window_attn_prompt(q, k, v, sink):
    b, s = q.shape[:2]
    nb = s // WINDOW
    qb = q.reshape(b, nb, WINDOW, N_KV_HEADS, GQA_GROUP, HEAD_DIM)
    pad = jnp.zeros((b, WINDOW, N_KV_HEADS, HEAD_DIM), k.dtype)
    kp = jnp.concatenate([pad, k], axis=1).reshape(b, nb + 1, WINDOW, N_KV_HEADS, HEAD_DIM)
    vp = jnp.concatenate([pad, v], axis=1).reshape(b, nb + 1, WINDOW, N_KV_HEADS, HEAD_DIM)
    kb = jnp.concatenate([kp[:, :-1], kp[:, 1:]], axis=2)
    vb = jnp.concatenate([vp[:, :-1], vp[:, 1:]], axis=2)
    scores = jnp.einsum('bnqkgd,bnskd->bnkgqs', qb, kb, preferred_element_type=jnp.float32) * (HEAD_DIM ** -0.5)
    blk = jnp.arange(nb)[:, None, None]
    qpos = blk * WINDOW + jnp.arange(WINDOW)[None, :, None]
    kpos = (blk - 1) * WINDOW + jnp.arange(2 * WINDOW)[None, None, :]
    mask = (kpos <= qpos) & (qpos - kpos < WINDOW) & (kpos >= 0)
    p = sink_softmax(scores, mask[None, :, None, None], sink)
    o = jnp.einsum('bnkgqs,bnskd->bnqkgd', p.astype(v.dtype), vb)
    return o.reshape(b, s, ATTN_WIDTH)


def window_attn_sample(q, k_new, v_new, cache_k, cache_v, sink):
    b, t = q.shape[:2]
    cw = cache_k.shape[1]
    kc = jnp.concatenate([cache_k, k_new], axis=1)
    vc = jnp.concatenate([cache_v, v_new], axis=1)
    qpos = PAST_LEN + jnp.arange(t)
    kpos = PAST_LEN - cw + jnp.arange(cw + t)
    mask = (kpos[None, :] <= qpos[:, None]) & (qpos[:, None] - kpos[None, :] < WINDOW)
    qg = q.reshape(b, t, N_KV_HEADS, GQA_GROUP, HEAD_DIM)
    scores = jnp.einsum('bqkgd,bskd->bkgqs', qg, kc, preferred_element_type=jnp.float32) * (HEAD_DIM ** -0.5)
    p = sink_softmax(scores, mask, sink)
    o = jnp.einsum('bkgqs,bskd->bqkgd', p.astype(vc.dtype), vc).reshape(b, t, ATTN_WIDTH)
    return o, kc[:, -cw:], vc[:, -cw:]


def causal_conv(xbc, conv_state, w, bias):
    xp = jnp.concatenate([conv_state, xbc], axis=1)
    y = lax.conv_general_dilated(xp, w[:, None, :], window_strides=(1,), padding='VALID',
                                 dimension_numbers=('NWC', 'WIO', 'NWC'), feature_group_count=CONV_DIM)
    return jax.nn.silu(y + bias), xp[:, -(CONV_W - 1):]


def ssd_chunked(x, dt, a, bm, cm):
    b, l = x.shape[:2]
    nc = l // CHUNK
    X = (x.astype(jnp.float32) * dt[..., None]).reshape(b, nc, CHUNK, N_SSM_GROUPS, HEADS_PER_GROUP, SSM_HEAD_DIM)
    dA = (dt * a).reshape(b, nc, CHUNK, N_SSM_GROUPS, HEADS_PER_GROUP)
    Bc = bm.astype(jnp.float32).reshape(b, nc, CHUNK, N_SSM_GROUPS, D_STATE)
    Cc = cm.astype(jnp.float32).reshape(b, nc, CHUNK, N_SSM_GROUPS, D_STATE)
    acum = jnp.cumsum(dA, axis=2)
    causal = jnp.tril(jnp.ones((CHUNK, CHUNK), dtype=bool))[None, None, :, :, None, None]
    seg = acum[:, :, :, None] - acum[:, :, None, :]
    lmat = jnp.exp(jnp.where(causal, seg, -jnp.inf))
    cb = jnp.einsum('bclgn,bcsgn->bclsg', Cc, Bc)
    y_diag = jnp.einsum('bclsgr,bcsgrp->bclgrp', cb[..., None] * lmat, X)
    decay_states = jnp.exp(acum[:, :, -1:] - acum)
    states = jnp.einsum('bclgn,bclgrp->bcgrpn', Bc, X * decay_states[..., None])
    chunk_decay = jnp.exp(acum[:, :, -1])

    def step(h, inp):
        dec, st = inp
        return h * dec[..., None, None] + st, h

    h0 = jnp.zeros((b, N_SSM_GROUPS, HEADS_PER_GROUP, SSM_HEAD_DIM, D_STATE), jnp.float32)
    h_last, h_prev = lax.scan(step, h0, (jnp.moveaxis(chunk_decay, 1, 0), jnp.moveaxis(states, 1, 0)))
    h_prev = jnp.moveaxis(h_prev, 0, 1)
    y_off = jnp.einsum('bclgn,bcgrpn->bclgrp', Cc, h_prev) * jnp.exp(acum)[..., None]
    y = (y_diag + y_off).reshape(b, l, N_SSM_HEADS, SSM_HEAD_DIM)
    return y, h_last.reshape(b, N_SSM_HEADS, SSM_HEAD_DIM, D_STATE)


def ssd_recurrent(x, dt, a, bm, cm, h0):
    b, l = x.shape[:2]
    X = (x.astype(jnp.float32) * dt[..., None]).reshape(b, l, N_SSM_GROUPS, HEADS_PER_GROUP, SSM_HEAD_DIM)
    da = jnp.exp(dt * a).reshape(b, l, N_SSM_GROUPS, HEADS_PER_GROUP)
    h = h0.astype(jnp.float32).reshape(b, N_SSM_GROUPS, HEADS_PER_GROUP, SSM_HEAD_DIM, D_STATE)

    def step(h, inp):
        xt, dat, bt, ct = inp
        h = h * dat[..., None, None] + jnp.einsum('bgrp,bgn->bgrpn', xt, bt)
        return h, jnp.einsum('bgrpn,bgn->bgrp', h, ct)

    seq = (jnp.moveaxis(X, 1, 0), jnp.moveaxis(da, 1, 0),
           jnp.moveaxis(bm.astype(jnp.float32), 1, 0), jnp.moveaxis(cm.astype(jnp.float32), 1, 0))
    h_last, ys = lax.scan(step, h, seq)
    y = jnp.moveaxis(ys, 0, 1).reshape(b, l, N_SSM_HEADS, SSM_HEAD_DIM)
    return y, h_last.reshape(b, N_SSM_HEADS, SSM_HEAD_DIM, D_STATE)


def token_mixers(hn, pos, lw, state):
    b, l = hn.shape[:2]
    proj = jnp.einsum('bld,de->ble', hn, lw['w_in'])
    q, k, v, z, xbc, dt_raw, gates = jnp.split(proj, SPLITS, axis=-1)
    q = rope(q.reshape(b, l, N_HEADS, HEAD_DIM), pos)
    k = rope(k.reshape(b, l, N_KV_HEADS, HEAD_DIM), pos)
    v = v.reshape(b, l, N_KV_HEADS, HEAD_DIM)
    if state is None:
        attn = window_attn_prompt(q, k, v, lw['attn_sinks'])
        win = min(WINDOW, l)
        new_k, new_v = k[:, -win:], v[:, -win:]
        conv_in = jnp.zeros((b, CONV_W - 1, CONV_DIM), xbc.dtype)
    else:
        attn, new_k, new_v = window_attn_sample(q, k, v, state['k'], state['v'], lw['attn_sinks'])
        conv_in = state['conv']
    xbc_act, new_conv = causal_conv(xbc, conv_in, lw['conv_w'], lw['conv_b'])
    xs, bm, cm = jnp.split(xbc_act, (D_INNER, D_INNER + N_SSM_GROUPS * D_STATE), axis=-1)
    xs = xs.reshape(b, l, N_SSM_HEADS, SSM_HEAD_DIM)
    bm = bm.reshape(b, l, N_SSM_GROUPS, D_STATE)
    cm = cm.reshape(b, l, N_SSM_GROUPS, D_STATE)
    dt = jax.nn.softplus(dt_raw.astype(jnp.float32) + lw['dt_bias'].astype(jnp.float32))
    a = -jnp.exp(lw['a_log'].astype(jnp.float32))
    if state is None:
        y, h_last = ssd_chunked(xs, dt, a, bm, cm)
    else:
        y, h_last = ssd_recurrent(xs, dt, a, bm, cm, state['ssm'])
    y = y + lw['d_skip'].astype(jnp.float32)[:, None] * xs.astype(jnp.float32)
    y = gated_group_rms_norm(y.reshape(b, l, D_INNER).astype(hn.dtype), z, lw['ssm_norm'])
    attn_d = jnp.einsum('ble,ed->bld', attn, lw['w_attn_branch'])
    ssm_d = jnp.einsum('ble,ed->bld', y, lw['w_ssm_branch'])
    g_attn, g_ssm = jnp.split(gates, N_BRANCH, axis=-1)
    merged = jax.nn.sigmoid(g_attn) * attn_d + jax.nn.sigmoid(g_ssm) * ssm_d
    out = jnp.einsum('bld,de->ble', merged, lw['w_out'])
    return out, (new_k, new_v, new_conv, h_last.astype(hn.dtype))


def swiglu_ffn(h, w_gate_up, w_down):
    g, u = jnp.split(jnp.einsum('bld,df->blf', h, w_gate_up), 2, axis=-1)
    return jnp.einsum('blf,fd->bld', jax.nn.silu(g) * u, w_down)


def decoder_layer(x, pos, lw, state):
    mix, new_state = token_mixers(rms_norm(x, lw['norm_mix_pre']), pos, lw, state)
    h = x + rms_norm(mix, lw['norm_mix_post'])
    f = swiglu_ffn(rms_norm(h, lw['norm_ffn_pre']), lw['w_gate_up'], lw['w_down'])
    return h + rms_norm(f, lw['norm_ffn_post']), new_state


def setup_inputs(seed: int = 0) -> dict:
    key = jax.random.key(seed)
    ks = jax.random.split(key, 24)
    f32 = jnp.float32
    cache_w = min(WINDOW, PAST_LEN)
    nrm = lambda k, shape, s: jax.random.normal(k, shape, f32) * s
    dt0 = jnp.exp(jax.random.uniform(ks[10], (DEPTH, N_SSM_HEADS), f32, np.log(1e-3), np.log(1e-1)))
    return {
        'x_prompt': nrm(ks[0], (BATCH, SEQ, D_MODEL), 1.0),
        'x_sample': nrm(ks[1], (DEC_BATCH, DEC_SEQ, D_MODEL), 1.0),
        'cache_win_k': nrm(ks[2], (DEPTH, DEC_BATCH, cache_w, N_KV_HEADS, HEAD_DIM), 1.0),
        'cache_win_v': nrm(ks[3], (DEPTH, DEC_BATCH, cache_w, N_KV_HEADS, HEAD_DIM), 1.0),
        'state_conv': nrm(ks[4], (DEPTH, DEC_BATCH, CONV_W - 1, CONV_DIM), 1.0),
        'state_ssm': nrm(ks[5], (DEPTH, DEC_BATCH, N_SSM_HEADS, SSM_HEAD_DIM, D_STATE), 0.5),
        'norm_mix_pre': 1.0 + nrm(ks[6], (DEPTH, D_MODEL), 0.02),
        'norm_mix_post': 1.0 + nrm(ks[7], (DEPTH, D_MODEL), 0.02),
        'w_in': nrm(ks[8], (DEPTH, D_MODEL, IN_PROJ_DIM), D_MODEL ** -0.5),
        'attn_sinks': nrm(ks[9], (DEPTH, N_HEADS), 1.0),
        'w_attn_branch': nrm(ks[11], (DEPTH, ATTN_WIDTH, D_MODEL), ATTN_WIDTH ** -0.5),
        'conv_w': nrm(ks[12], (DEPTH, CONV_W, CONV_DIM), CONV_W ** -0.5),
        'conv_b': nrm(ks[13], (DEPTH, CONV_DIM), 0.01),
        'dt_bias': dt0 + jnp.log(-jnp.expm1(-dt0)),
        'a_log': jnp.log(jax.random.uniform(ks[14], (DEPTH, N_SSM_HEADS), f32, 1.0, 16.0)),
        'd_skip': 1.0 + nrm(ks[15], (DEPTH, N_SSM_HEADS), 0.1),
        'ssm_norm': 1.0 + nrm(ks[16], (DEPTH, D_INNER), 0.02),
        'w_ssm_branch': nrm(ks[17], (DEPTH, D_INNER, D_MODEL), D_INNER ** -0.5),
        'w_out': nrm(ks[18], (DEPTH, D_MODEL, D_MODEL), D_MODEL ** -0.5),
        'norm_ffn_pre': 1.0 + nrm(ks[19], (DEPTH, D_MODEL), 0.02),
        'norm_ffn_post': 1.0 + nrm(ks[20], (DEPTH, D_MODEL), 0.02),
        'w_gate_up': nrm(ks[21], (DEPTH, D_MODEL, 2 * D_FF), D_MODEL ** -0.5),
        'w_down': nrm(ks[22], (DEPTH, D_FF, D_MODEL), D_FF ** -0.5),
    }


def reference(x_prompt, x_sample, cache_win_k, cache_win_v, state_conv, state_ssm,
              norm_mix_pre, norm_mix_post, w_in, attn_sinks, w_attn_branch, conv_w, conv_b,
              dt_bias, a_log, d_skip, ssm_norm, w_ssm_branch, w_out,
              norm_ffn_pre, norm_ffn_post, w_gate_up, w_down):
    pos_p = jnp.arange(x_prompt.shape[1], dtype=jnp.int32)
    pos_s = PAST_LEN + jnp.arange(x_sample.shape[1], dtype=jnp.int32)
    yp, ys = x_prompt, x_sample
    pk, pv, pc, ph, sk, sv, sc, sh = [], [], [], [], [], [], [], []
    for i in range(DEPTH):
        lw = {'norm_mix_pre': norm_mix_pre[i], 'norm_mix_post': norm_mix_post[i], 'w_in': w_in[i],
              'attn_sinks': attn_sinks[i], 'w_attn_branch': w_attn_branch[i], 'conv_w': conv_w[i],
              'conv_b': conv_b[i], 'dt_bias': dt_bias[i], 'a_log': a_log[i], 'd_skip': d_skip[i],
              'ssm_norm': ssm_norm[i], 'w_ssm_branch': w_ssm_branch[i], 'w_out': w_out[i],
              'norm_ffn_pre': norm_ffn_pre[i], 'norm_ffn_post': norm_ffn_post[i],
              'w_gate_up': w_gate_up[i], 'w_down': w_down[i]}
        yp, (k1, v1, c1, h1) = decoder_layer(yp, pos_p, lw, None)
        st = {'k': cache_win_k[i], 'v': cache_win_v[i], 'conv': state_conv[i], 'ssm': state_ssm[i]}
        ys, (k2, v2, c2, h2) = decoder_layer(ys, pos_s, lw, st)
        pk.append(k1); pv.append(v1); pc.append(c1); ph.append(h1)
        sk.append(k2); sv.append(v2); sc.append(c2); sh.append(h2)
    return (yp, ys,
            jnp.stack(pk), jnp.stack(pv), jnp.stack(pc), jnp.stack(ph),
            jnp.stack(sk), jnp.stack(sv), jnp.stack(sc), jnp.stack(sh))
```

```python
import numpy as np
import concourse.bass as bass
import concourse.mybir as mybir

F32 = mybir.dt.float32
BF16 = mybir.dt.bfloat16
AF = mybir.ActivationFunctionType
ALU = mybir.AluOpType
AX = mybir.AxisListType

KQ = 12
STRICT_SAME_ENGINE = True


class _Op(object):
    __slots__ = ("eng", "fn", "reads", "writes", "dma", "deps", "sig", "ev", "qidx", "waits")


class Prog(object):
    ENGS = ("pe", "act", "dve", "pool", "sp")

    def __init__(self, nc):
        self.nc = nc
        self.ops = []
        self.nbar = 0

    def op(self, eng, fn, reads=(), writes=()):
        o = _Op()
        o.eng = eng; o.fn = fn; o.reads = tuple(reads); o.writes = tuple(writes)
        o.dma = False; o.sig = False; o.ev = None; o.qidx = -1
        self.ops.append(o)
        return o

    def dma(self, q, out, in_, reads=(), writes=()):
        o = _Op()
        o.eng = q
        o.fn = (lambda e, out=out, in_=in_: e.dma_start(out=out, in_=in_))
        o.reads = tuple(reads); o.writes = tuple(writes)
        o.dma = True; o.sig = True; o.ev = None; o.qidx = -1
        self.ops.append(o)
        return o

    def barrier(self, tiny):
        n = self.nbar
        self.nbar += 1
        dq = [("dq", q, i) for q in ("sp", "act", "pool") for i in range(KQ)]
        for e in self.ENGS:
            o = self.op(e, tiny[e], reads=(dq + ["scr"] if e == "sp" else ["scr"]), writes=[("bar1", n, e)])
            if e == "sp":
                o.dma = True; o.sig = True
        for e in self.ENGS:
            o = self.op(e, tiny[e], reads=[("bar1", n, e2) for e2 in self.ENGS], writes=[("bar2", n, e)])
            if e == "sp":
                o.dma = True; o.sig = True

    def finalize(self, stack):
        nc = self.nc
        ops = self.ops
        esem = {e: stack.enter_context(nc.semaphore("se_" + e)) for e in self.ENGS}
        dsem = {q: [stack.enter_context(nc.semaphore("sd_%s%d" % (q, i))) for i in range(KQ)]
                for q in ("sp", "act", "pool")}
        semobj = {}
        for e in self.ENGS:
            semobj[("e", e)] = esem[e]
        for q in dsem:
            for i in range(KQ):
                semobj[("d", q, i)] = dsem[q][i]

        last_w = {}
        readers = {}
        dq_hist = {"sp": [], "act": [], "pool": []}
        for j, op in enumerate(ops):
            deps = {}
            for k in op.reads:
                i = last_w.get(k)
                if i is not None:
                    deps[i] = True
            for k in op.writes:
                i = last_w.get(k)
                if i is not None:
                    o = ops[i]
                    if o.dma or op.dma or o.eng != op.eng or (STRICT_SAME_ENGINE and op.eng != "pe"):
                        deps[i] = True
                for i in readers.get(k, {}).values():
                    o = ops[i]
                    if o.dma or op.dma or o.eng != op.eng or (STRICT_SAME_ENGINE and op.eng != "pe"):
                        deps[i] = True
            if op.dma:
                h = dq_hist[op.eng]
                op.qidx = len(h)
                op.writes = op.writes + (("dq", op.eng, op.qidx % KQ),)
                if op.qidx >= KQ:
                    deps[h[op.qidx - KQ]] = True
                h.append(j)
            deps.pop(j, None)
            op.deps = sorted(deps, reverse=True)
            rid = (op.eng, op.qidx % KQ) if op.dma else op.eng
            for k in op.reads:
                readers.setdefault(k, {})[rid] = j
            for k in op.writes:
                last_w[k] = j
                readers[k] = {}
        for op in ops:
            for i in op.deps:
                ops[i].sig = True
        cnt = {e: 0 for e in self.ENGS}
        for op in ops:
            if op.dma:
                op.ev = (("d", op.eng, op.qidx % KQ), 16 * (op.qidx // KQ + 1))
            elif op.sig:
                cnt[op.eng] += 1
                op.ev = (("e", op.eng), cnt[op.eng])
        know = {e: {} for e in self.ENGS}
        snap = {}
        nw = 0
        for op in ops:
            kn = know[op.eng]
            waits = []
            for i in op.deps:
                sid, val = ops[i].ev
                if kn.get(sid, 0) >= val:
                    continue
                waits.append((sid, val))
                for s, v in snap[(sid, val)].items():
                    if kn.get(s, 0) < v:
                        kn[s] = v
            op.waits = waits
            nw += len(waits)
            if op.sig:
                s = dict(kn)
                s[op.ev[0]] = op.ev[1]
                snap[op.ev] = s
        self.stats = dict(n_ops=len(ops), n_waits=nw, cnt=dict(cnt),
                          ndma={q: len(h) for q, h in dq_hist.items()})
        final_waits = []
        for q, h in dq_hist.items():
            for j in h[-KQ:]:
                final_waits.append(ops[j].ev)

        block = stack.enter_context(nc.Block())

        def emit(e, eng, extra=None):
            for op in ops:
                if op.eng != eng:
                    continue
                for sid, val in op.waits:
                    e.wait_ge(semobj[sid], val)
                ins = op.fn(e)
                if op.sig:
                    if op.dma:
                        ins.then_inc(semobj[op.ev[0]], 16)
                    else:
                        ins.then_inc(semobj[op.ev[0]], 1)
            if extra:
                for sid, val in extra:
                    e.wait_ge(semobj[sid], val)

        @block.tensor
        def _(e):
            emit(e, "pe")

        @block.scalar
        def _(e):
            emit(e, "act")

        @block.vector
        def _(e):
            emit(e, "dve")

        @block.gpsimd
        def _(e):
            emit(e, "pool")

        @block.sync
        def _(e):
            emit(e, "sp", final_waits)


from contextlib import ExitStack
from concourse.bass_utils import run_bass_kernel_spmd

D = 2048; NH = 32; NKV = 8; HD = 64; DI = 4096; NSH = 64; NST = 128; NG = 8
CONVD = 6144; DFF = 5632; EPS = 1e-6
OQ = 0; OK_ = 2048; OV = 2560; OZ = 3072; OX = 7168; OB = 11264; OC = 12288; ODT = 13312; OGA = 13376; OGS = 15424
INP = 17472
TP = 512
NS = 16
NEG = -30000.0
SB_BYTES = 212480


def _prod(s):
    r = 1
    for v in s:
        r *= v
    return r


class Arena(object):
    def __init__(self, nc, stack, name, nbytes):
        self.t = stack.enter_context(nc.sbuf_tensor(name, [128, nbytes // 4], F32))
        self.ap = self.t[:]
        self.cap = nbytes
        self.top = 0

    def alloc(self, free_shape, dtype, parts=128):
        n = _prod(free_shape)
        nb = n * (4 if dtype == F32 else 2)
        nb = (nb + 63) // 64 * 64
        off = self.top
        self.top += nb
        assert self.top <= self.cap, ("SBUF arena overflow", self.top, self.cap)
        v = self.ap[:, off // 4:(off + nb) // 4]
        if dtype != F32:
            v = v.bitcast(dtype)
        v = v[:, 0:n]
        if len(free_shape) == 2:
            v = v.rearrange("p (a b) -> p a b", a=free_shape[0])
        elif len(free_shape) == 3:
            v = v.rearrange("p (a b c) -> p a b c", a=free_shape[0], b=free_shape[1])
        return v

    def mark(self):
        return self.top

    def release(self, m):
        self.top = m


def build_program(dbg=None, opts=()):
    nc = bass.Bass("TRN2", target_bir_lowering=False)
    st = ExitStack()
    P = Prog(nc)

    def din(name, shape):
        return nc.dram_tensor(name, list(shape), F32, kind="ExternalInput").ap()

    def dout(name, shape):
        return nc.dram_tensor(name, list(shape), F32, kind="ExternalOutput").ap()

    x_all = din("x_all", [2048, D])
    x_smp = din("x_smp", [NS, D])
    ck = din("ck", [NS, 128, NKV, HD]); cv = din("cv", [NS, 128, NKV, HD])
    sconv = din("sconv", [NS, 3, CONVD]); sssm = din("sssm", [NS, NSH, HD, NST])
    w_in = din("w_in", [D, INP]); w_ab = din("w_ab", [D, D]); w_sb = din("w_sb", [DI, D])
    w_o = din("w_o", [D, D]); w_gu = din("w_gu", [D, 2 * DFF]); w_dn = din("w_dn", [DFF, D])
    nw_in = din("nw", [128, 4, 16])
    convw_in = din("convw", [128, 48, 4]); convb_in = din("convb", [128, 48])
    dtb_in = din("dtb", [8, 8]); alog_in = din("alog", [1, 64]); dsk_in = din("dsk", [128, 32])
    snw_in = din("snw", [128, 32]); sink_in = din("sinks", [1, 32])
    cos_in = din("cosT", [128, 4, TP + NS]); sin_in = din("sinT", [128, 4, TP + NS])
    cst_in = din("consts", [128, 6, 128])
    msk_in = din("masks", [128, 2, 256])
    flag_in = din("flag", [128, 1])

    y_out = dout("y_own", [1024, D]); ys_out = dout("y_smp", [NS, D])
    wk_out = dout("wk", [128, 512]); wv_out = dout("wv", [128, 512])
    cv_out = dout("convo", [3, CONVD]); ss_out = dout("ssmo", [DI, NST])
    wks_out = dout("wks", [NS, 128, 512]); wvs_out = dout("wvs", [NS, 128, 512])
    cvs_out = dout("convs", [NS, 3, CONVD]); sss_out = dout("ssms", [NS, NSH, HD, NST])
    scr_x = nc.dram_tensor("scr_x", [NS, CONVD], F32).ap()
    scr_dt = nc.dram_tensor("scr_dt", [NS, 64], F32).ap()
    scr_y = nc.dram_tensor("scr_y", [NS, DI], F32).ap()
    scr_q = nc.dram_tensor("scr_q", [NS, D], F32).ap()
    scr_k = nc.dram_tensor("scr_k", [NS, 512], F32).ap()
    scr_v = nc.dram_tensor("scr_v", [NS, 512], F32).ap()
    scr_o = nc.dram_tensor("scr_o", [NS, D], F32).ap()
    dbg_out = {}
    if dbg:
        for k, shp in dbg.items():
            dbg_out[k] = dout("dbg_" + k, shp)

    A = Arena(nc, st, "arena", SB_BYTES)
    cst = A.alloc([6, 128], F32)
    ident = cst[:, 0, :]; triU = cst[:, 1, :]; ones_f = cst[:, 2, :]; rotR = cst[:, 3, :]; ssdmask = cst[:, 4, :]
    cst_b = A.alloc([6, 128], BF16)
    ident_b = cst_b[:, 0, :]; ones_b = cst_b[:, 2, :]; ssdmask_b = cst_b[:, 4, :]
    masks = A.alloc([2, 256], F32)
    nw = A.alloc([4, 16], F32)
    convw = A.alloc([48, 4], F32); convb = A.alloc([48], F32)
    dtb = A.alloc([8], F32); dsk = A.alloc([32], F32); snw = A.alloc([32], F32)
    negA = A.alloc([64], F32)
    sink8 = A.alloc([32], F32)
    flag = A.alloc([1], F32)
    cosT = A.alloc([544], F32); sinT = A.alloc([544], F32)
    scr = A.alloc([128], F32)
    kTc = A.alloc([4, 128], BF16); Vc = A.alloc([512], BF16)
    convc = A.alloc([48, 3], F32)
    hT = A.alloc([NG, 512], F32)
    NSLOT = 2
    wslots = [A.alloc([8192], BF16) for _ in range(NSLOT)]
    hnT = A.alloc([16, 544], BF16)
    base_mark = A.mark()

    ps = [st.enter_context(nc.psum_tensor("ps%d" % i, [128, 512], F32))[:] for i in range(8)]
    psn = [0]

    def PS():
        i = psn[0] % 8
        psn[0] += 1
        return ps[i], "ps%d" % i

    tiny = {
        "pe": lambda e: e.matmul(ps[7][0:1, 0:1], lhsT=cst_b[0:1, 0, 0:1], rhs=cst_b[0:1, 0, 0:1], start=True, stop=True),
        "act": lambda e: e.activation(out=scr[0:1, 0:1], in_=scr[0:1, 16:17], func=AF.Copy),
        "dve": lambda e: e.memset(scr[0:1, 32:33], 0.0),
        "pool": lambda e: e.memset(scr[0:1, 48:49], 0.0),
        "sp": lambda e: e.dma_start(out=scr[0:1, 64:72], in_=scr[0:1, 96:104]),
    }

    def barrier():
        P.barrier(tiny)

    P.dma("sp", cst, cst_in, writes=["cst"])
    P.dma("pool", cst_b, cst_in, writes=["cst_b"])
    P.dma("sp", masks, msk_in, writes=["masks"])
    P.dma("sp", nw, nw_in, writes=["nw"])
    P.dma("sp", convw, convw_in, writes=["convw"]); P.dma("sp", convb, convb_in, writes=["convb"])
    P.dma("sp", dtb[0:8, :], dtb_in, writes=["dtb"]); P.dma("sp", dsk, dsk_in, writes=["dsk"])
    P.dma("sp", snw, snw_in, writes=["snw"]); P.dma("sp", flag, flag_in, writes=["flag"])
    P.dma("sp", negA, alog_in.partition_broadcast(128), writes=["negA"])
    P.dma("sp", sink8, sink_in.partition_broadcast(128), writes=["sink8"])
    P.op("act", lambda e: e.activation(out=negA, in_=negA, func=AF.Exp), reads=["negA"], writes=["negA"])
    P.op("dve", lambda e: e.tensor_scalar(out=negA, in0=negA, scalar1=-1.0, scalar2=None, op0=ALU.mult), reads=["negA"], writes=["negA"])
    P.op("dve", lambda e: e.tensor_scalar(out=sink8, in0=sink8, scalar1=8.0, scalar2=None, op0=ALU.mult), reads=["sink8"], writes=["sink8"])
    P.op("dve", lambda e: e.memset(hT, 0.0), writes=["hT"])
    P.op("dve", lambda e: e.memset(scr, 0.0), writes=["scr"])
    P.op("dve", lambda e: e.memset(kTc, 0.0), writes=["kTc"])
    P.op("dve", lambda e: e.memset(Vc, 0.0), writes=["Vc"])
    P.op("dve", lambda e: e.memset(convc, 0.0), writes=["convc"])


    T = 544
    TASKS = []

    CUR = [TASKS]

    def task(loads, compute):
        CUR[0].append((loads, compute))

    pss_n = [0]

    def PSS():
        i = psn[0] % 7
        psn[0] += 1
        return ps[i][:, 0:16], "ps%d" % i

    def PS7():
        i = psn[0] % 7
        psn[0] += 1
        return ps[i], "ps%d" % i

    def ranges(Tn):
        return [(0, TP)] if Tn == TP else [(0, Tn // 2), (Tn // 2, Tn)]

    def op(eng, fn, reads, writes):
        return P.op(eng, fn, reads=reads, writes=writes)

    def ACT(out, in_, func, reads, writes, **kw):
        op("act", lambda e: e.activation(out=out, in_=in_, func=func, **kw), reads, writes)

    def TT(eng, out, in0, in1, o, reads, writes):
        op(eng, lambda e: e.tensor_tensor(out=out, in0=in0, in1=in1, op=o), reads, writes)

    def TS(eng, out, in0, s1, s2, o0, o1, reads, writes):
        if o1 is None:
            op(eng, lambda e: e.tensor_scalar(out=out, in0=in0, scalar1=s1, scalar2=None, op0=o0), reads, writes)
        else:
            op(eng, lambda e: e.tensor_scalar(out=out, in0=in0, scalar1=s1, scalar2=s2, op0=o0, op1=o1), reads, writes)

    def STT(eng, out, in0, sc, in1, o0, o1, reads, writes):
        op(eng, lambda e: e.scalar_tensor_tensor(out=out, in0=in0, scalar=sc, in1=in1, op0=o0, op1=o1), reads, writes)

    def MM(out, lhsT, rhs, start, stop, reads, writes):
        op("pe", lambda e: e.matmul(out, lhsT=lhsT, rhs=rhs, start=start, stop=stop), reads, writes)

    def TR(out, in_, idn, reads, writes):
        op("pe", lambda e: e.transpose(out, in_, idn), reads, writes)

    def CP(eng, out, in_, reads, writes):
        if eng == "act":
            ACT(out, in_, AF.Copy, reads, writes)
        else:
            op(eng, lambda e: e.tensor_copy(out=out, in_=in_), reads, writes)

    stg = A.alloc([8, 16], F32)
    stg_n = [0]

    def proj(M, lhsT_of, rhs_of, nk, Tn, rd):
        res = []
        for (c0, c1) in ranges(Tn):
            if c1 - c0 > 16:
                pt, kp = PS7()
            else:
                pt, kp = PSS()
            o = pt[0:M, 0:c1 - c0]
            for kc in range(nk):
                MM(o, lhsT_of(kc), rhs_of(kc, c0, c1), kc == 0, kc == nk - 1, rd, [kp])
            if c1 - c0 <= 16:
                i = stg_n[0] % 8
                stg_n[0] += 1
                so = stg[0:M, i, 0:c1 - c0]
                CP("act", so, o, [kp], ["stg%d" % i])
                res.append((so, "stg%d" % i, c0, c1))
            else:
                res.append((o, kp, c0, c1))
        return res

    def wview(slot, nk, cw):
        return slot[:, 0:nk * cw].rearrange("p (k c) -> p k c", k=nk)

    def wloads_plain(W, c0, cw, nk, col_off=0, tot=None, sfx_default="a"):
        tot = tot or cw
        def f(slot):
            v = wview(slot, nk, tot)
            src = W[:, c0:c0 + cw].rearrange("(k p) c -> p k c", p=128)
            out = []
            step = max(1, 4096 // (cw * 4) * 4) if cw * 4 < 1024 else 4
            step = min(nk, max(4, step))
            if cw >= 256 and col_off == 0 and tot == cw:
                hw = cw // 2
                for (ca, sfx) in ((0, "a"), (hw, "b")):
                    for k0 in range(0, nk, nk // 2):
                        out.append((v[:, k0:k0 + nk // 2, ca:ca + hw], src[:, k0:k0 + nk // 2, ca:ca + hw], sfx))
                return out
            for k0 in range(0, nk, step):
                k1 = min(nk, k0 + step)
                out.append((v[:, k0:k1, col_off:col_off + cw], src[:, k0:k1, :], sfx_default))
            return out
        return f

    def fm_rstd(src, ksrc, Tn, sq, rstd, nfeat):
        ACT(sq[:, :, 0:Tn], src[:, :, 0:Tn], AF.Square, [ksrc], ["sq"])
        res = proj(128, lambda kc: ones_b, lambda kc, c0, c1: sq[:, kc, c0:c1], 16, Tn, ["sq", "cst_b"])
        for (o, kp, c0, c1) in res:
            TS("dve", rstd[:, c0:c1], o, 1.0 / nfeat, EPS, ALU.mult, ALU.add, [kp], ["rstd"])
        ACT(rstd[:, 0:Tn], rstd[:, 0:Tn], AF.Ln, ["rstd"], ["rstd"])
        ACT(rstd[:, 0:Tn], rstd[:, 0:Tn], AF.Exp, ["rstd"], ["rstd"], scale=-0.5)

    def emit_pass(pi, full, x_row0, has_s, y_row0, first_block_mask, dbgtag=None, kv=True):
        Tn = TP + (NS if has_s else 0)
        last = (pi == 3)
        pm = A.mark()

        def s01(slot=None, sk=None):
            m = A.mark()
            xT = A.alloc([16, T], F32); sq = A.alloc([16, T], BF16); rstd = A.alloc([T], F32)
            xtok = [A.alloc([D], F32) for _ in range(2)]
            tiles = [(x_all[x_row0 + t * 128:x_row0 + (t + 1) * 128, :], 128, t * 128) for t in range(4)]
            if has_s:
                tiles.append((x_smp, NS, TP))
            P.dma("sp", cosT[:, 0:TP + NS], cos_in[:, pi, :], writes=["cosT"])
            P.dma("sp", sinT[:, 0:TP + NS], sin_in[:, pi, :], writes=["sinT"])
            for ti, (src, n, c0) in enumerate(tiles):
                xt = xtok[ti % 2]; kx = "xtok%d" % (ti % 2)
                P.dma("sp", xt[0:n, :], src, writes=[kx])
                for q in range(4):
                    pt, kp = PS7()
                    for jj in range(4):
                        c = q * 4 + jj
                        TR(pt[:, jj * 128:jj * 128 + n], xt[0:n, c * 128:(c + 1) * 128], ident[0:n, 0:n], [kx, "cst"], [kp])
                    CP("act", xT[:, q * 4:(q + 1) * 4, c0:c0 + n],
                       pt.rearrange("p (a b) -> p a b", a=4)[:, :, 0:n], [kp], ["xT"])
            fm_rstd(xT, "xT", Tn, sq, rstd, D)
            for kc in range(16):
                STT("dve", hnT[:, kc, 0:Tn], xT[:, kc, 0:Tn], nw[:, 0, kc:kc + 1], rstd[:, 0:Tn],
                    ALU.mult, ALU.mult, ["xT", "rstd", "nw"], ["hnT"])
            if dbgtag and "hnT" in dbg_out:
                hd = A.alloc([16, T], F32)
                CP("dve", hd, hnT, ["hnT"], ["hd"])
                P.dma("sp", dbg_out["hnT"], hd, reads=["hd"])
            barrier()
            A.release(m)
        task(None, s01)

        B = {}

        att_list = []; ssd_list = []
        CUR[0] = att_list

        def s2_alloc():
            barrier()
            A.release(B["m_ssd"])
            B["attnT"] = A.alloc([16, T], BF16)
            B["m_att"] = A.mark()
            B["qT"] = A.alloc([16, T], BF16)
            B["kT"] = A.alloc([4, 128 + T], BF16)
            B["kfl"] = A.alloc([4, T], F32)
            B["Vt"] = A.alloc([5, 512], BF16)
            B["Vlast"] = A.alloc([512], F32)
            B["vs_tok"] = A.alloc([512], F32)
            B["qsf"] = A.alloc([16, NS], F32)
            for nm in ("qf", "qr", "t1", "t2"):
                B[nm] = A.alloc([T], F32)
            CP("dve", B["kT"][:, :, 0:128], kTc, ["kTc"], ["kT"])
            CP("dve", B["Vt"][:, 0, :], Vc, ["Vc"], ["Vt0"])
        task(None, s2_alloc)

        def rope_chunk(res, kq):
            qf, qr, t1, t2 = B["qf"], B["qr"], B["t1"], B["t2"]
            for (o, kp, c0, c1) in res:
                CP("act", qf[:, c0:c1], o, [kp], ["qf"])
            if "no_rope" in opts:
                CP("dve", qr[:, 0:Tn], qf[:, 0:Tn], ["qf"], ["qr"])
                return
            for (c0, c1) in ranges(Tn):
                pt, kp2 = PS7() if c1 - c0 > 16 else PSS()
                o2 = pt[:, 0:c1 - c0]
                for a0 in range(c0, c1, 256):
                    a1 = min(c1, a0 + 256)
                    MM(o2[:, a0 - c0:a1 - c0], rotR, qf[:, a0:a1], True, True, ["qf", "cst"], [kp2])
                if c1 - c0 <= 16:
                    i = stg_n[0] % 8
                    stg_n[0] += 1
                    CP("act", stg[:, i, 0:c1 - c0], o2, [kp2], ["stg%d" % i])
                    TT("dve", t2[:, c0:c1], stg[:, i, 0:c1 - c0], sinT[:, c0:c1], ALU.mult, ["stg%d" % i, "sinT"], ["t2"])
                else:
                    TT("dve", t2[:, c0:c1], o2, sinT[:, c0:c1], ALU.mult, [kp2, "sinT"], ["t2"])
            TT("pool", t1[:, 0:Tn], qf[:, 0:Tn], cosT[:, 0:Tn], ALU.mult, ["qf", "cosT"], ["t1"])
            TT("pool", qr[:, 0:Tn], t1[:, 0:Tn], t2[:, 0:Tn], ALU.add, ["t1", "t2"], ["qr"])

        if full and "no_q" not in opts:
            for jb in range(4):
                def loads_q(slot, jb=jb):
                    v = wview(slot, 16, 512).rearrange("p k (i h d) -> p k i h d", i=4, h=2)
                    src = w_in[:, OQ + jb * 512:OQ + (jb + 1) * 512].rearrange("(k p) (h i d) -> p k i h d", p=128, h=2, i=4)
                    out = []
                    for i in range(4):
                        for h in range(2):
                            out.append((v[:, :, i, h, :], src[:, :, i, h, :], "a" if i < 2 else "b"))
                    return out

                def comp_q(slot, sk, jb=jb):
                    wv_ = wview(slot, 16, 512)
                    for i in range(4):
                        c = 4 * jb + i
                        res = proj(128, lambda kc: wv_[:, kc, i * 128:(i + 1) * 128], lambda kc, c0, c1: hnT[:, kc, c0:c1], 16, Tn, ["hnT", sk + ("a" if i < 2 else "b")])
                        rope_chunk(res, "q")
                        CP("act", B["qT"][:, c, 0:Tn], B["qr"][:, 0:Tn], ["qr"], ["qT"])
                        if has_s:
                            CP("dve", B["qsf"][:, c, :], B["qr"][:, TP:Tn], ["qr"], ["qsf"])
                task(loads_q, comp_q)

        def comp_k(slot, sk):
            wv_ = wview(slot, 16, 512)
            for c in range(4):
                res = proj(128, lambda kc: wv_[:, kc, c * 128:(c + 1) * 128], lambda kc, c0, c1: hnT[:, kc, c0:c1], 16, Tn, ["hnT", sk + ("a" if c < 2 else "b")])
                rope_chunk(res, "k")
                CP("act", B["kT"][:, c, 128:128 + Tn], B["qr"][:, 0:Tn], ["qr"], ["kT"])
                CP("dve", B["kfl"][:, c, 0:Tn], B["qr"][:, 0:Tn], ["qr"], ["kfl"])
        if "no_k" not in opts:
            task(wloads_plain(w_in, OK_, 512, 16), comp_k)

        def comp_v(slot, sk):
            wv_ = wview(slot, 16, 512)
            for tt in range(4):
                pt, kp = PS7()
                for kc in range(16):
                    MM(pt, hnT[:, kc, tt * 128:(tt + 1) * 128], wv_[:, kc, :], kc == 0, kc == 15, ["hnT", sk + "a", sk + "b"], [kp])
                CP("act", B["Vt"][:, 1 + tt, :], pt, [kp], ["Vt%d" % (1 + tt)])
                if tt == 3 and "no_vlast" not in opts:
                    CP("act", B["Vlast"], pt, [kp], ["Vlast"])
            if has_s:
                pt, kp = PS7()
                for kc in range(16):
                    MM(pt[0:NS, :], hnT[:, kc, TP:Tn], wv_[:, kc, :], kc == 0, kc == 15, ["hnT", sk + "a", sk + "b"], [kp])
                CP("act", B["vs_tok"][0:NS, :], pt[0:NS, :], [kp], ["vs_tok"])
        if "no_v" not in opts:
            task(wloads_plain(w_in, OV, 512, 16), comp_v)

        def s3(slot=None, sk=None):
            m = A.mark()
            s_sb = A.alloc([2, 256], F32); Pb = A.alloc([2, 256], BF16); PT = A.alloc([2, 2, 128], BF16)
            rmx = A.alloc([2], F32); ngm = A.alloc([2], F32); rsm = A.alloc([2], F32); es = A.alloc([2], F32)
            atok = A.alloc([D], BF16)
            qT, kT, Vt = B["qT"], B["kT"], B["Vt"]
            for blk in range(4):
                mk = masks[:, 1, :] if (blk == 0 and first_block_mask) else masks[:, 0, :]
                for hp in range(16):
                    pS, kS = PS7()
                    hs = (2 * hp, 2 * hp + 1)
                    for u, h in enumerate(hs):
                        kvh = h // 4; g = h % 4; jj = kvh // 2; half = kvh % 2
                        cq = 4 * jj + g
                        MM(pS[:, u * 256:(u + 1) * 256], qT[half * 64:(half + 1) * 64, cq, blk * 128:(blk + 1) * 128],
                           kT[half * 64:(half + 1) * 64, jj, blk * 128:blk * 128 + 256], True, True, ["qT", "kT"], [kS])
                    TT("dve", s_sb, pS.rearrange("p (a b) -> p a b", a=2), mk.unsqueeze(1).to_broadcast([128, 2, 256]), ALU.add,
                       [kS, "masks"], ["s_sb"])
                    op("dve", lambda e: e.tensor_reduce(out=rmx, in_=s_sb, axis=AX.X, op=ALU.max), ["s_sb"], ["rmx"])
                    TT("dve", rmx, rmx, sink8[:, 2 * hp:2 * hp + 2], ALU.max, ["rmx", "sink8"], ["rmx"])
                    TS("dve", ngm, rmx, -0.125, None, ALU.mult, None, ["rmx"], ["ngm"])
                    for u in range(2):
                        ACT(Pb[:, u, :], s_sb[:, u, :], AF.Exp, ["s_sb", "ngm"], ["Pb", "rsm"], scale=0.125, bias=ngm[:, u:u + 1], accum_out=rsm[:, u:u + 1])
                    TT("dve", es, sink8[:, 2 * hp:2 * hp + 2], rmx, ALU.subtract, ["rmx", "sink8"], ["es"])
                    ACT(es, es, AF.Exp, ["es"], ["es"], scale=0.125)
                    TT("dve", es, es, rsm, ALU.add, ["es", "rsm"], ["es"])
                    op("dve", lambda e: e.reciprocal(out=es, in_=es), ["es"], ["es"])
                    pT, kT_ = PS7()
                    pTb = pT.bitcast(BF16)
                    for u in range(2):
                        for kb in range(2):
                            TR(pTb[:, (u * 2 + kb) * 128:(u * 2 + kb + 1) * 128], Pb[:, u, kb * 128:(kb + 1) * 128], ident_b, ["Pb", "cst_b"], [kT_])
                    CP("act", PT, pTb[:, 0:512].rearrange("p (a b c) -> p a b c", a=2, b=2), [kT_], ["PT"])
                    pO, kO = PS7()
                    for u, h in enumerate(hs):
                        kvh = h // 4
                        for kb in range(2):
                            MM(pO[:, u * 64:(u + 1) * 64], PT[:, u, kb, :], Vt[:, blk + kb, kvh * 64:(kvh + 1) * 64], kb == 0, kb == 1,
                               ["PT", "Vt%d" % (blk + kb)], [kO])
                    for u, h in enumerate(hs):
                        ACT(atok[:, h * 64:(h + 1) * 64], pO[:, u * 64:(u + 1) * 64], AF.Copy, [kO, "es"], ["atok"], scale=es[:, u:u + 1])
                for q2 in range(2):
                    pT, kT_ = PS7()
                    pTb = pT.bitcast(BF16)
                    for c8 in range(8):
                        c = q2 * 8 + c8
                        TR(pTb[:, c8 * 128:(c8 + 1) * 128], atok[:, c * 128:(c + 1) * 128], ident_b, ["atok", "cst_b"], [kT_])
                    CP("act", B["attnT"][:, q2 * 8:(q2 + 1) * 8, blk * 128:(blk + 1) * 128], pTb.rearrange("p (a b) -> p a b", a=8), [kT_], ["attnT"])
            A.release(m)
            if dbgtag and "attnT" in dbg_out:
                hd = A.alloc([8, T], F32)
                for hh in range(2):
                    CP("dve", hd, B["attnT"][:, hh * 8:(hh + 1) * 8, :], ["attnT"], ["hd"])
                    P.dma("sp", dbg_out["attnT"][:, hh * 8:(hh + 1) * 8, :], hd[:, :, 0:TP + NS], reads=["hd"], writes=["hdo"])
        if full and "no_s3" not in opts:
            task(None, s3)

        def s3_carry(slot=None, sk=None):
            CP("dve", kTc, B["kT"][:, :, TP:TP + 128], ["kT"], ["kTc"])
            CP("dve", Vc, B["Vt"][:, 4, :], ["Vt4"], ["Vc"])
            if last:
                P.dma("sp", wv_out, B["Vlast"], reads=["Vlast"], writes=["wvo"])
                pt, kp = PS7()
                for c in range(4):
                    TR(pt[:, c * 128:(c + 1) * 128], B["kfl"][:, c, TP - 128:TP], ident, ["kfl", "cst"], [kp])
                CP("act", B["qf"][:, 0:512], pt, [kp], ["qf"])
                P.dma("sp", wk_out, B["qf"][:, 0:512], reads=["qf"], writes=["wko"])
        task(None, s3_carry)

        def smp_attn(slot=None, sk=None):
            qtok = A.alloc([D], F32); ktok = A.alloc([512], F32)
            for q4 in range(4):
                pt, kp = PS7()
                for jj in range(4):
                    TR(pt[0:NS, jj * 128:(jj + 1) * 128], B["qsf"][:, q4 * 4 + jj, :], ident, ["qsf", "cst"], [kp])
                for jj in range(4):
                    CP("act", qtok[0:NS, (8 * q4 + jj) * 64:(8 * q4 + jj + 1) * 64], pt[0:NS, jj * 128:jj * 128 + 64], [kp], ["qtok"])
                    CP("act", qtok[0:NS, (8 * q4 + 4 + jj) * 64:(8 * q4 + 5 + jj) * 64], pt[0:NS, jj * 128 + 64:(jj + 1) * 128], [kp], ["qtok"])
            pt, kp = PS7()
            for c in range(4):
                TR(pt[0:NS, c * 128:(c + 1) * 128], B["kfl"][:, c, TP:Tn], ident, ["kfl", "cst"], [kp])
            CP("act", ktok[0:NS, :], pt[0:NS, :], [kp], ["ktok"])
            P.dma("sp", scr_q, qtok[0:NS, :], reads=["qtok"], writes=["scr_q"])
            P.dma("sp", scr_k, ktok[0:NS, :], reads=["ktok"], writes=["scr_k"])
            P.dma("sp", scr_v, B["vs_tok"][0:NS, :], reads=["vs_tok"], writes=["scr_v"])
            P.dma("sp", wks_out[:, 127, :], ktok[0:NS, :], reads=["ktok"], writes=["wks1"])
            P.dma("sp", wvs_out[:, 127, :], B["vs_tok"][0:NS, :], reads=["vs_tok"], writes=["wvs1"])
            P.dma("sp", wks_out[:, 0:127, :], ck[:, 1:128, :, :].rearrange("b s k d -> b s (k d)"), writes=["wks0"])
            P.dma("sp", wvs_out[:, 0:127, :], cv[:, 1:128, :, :].rearrange("b s k d -> b s (k d)"), writes=["wvs0"])
            barrier()
            A.release(B["m_att"])
            qb = A.alloc([256], F32); kb_ = A.alloc([64], F32); vb = A.alloc([64], F32); sk8 = A.alloc([4], F32)
            P.dma("sp", qb, scr_q.rearrange("b (k f) -> (b k) f", k=8), reads=["scr_q"], writes=["qb"])
            P.dma("sp", kb_, scr_k.rearrange("b (k f) -> (b k) f", k=8), reads=["scr_k"], writes=["kb_"])
            P.dma("sp", vb, scr_v.rearrange("b (k f) -> (b k) f", k=8), reads=["scr_v"], writes=["vb"])
            for b in range(NS):
                P.dma("pool", sk8[b * 8:(b + 1) * 8, :], sink_in[0:1, :].rearrange("o (k g) -> (o k) g", g=4), writes=["sk8"])
            TS("dve", sk8, sk8, 8.0, None, ALU.mult, None, ["sk8"], ["sk8"])
            cbuf = [A.alloc([64, 64], F32) for _ in range(2)]
            prod = A.alloc([64, 64], F32)
            sc = A.alloc([4, 128], F32); pp = A.alloc([4, 128], F32)
            rmx = A.alloc([4], F32); ngm = A.alloc([4], F32); rsm = A.alloc([4], F32); es = A.alloc([4], F32)
            oacc = A.alloc([4, 64], F32); opart = A.alloc([4, 64], F32)
            for hf in range(2):
                cb_ = cbuf[hf]; kc_ = "cbuf%d" % hf
                for b in range(NS):
                    P.dma("pool", cb_[b * 8:(b + 1) * 8, :, :], ck[b, hf * 64:(hf + 1) * 64, :, :].rearrange("s k d -> k s d"), writes=[kc_])
                if hf == 0:
                    CP("dve", cb_[:, 0, :], kb_, ["kb_", kc_], [kc_])
                for g in range(4):
                    TT("pool" if g % 2 == 0 else "dve", prod, cb_, qb[:, g * 64:(g + 1) * 64].unsqueeze(1).to_broadcast([128, 64, 64]), ALU.mult, [kc_, "qb"], ["sprod"])
                    op("dve", lambda e, g=g, hf=hf: e.tensor_reduce(out=sc[:, g, hf * 64:(hf + 1) * 64], in_=prod, axis=AX.X, op=ALU.add), ["sprod"], ["ssc"])
            op("dve", lambda e: e.tensor_reduce(out=rmx, in_=sc, axis=AX.X, op=ALU.max), ["ssc"], ["srmx"])
            TT("dve", rmx, rmx, sk8, ALU.max, ["srmx", "sk8"], ["srmx"])
            TS("dve", ngm, rmx, -0.125, None, ALU.mult, None, ["srmx"], ["sngm"])
            for g in range(4):
                ACT(pp[:, g, :], sc[:, g, :], AF.Exp, ["ssc", "sngm"], ["spp", "srsm"], scale=0.125, bias=ngm[:, g:g + 1], accum_out=rsm[:, g:g + 1])
            TT("dve", es, sk8, rmx, ALU.subtract, ["sk8", "srmx"], ["ses"])
            ACT(es, es, AF.Exp, ["ses"], ["ses"], scale=0.125)
            TT("dve", es, es, rsm, ALU.add, ["ses", "srsm"], ["ses"])
            op("dve", lambda e: e.reciprocal(out=es, in_=es), ["ses"], ["ses"])
            for hf in range(2):
                cb_ = cbuf[hf]; kc_ = "cbuf%d" % hf
                for b in range(NS):
                    P.dma("pool", cb_[b * 8:(b + 1) * 8, :, :], cv[b, hf * 64:(hf + 1) * 64, :, :].rearrange("s k d -> k s d"), writes=[kc_])
                if hf == 0:
                    CP("dve", cb_[:, 0, :], vb, ["vb", kc_], [kc_])
                for g in range(4):
                    TT("pool" if g % 2 == 0 else "dve", prod, cb_, pp[:, g, hf * 64:(hf + 1) * 64].unsqueeze(2).to_broadcast([128, 64, 64]), ALU.mult, [kc_, "spp"], ["sprod"])
                    dst = oacc if hf == 0 else opart
                    op("dve", lambda e, g=g, dst=dst: e.tensor_reduce(out=dst[:, g, :], in_=prod.rearrange("p s d -> p d s"), axis=AX.X, op=ALU.add),
                       ["sprod"], ["soacc" if hf == 0 else "sopart"])
            TT("dve", oacc, oacc, opart, ALU.add, ["soacc", "sopart"], ["soacc"])
            TT("dve", oacc, oacc, es.unsqueeze(2).to_broadcast([128, 4, 64]), ALU.mult, ["soacc", "ses"], ["soacc"])
            P.dma("sp", scr_o.rearrange("b (k f) -> (b k) f", k=8), oacc.rearrange("p g d -> p (g d)"), reads=["soacc"], writes=["scr_o"])
            otok = A.alloc([D], F32)
            P.dma("sp", otok[0:NS, :], scr_o, reads=["scr_o"], writes=["otok"])
            pt, kp = PS7()
            ptb = pt
            for c in range(16):
                TR(pt[:, c * NS:(c + 1) * NS], otok[0:NS, c * 128:(c + 1) * 128], ident[0:NS, 0:NS], ["otok", "cst"], [kp])
            CP("act", B["attnT"][:, :, TP:Tn], pt[:, 0:16 * NS].rearrange("p (a b) -> p a b", a=16), [kp], ["attnT"])
        if has_s and full and "no_smp_attn" not in opts:
            task(None, smp_attn)

        CUR[0] = ssd_list

        def s4_alloc(slot=None, sk=None):
            B["m_pass"] = A.mark()
            B["ynT"] = A.alloc([32, T], BF16)
            if has_s:
                B["xpre_s"] = A.alloc([48, NS], F32); B["zs_s"] = A.alloc([32, NS], F32); B["dtT_s"] = A.alloc([8, NS], F32)
            B["m_ssd"] = A.mark()
            for nm in ("zs", "xs", "yT"):
                B[nm] = A.alloc([4, T], F32)
            B["xpre"] = A.alloc([4, 3 + T], F32)
            B["bcpre"] = A.alloc([2, 3 + T], F32)
            B["bcs"] = A.alloc([2, T], F32)
            B["BCT"] = A.alloc([2, T], BF16)
            B["dtT"] = A.alloc([T], F32)
            B["gsq"] = A.alloc([4, T], BF16)
            B["rstd2"] = A.alloc([T], F32)
            B["acc"] = A.alloc([T], F32)
            B["Xpad"] = [A.alloc([8, 128], BF16) for _ in range(2)]
            B["Xd"] = [A.alloc([512], BF16) for _ in range(2)]; B["Btok"] = [A.alloc([128], BF16) for _ in range(2)]
            for nm in ("dtk", "dA", "acum", "tot", "dec", "cd", "ndA", "nacum"):
                B[nm] = [A.alloc([8], F32) for _ in range(2)]
            B["dAtri"] = A.alloc([8, 128], F32)
            B["L"] = A.alloc([8, 128], F32); B["MT"] = A.alloc([8, 128], BF16)
            B["cbT"] = A.alloc([128], F32); B["Ebc"] = A.alloc([4, 128], F32)
            B["hTb"] = A.alloc([512], BF16); B["ytmp"] = A.alloc([4, 128], F32); B["htmp"] = A.alloc([512], F32)
            op("pool", lambda e: e.memset(B["Xpad"][0], 0.0), [], ["Xpad0"])
            op("pool", lambda e: e.memset(B["Xpad"][1], 0.0), [], ["Xpad1"])
        task(None, s4_alloc)

        def conv_silu(pre, c_in, ch, dst, kpre, kdst):
            acc = B["acc"]
            ACT(acc[:, 0:TP], pre[:, 0:TP], AF.Copy, [kpre, "convw"], ["acc"], scale=convw[:, ch, 0:1])
            for k in range(1, 4):
                STT("dve", acc[:, 0:TP], pre[:, k:k + TP], convw[:, ch, k:k + 1], acc[:, 0:TP], ALU.mult, ALU.add, [kpre, "acc", "convw"], ["acc"])
            ACT(dst[:, 0:TP], acc[:, 0:TP], AF.Silu, ["acc", "convb"], [kdst], bias=convb[:, ch:ch + 1])

        for g in range(NG):
            if full:
                def comp_z(slot, sk, g=g):
                    wv_ = wview(slot, 16, 512)
                    for i in range(4):
                        res = proj(128, lambda kc: wv_[:, kc, i * 128:(i + 1) * 128], lambda kc, c0, c1: hnT[:, kc, c0:c1], 16, Tn, ["hnT", sk + ("a" if i < 2 else "b")])
                        for (o, kp, c0, c1) in res:
                            ACT(B["zs"][:, i, c0:c1], o, AF.Silu, [kp], ["zs"])
                        if has_s:
                            CP("pool", B["zs_s"][:, 4 * g + i, :], B["zs"][:, i, TP:Tn], ["zs"], ["zs_s"])
                task(wloads_plain(w_in, OZ + g * 512, 512, 16), comp_z)

            def comp_x(slot, sk, g=g):
                wv_ = wview(slot, 16, 512)
                xpre = B["xpre"]
                CP("dve", xpre[:, :, 0:3], convc[:, 4 * g:4 * g + 4, :], ["convc"], ["xpre"])
                for i in range(4):
                    res = proj(128, lambda kc: wv_[:, kc, i * 128:(i + 1) * 128], lambda kc, c0, c1: hnT[:, kc, c0:c1], 16, Tn, ["hnT", sk + ("a" if i < 2 else "b")])
                    for (o, kp, c0, c1) in res:
                        CP("act", xpre[:, i, 3 + c0:3 + c1], o, [kp], ["xpre"])
                for i in range(4):
                    conv_silu(xpre[:, i, :], None, 4 * g + i, B["xs"][:, i, :], "xpre", "xs")
                CP("dve", convc[:, 4 * g:4 * g + 4, :], xpre[:, :, TP:TP + 3], ["xpre"], ["convc"])
                if has_s:
                    CP("pool", B["xpre_s"][:, 4 * g:4 * g + 4, :], xpre[:, :, 3 + TP:3 + Tn], ["xpre"], ["xpre_s"])
            task(wloads_plain(w_in, OX + g * 512, 512, 16), comp_x)

            def loads_bcdt(slot, g=g):
                f1 = wloads_plain(w_in, OB + g * 128, 128, 16, 0, 264, "a")(slot)
                f2 = wloads_plain(w_in, OC + g * 128, 128, 16, 128, 264, "b")(slot)
                f3 = wloads_plain(w_in, ODT + g * 8, 8, 16, 256, 264, "b")(slot)
                return f1 + f2 + f3

            def comp_ssd(slot, sk, g=g):
                wv_ = wview(slot, 16, 264)
                bcpre, bcs, BCT, dtT = B["bcpre"], B["bcs"], B["BCT"], B["dtT"]
                CP("dve", bcpre[:, 0, 0:3], convc[:, 32 + g, :], ["convc"], ["bcpre"])
                CP("dve", bcpre[:, 1, 0:3], convc[:, 40 + g, :], ["convc"], ["bcpre"])
                for i in range(2):
                    if i == 1 and not full and not kv:
                        continue
                    res = proj(128, lambda kc: wv_[:, kc, i * 128:(i + 1) * 128], lambda kc, c0, c1: hnT[:, kc, c0:c1], 16, Tn, ["hnT", sk + ("a" if i == 0 else "b")])
                    for (o, kp, c0, c1) in res:
                        CP("act", bcpre[:, i, 3 + c0:3 + c1], o, [kp], ["bcpre"])
                res = proj(8, lambda kc: wv_[:, kc, 256:264], lambda kc, c0, c1: hnT[:, kc, c0:c1], 16, Tn, ["hnT", sk + "b"])
                for (o, kp, c0, c1) in res:
                    ACT(dtT[0:8, c0:c1], o, AF.Exp, [kp, "dtb"], ["dtT"], bias=dtb[0:8, g:g + 1])
                ACT(dtT[0:8, 0:Tn], dtT[0:8, 0:Tn], AF.Ln, ["dtT"], ["dtT"], bias=1.0)
                for i in range(2 if full else 1):
                    conv_silu(bcpre[:, i, :], None, (32 if i == 0 else 40) + g, bcs[:, i, :], "bcpre", "bcs")
                CP("dve", convc[:, 32 + g, :], bcpre[:, 0, TP:TP + 3], ["bcpre"], ["convc"])
                if full or kv:
                    CP("dve", convc[:, 40 + g, :], bcpre[:, 1, TP:TP + 3], ["bcpre"], ["convc"])
                if full:
                    CP("act", BCT[:, :, 0:TP], bcs[:, :, 0:TP], ["bcs"], ["BCT"])
                if has_s:
                    CP("pool", B["xpre_s"][:, 32 + g, :], bcpre[:, 0, 3 + TP:3 + Tn], ["bcpre"], ["xpre_s"])
                    CP("pool", B["xpre_s"][:, 40 + g, :], bcpre[:, 1, 3 + TP:3 + Tn], ["bcpre"], ["xpre_s"])
                    CP("pool", B["dtT_s"][0:8, g, :], dtT[0:8, TP:Tn], ["dtT"], ["dtT_s"])
                hTg = hT[:, g, :]
                L, MT, cbT, Ebc, hTb, ytmp, htmp, dAtri = [B[n] for n in ("L", "MT", "cbT", "Ebc", "hTb", "ytmp", "htmp", "dAtri")]
                for c in range(4):
                    cs = slice(c * 128, (c + 1) * 128)
                    pr_ = c % 2
                    dtk, dA, acum, tot, dec, cd, ndA, nacum = [B[n][pr_] for n in ("dtk", "dA", "acum", "tot", "dec", "cd", "ndA", "nacum")]
                    Xpad, Xd, Btok = B["Xpad"][pr_], B["Xd"][pr_], B["Btok"][pr_]
                    pt, kp = PS7()
                    TR(pt[:, 0:8], dtT[0:8, cs], ident[0:8, 0:8], ["dtT", "cst"], [kp])
                    CP("act", dtk, pt[:, 0:8], [kp], ["dtk%d" % pr_])
                    TT("dve", dA, dtk, negA[:, g * 8:(g + 1) * 8], ALU.mult, ["dtk%d" % pr_, "negA"], ["dA%d" % pr_])
                    pa, kpa = PS7()
                    MM(pa[:, 0:8], triU, dA, True, True, ["dA%d" % pr_, "cst"], [kpa])
                    MM(pa[:, 8:16], ones_f, dA, True, True, ["dA%d" % pr_, "cst"], [kpa])
                    CP("act", acum, pa[:, 0:8], [kpa], ["acum%d" % pr_])
                    CP("act", tot, pa[:, 8:16], [kpa], ["tot%d" % pr_])
                    TT("dve", dec, tot, acum, ALU.subtract, ["tot%d" % pr_, "acum%d" % pr_], ["dec%d" % pr_])
                    ACT(dec, dec, AF.Exp, ["dec%d" % pr_], ["dec%d" % pr_])
                    TT("dve", dec, dec, dtk, ALU.mult, ["dec%d" % pr_, "dtk%d" % pr_], ["dec%d" % pr_])
                    ACT(cd, tot, AF.Exp, ["tot%d" % pr_], ["cd%d" % pr_])
                    px, kpx = PS7()
                    for i in range(4):
                        TR(px[:, i * 128:(i + 1) * 128], B["xs"][:, i, cs], ident, ["xs", "cst"], [kpx])
                    px3 = px.rearrange("p (r d) -> p r d", r=8)
                    TT("dve", Xd.rearrange("p (r d) -> p r d", r=8), px3, dec.unsqueeze(2).to_broadcast([128, 8, 64]), ALU.mult, [kpx, "dec%d" % pr_], ["Xd%d" % pr_])
                    if full:
                        for par in range(2):
                            TT("dve", Xpad[:, par::2, par * 64:(par + 1) * 64], px3[:, par::2, :],
                               dtk[:, par::2].unsqueeze(2).to_broadcast([128, 4, 64]), ALU.mult, [kpx, "dtk%d" % pr_], ["Xpad%d" % pr_])
                    pb, kpb = PS7()
                    TR(pb[:, 0:128], bcs[:, 0, cs], ident, ["bcs", "cst"], [kpb])
                    CP("act", Btok, pb[:, 0:128], [kpb], ["Btok%d" % pr_])
                    if full:
                        TT("pool", dAtri, triU.unsqueeze(1).to_broadcast([128, 8, 128]), dA.unsqueeze(2).to_broadcast([128, 8, 128]), ALU.mult,
                           ["dA%d" % pr_, "cst"], ["dAtri"])
                        pA = []
                        for hf in range(2):
                            p_, k_ = PS7()
                            for q4 in range(2):
                                r0 = hf * 4 + q4 * 2
                                MM(p_[:, q4 * 256:(q4 + 1) * 256], ones_f, dAtri[:, r0:r0 + 2, :].rearrange("p a b -> p (a b)"), True, True, ["dAtri", "cst"], [k_])
                            pA.append((p_, k_))
                        TS("dve", nacum, acum, -1.0, None, ALU.mult, None, ["acum%d" % pr_], ["nacum%d" % pr_])
                        pc, kpc = PS7()
                        MM(pc[:, 0:128], BCT[:, 0, cs], BCT[:, 1, cs], True, True, ["BCT"], [kpc])
                        TT("dve", cbT, pc[:, 0:128], ssdmask, ALU.mult, [kpc, "cst"], ["cbT"])
                        for r in range(8):
                            p_, k_ = pA[r // 4]
                            a_ = p_[:, (r % 4) * 128:(r % 4 + 1) * 128]
                            STT("dve", L[:, r, :], a_, nacum[:, r:r + 1], ssdmask, ALU.add, ALU.mult, [k_, "nacum%d" % pr_, "cst"], ["L"])
                        ACT(L, L, AF.Exp, ["L"], ["L"])
                        TT("pool", MT, L, cbT.unsqueeze(1).to_broadcast([128, 8, 128]), ALU.mult, ["L", "cbT"], ["MT"])
                        for jx in range(4):
                            for par in range(2):
                                r = 2 * jx + par
                                p_, k_ = pA[r // 4]
                                CP("dve", Ebc[par * 64:(par + 1) * 64, jx, :], p_[par * 64:(par + 1) * 64, (r % 4) * 128:(r % 4 + 1) * 128], [k_], ["Ebc"])
                        ACT(Ebc, Ebc, AF.Exp, ["Ebc"], ["Ebc"])
                        CP("act", hTb, hTg, ["hT"], ["hTb"])
                        po, kpo = PS7()
                        for jx in range(4):
                            MM(po[:, jx * 128:(jx + 1) * 128], hTb[:, jx * 128:(jx + 1) * 128], BCT[:, 1, cs], True, True, ["hTb", "BCT"], [kpo])
                        TT("dve", ytmp, po.rearrange("p (a b) -> p a b", a=4), Ebc, ALU.mult, [kpo, "Ebc"], ["ytmp"])
                        pd, kpd = PS7()
                        for jx in range(4):
                            for par in range(2):
                                r = 2 * jx + par
                                MM(pd[:, jx * 128:(jx + 1) * 128], Xpad[:, r, :], MT[:, r, :], par == 0, par == 1, ["Xpad%d" % pr_, "MT"], [kpd])
                        TT("dve", ytmp, pd.rearrange("p (a b) -> p a b", a=4), ytmp, ALU.add, [kpd, "ytmp"], ["ytmp"])
                        for jx in range(4):
                            STT("dve", B["yT"][:, jx, cs], B["xs"][:, jx, cs], dsk[:, 4 * g + jx:4 * g + jx + 1], ytmp[:, jx, :], ALU.mult, ALU.add,
                                ["xs", "ytmp", "dsk"], ["yT"])
                    pst, kps = PS7()
                    MM(pst, Btok, Xd, True, True, ["Btok%d" % pr_, "Xd%d" % pr_], [kps])
                    TT("dve", htmp.rearrange("p (r d) -> p r d", r=8), hTg.rearrange("p (r d) -> p r d", r=8),
                       cd.unsqueeze(2).to_broadcast([128, 8, 64]), ALU.mult, ["hT", "cd%d" % pr_], ["htmp"])
                    TT("dve", hTg, htmp, pst, ALU.add, ["htmp", kps], ["hT"])
                if full:
                    yT, zs, gsq, rstd2 = B["yT"], B["zs"], B["gsq"], B["rstd2"]
                    TT("pool", yT[:, :, 0:TP], yT[:, :, 0:TP], zs[:, :, 0:TP], ALU.mult, ["yT", "zs"], ["yT"])
                    ACT(gsq[:, :, 0:TP], yT[:, :, 0:TP], AF.Square, ["yT"], ["gsq"])
                    res = proj(128, lambda kc: ones_b, lambda kc, c0, c1: gsq[:, kc, c0:c1], 4, TP, ["gsq", "cst_b"])
                    for (o, kp, c0, c1) in res:
                        TS("dve", rstd2[:, c0:c1], o, 1.0 / 512, EPS, ALU.mult, ALU.add, [kp], ["rstd2"])
                    ACT(rstd2[:, 0:TP], rstd2[:, 0:TP], AF.Ln, ["rstd2"], ["rstd2"])
                    ACT(rstd2[:, 0:TP], rstd2[:, 0:TP], AF.Exp, ["rstd2"], ["rstd2"], scale=-0.5)
                    for jx in range(4):
                        STT("dve", B["ynT"][:, 4 * g + jx, 0:TP], yT[:, jx, 0:TP], snw[:, 4 * g + jx:4 * g + jx + 1], rstd2[:, 0:TP], ALU.mult, ALU.mult,
                            ["yT", "rstd2", "snw"], ["ynT"])
            task(loads_bcdt, comp_ssd)


        def smp_ssm(slot=None, sk=None):
            barrier()
            A.release(B["m_ssd"])
            xpre_s, zs_s, dtT_s = B["xpre_s"], B["zs_s"], B["dtT_s"]
            ST = A.alloc([48, 48], F32)
            sct = A.alloc([1536], F32)
            sc2 = sconv.rearrange("b k c -> (b k) c")
            for q in range(4):
                P.dma("sp", sct[0:48, :], sc2[:, q * 1536:(q + 1) * 1536], writes=["sct"])
                for j4 in range(3):
                    pt, kp = PS7()
                    for jj in range(4):
                        lc = j4 * 4 + jj
                        TR(pt[:, jj * 48:(jj + 1) * 48], sct[0:48, lc * 128:(lc + 1) * 128], ident[0:48, 0:48], ["sct", "cst"], [kp])
                    CP("act", ST[:, q * 12 + j4 * 4:q * 12 + j4 * 4 + 4, :], pt[:, 0:192].rearrange("p (a b) -> p a b", a=4), [kp], ["ST"])
            STv = ST.rearrange("p c (b k) -> p c b k", k=3)
            acc = A.alloc([48, NS], F32); tmp = A.alloc([48, NS], F32); xc_s = A.alloc([48, NS], F32)
            TT("pool", acc, STv[:, :, :, 0], convw[:, :, 0:1].to_broadcast([128, 48, NS]), ALU.mult, ["ST", "convw"], ["sacc"])
            for k in (1, 2):
                TT("pool", tmp, STv[:, :, :, k], convw[:, :, k:k + 1].to_broadcast([128, 48, NS]), ALU.mult, ["ST", "convw"], ["stmp"])
                TT("pool", acc, acc, tmp, ALU.add, ["sacc", "stmp"], ["sacc"])
            TT("pool", tmp, xpre_s, convw[:, :, 3:4].to_broadcast([128, 48, NS]), ALU.mult, ["xpre_s", "convw"], ["stmp"])
            TT("pool", acc, acc, tmp, ALU.add, ["sacc", "stmp"], ["sacc"])
            TT("pool", acc, acc, convb.unsqueeze(2).to_broadcast([128, 48, NS]), ALU.add, ["sacc", "convb"], ["sacc"])
            ACT(xc_s, acc, AF.Silu, ["sacc"], ["xc_s"])
            if "ssm_stop1" in opts:
                return
            P.dma("sp", cvs_out[:, 0:2, :], sconv[:, 1:3, :], writes=["cvs01"])
            tok = A.alloc([CONVD], F32)
            for (srcT, ksrc, dst, kd) in ((xpre_s, "xpre_s", cvs_out[:, 2, :], "cvs2"), (xc_s, "xc_s", scr_x, "scr_x")):
                for q in range(12):
                    pt, kp = PS7()
                    for jj in range(4):
                        TR(pt[0:NS, jj * 128:(jj + 1) * 128], srcT[:, q * 4 + jj, :], ident, [ksrc, "cst"], [kp])
                    CP("act", tok[0:NS, q * 512:(q + 1) * 512], pt[0:NS, :], [kp], ["stok"])
                P.dma("sp", dst, tok[0:NS, :], reads=["stok"], writes=[kd])
            dtt = A.alloc([64], F32)
            pt, kp = PS7()
            for g in range(NG):
                TR(pt[0:NS, g * 8:(g + 1) * 8], dtT_s[0:8, g, :], ident[0:8, 0:8], ["dtT_s", "cst"], [kp])
            CP("act", dtt[0:NS, :], pt[0:NS, 0:64], [kp], ["dtt"])
            P.dma("sp", scr_dt, dtt[0:NS, :], reads=["dtt"], writes=["scr_dt"])
            if "ssm_stop2" in opts:
                return
            Xg = A.alloc([64], F32); Bg = A.alloc([128], F32); Cg = A.alloc([128], F32)
            dtg = A.alloc([1], F32); ag = A.alloc([1], F32); da = A.alloc([1], F32)
            yg = [A.alloc([64], F32) for _ in range(2)]
            hb = [A.alloc([8, 128], F32) for _ in range(2)]
            tm = A.alloc([8, 128], F32); pr = A.alloc([8, 128], F32)
            for g in range(NG):
                P.dma("pool", Xg, scr_x[:, g * 512:(g + 1) * 512].rearrange("b (r p) -> b r p", r=8), reads=["scr_x"], writes=["Xg"])
                P.dma("pool", Bg, scr_x[:, OB - OX + g * 128:OB - OX + (g + 1) * 128].unsqueeze(1).to_broadcast([NS, 8, 128]), reads=["scr_x"], writes=["Bg"])
                P.dma("pool", Cg, scr_x[:, OC - OX + g * 128:OC - OX + (g + 1) * 128].unsqueeze(1).to_broadcast([NS, 8, 128]), reads=["scr_x"], writes=["Cg"])
                P.dma("pool", dtg, scr_dt[:, g * 8:(g + 1) * 8].unsqueeze(2), reads=["scr_dt"], writes=["dtg"])
                P.dma("pool", ag, alog_in[:, g * 8:(g + 1) * 8].unsqueeze(2).to_broadcast([NS, 8, 1]), writes=["ag"])
                ACT(ag, ag, AF.Exp, ["ag"], ["ag"])
                TT("dve", da, dtg, ag, ALU.mult, ["dtg", "ag"], ["da"])
                ACT(da, da, AF.Exp, ["da"], ["da"], scale=-1.0)
                TS("dve", Xg, Xg, dtg[:, 0:1], None, ALU.mult, None, ["Xg", "dtg"], ["Xg"])
                y_ = yg[g % 2]; ky = "yg%d" % (g % 2)
                for pc in range(8):
                    h = hb[pc % 2]; kh = "hb%d" % (pc % 2)
                    hsrc = sssm[:, g * 8:(g + 1) * 8, pc * 8:(pc + 1) * 8, :].rearrange("b r p n -> b r (p n)")
                    hdst = sss_out[:, g * 8:(g + 1) * 8, pc * 8:(pc + 1) * 8, :].rearrange("b r p n -> b r (p n)")
                    if "ssm_noload" not in opts:
                        P.dma("sp", h.rearrange("q a b -> q (a b)"), hsrc, writes=[kh])
                    if "ssm_nocomp" in opts:
                        if "ssm_nostore" not in opts:
                            P.dma("act", hdst, h.rearrange("q a b -> q (a b)"), reads=[kh], writes=["ssso"])
                        continue
                    TT("pool", tm, Xg[:, pc * 8:(pc + 1) * 8].unsqueeze(2).to_broadcast([128, 8, 128]), Bg.unsqueeze(1).to_broadcast([128, 8, 128]), ALU.mult,
                       ["Xg", "Bg"], ["tm"])
                    STT("dve", h, h, da[:, 0:1], tm, ALU.mult, ALU.add, [kh, "da", "tm"], [kh])
                    if "ssm_nostore" not in opts:
                        P.dma("act", hdst, h.rearrange("q a b -> q (a b)"), reads=[kh], writes=["ssso"])
                    TT("pool" if pc % 2 == 0 else "dve", pr, h, Cg.unsqueeze(1).to_broadcast([128, 8, 128]), ALU.mult, [kh, "Cg"], ["pr"])
                    op("dve", lambda e, y_=y_, pc=pc: e.tensor_reduce(out=y_[:, pc * 8:(pc + 1) * 8], in_=pr, axis=AX.X, op=ALU.add), ["pr"], [ky])
                P.dma("sp", scr_y[:, g * 512:(g + 1) * 512].rearrange("b (r p) -> b r p", r=8), y_, reads=[ky], writes=["scr_y"])
            if "ssm_stop3" in opts:
                return
            ytk = A.alloc([DI], F32)
            P.dma("sp", ytk[0:NS, :], scr_y, reads=["scr_y"], writes=["ytk"])
            yTs = A.alloc([32, NS], F32); gq = A.alloc([32, 32], BF16)[:, :, 0:NS]; rs = A.alloc([8, NS], F32)
            for q in range(8):
                pt, kp = PS7()
                for jj in range(4):
                    TR(pt[:, jj * NS:(jj + 1) * NS], ytk[0:NS, (q * 4 + jj) * 128:(q * 4 + jj + 1) * 128], ident[0:NS, 0:NS], ["ytk", "cst"], [kp])
                CP("act", yTs[:, q * 4:(q + 1) * 4, :], pt[:, 0:4 * NS].rearrange("p (a b) -> p a b", a=4), [kp], ["yTs"])
            if "ssm_stop4" in opts:
                return
            TT("pool", tmp[:, 0:32, :], xc_s[:, 0:32, :], dsk.unsqueeze(2).to_broadcast([128, 32, NS]), ALU.mult, ["xc_s", "dsk"], ["stmp"])
            TT("pool", yTs, yTs, tmp[:, 0:32, :], ALU.add, ["yTs", "stmp"], ["yTs"])
            TT("pool", yTs, yTs, zs_s, ALU.mult, ["yTs", "zs_s"], ["yTs"])
            if "ssm_stop5" in opts:
                return
            ACT(gq, yTs, AF.Square, ["yTs"], ["gq"])
            if "ssm_stop6" in opts:
                return
            for g in range(NG):
                pt_, kp = PS7()
                o = pt_[:, 0:NS]
                for jx in range(4):
                    MM(o, ones_b, gq[:, 4 * g + jx, :], jx == 0, jx == 3, ["gq", "cst_b"], [kp])
                ACT(rs[:, g, :], o, AF.Copy, [kp], ["rs"], scale=1.0 / 512)
            TS("dve", rs, rs, EPS, None, ALU.add, None, ["rs"], ["rs"])
            ACT(rs, rs, AF.Ln, ["rs"], ["rs"])
            ACT(rs, rs, AF.Exp, ["rs"], ["rs"], scale=-0.5)
            if "ssm_stop7" in opts:
                return
            for c in range(32):
                STT("dve", B["ynT"][:, c, TP:Tn], yTs[:, c, :], snw[:, c:c + 1], rs[:, c // 4, :], ALU.mult, ALU.mult, ["yTs", "rs", "snw"], ["ynT"])
        if has_s and "no_smp_ssm" not in opts:
            task(None, smp_ssm)

        CUR[0] = TASKS
        TASKS.extend(ssd_list)
        if full or kv:
            TASKS.extend(att_list)
        if not full:
            def p_end(slot=None, sk=None):
                barrier()
                A.release(B["m_pass"])
            task(None, p_end)
            return

        def s5_alloc(slot=None, sk=None):
            barrier()
            A.release(B["m_att"])
            B["mergedT"] = A.alloc([16, T], BF16)
            B["sg"] = A.alloc([2, T], F32)
            B["mt"] = A.alloc([2, T], F32)
        task(None, s5_alloc)

        for cb in range(4):
            st5 = {}

            def mk_acc(name, nk, cw, sub):
                def comp(slot, sk, cb=cb):
                    wv_ = wview(slot, nk, cw)
                    src = {"ab": B["attnT"], "sb": B["ynT"], "ga": hnT, "gs": hnT}[name[0:2]]
                    ksrc = {"ab": "attnT", "sb": "ynT", "ga": "hnT", "gs": "hnT"}[name[0:2]]
                    return wv_, src, ksrc
                return comp

            def comp_merge_blk(slots, cb=cb):
                pass

        for cb in range(4):
            def comp_ga(slot, sk, cb=cb):
                wv_ = wview(slot, 16, 512)
                for i in range(4):
                    res = proj(128, lambda kc: wv_[:, kc, i * 128:(i + 1) * 128], lambda kc, c0, c1: hnT[:, kc, c0:c1], 16, Tn, ["hnT", sk + ("a" if i < 2 else "b")])
                    for (o, kp, c0, c1) in res:
                        ACT(B["sgA"][:, i, c0:c1], o, AF.Sigmoid, [kp], ["sgA"])
            def comp_gs(slot, sk, cb=cb):
                wv_ = wview(slot, 16, 512)
                for i in range(4):
                    res = proj(128, lambda kc: wv_[:, kc, i * 128:(i + 1) * 128], lambda kc, c0, c1: hnT[:, kc, c0:c1], 16, Tn, ["hnT", sk + ("a" if i < 2 else "b")])
                    for (o, kp, c0, c1) in res:
                        ACT(B["sgS"][:, i, c0:c1], o, AF.Sigmoid, [kp], ["sgS"])
            def comp_ab(slot, sk, cb=cb):
                wv_ = wview(slot, 16, 512)
                for i in range(4):
                    res = proj(128, lambda kc: wv_[:, kc, i * 128:(i + 1) * 128], lambda kc, c0, c1: B["attnT"][:, kc, c0:c1], 16, Tn, ["attnT", sk + ("a" if i < 2 else "b")])
                    for (o, kp, c0, c1) in res:
                        TT("dve", B["mtmp"][:, i, c0:c1], o, B["sgA"][:, i, c0:c1], ALU.mult, [kp, "sgA"], ["mtmp"])
            def comp_sb(slot, sk, cb=cb, half=0):
                pass
            if cb == 0:
                def s5b(slot=None, sk=None):
                    B["sgA"] = A.alloc([4, T], F32); B["sgS"] = A.alloc([4, T], F32); B["mtmp"] = A.alloc([4, T], F32)
                task(None, s5b)
            task(wloads_plain(w_in, OGA + cb * 512, 512, 16), comp_ga)
            task(wloads_plain(w_in, OGS + cb * 512, 512, 16), comp_gs)
            task(wloads_plain(w_ab, cb * 512, 512, 16), comp_ab)
            for hf in range(2):
                def comp_sbh(slot, sk, cb=cb, hf=hf):
                    wv_ = wview(slot, 32, 256)
                    for i2 in range(2):
                        i = hf * 2 + i2
                        res = proj(128, lambda kc: wv_[:, kc, i2 * 128:(i2 + 1) * 128], lambda kc, c0, c1: B["ynT"][:, kc, c0:c1], 32, Tn, ["ynT", sk + ("a" if i2 == 0 else "b")])
                        for (o, kp, c0, c1) in res:
                            TT("dve", B["sgS"][:, i, c0:c1], o, B["sgS"][:, i, c0:c1], ALU.mult, [kp, "sgS"], ["sgS"])
                        TT("pool", B["mergedT"][:, cb * 4 + i, 0:Tn], B["sgS"][:, i, 0:Tn], B["mtmp"][:, i, 0:Tn], ALU.add, ["sgS", "mtmp"], ["mergedT"])
                task(wloads_plain(w_sb, cb * 512 + hf * 256, 256, 32), comp_sbh)

        if has_s and "smp" in dbg_out:
            def dbg_smp(slot=None, sk=None):
                d1 = A.alloc([16, NS], F32); d2 = A.alloc([32, NS], F32)
                CP("dve", d1, B["attnT"][:, :, TP:Tn], ["attnT"], ["d1"])
                CP("dve", d2, B["ynT"][:, :, TP:Tn], ["ynT"], ["d2"])
                P.dma("sp", dbg_out["smp"][:, 0:16, :], d1, reads=["d1"], writes=["dbgo1"])
                P.dma("sp", dbg_out["smp"][:, 16:48, :], d2, reads=["d2"], writes=["dbgo2"])
            task(None, dbg_smp)

        def s6_alloc(slot=None, sk=None):
            barrier()
            A.release(B["m_pass"])
            B["mixT"] = A.alloc([16, T], F32)
            B["xT"] = A.alloc([16, T], F32)
            B["rstd"] = A.alloc([T], F32)
            B["m_ffn"] = A.mark()
            CP("dve", hnT[:, :, 0:Tn], B["mergedT"][:, :, 0:Tn], ["mergedT"], ["hnT"])
            barrier()
        task(None, s6_alloc)

        def reload_x(slot=None, sk=None):
            xtok = B["xtok2"] = [A.alloc([D], F32) for _ in range(2)]
            tiles = [(x_all[x_row0 + t * 128:x_row0 + (t + 1) * 128, :], 128, t * 128) for t in range(4)]
            if has_s:
                tiles.append((x_smp, NS, TP))
            for ti, (src, n, c0) in enumerate(tiles):
                xt = xtok[ti % 2]; kx = "xtokb%d" % (ti % 2)
                P.dma("sp", xt[0:n, :], src, writes=[kx])
                for q in range(4):
                    pt, kp = PS7()
                    for jj in range(4):
                        c = q * 4 + jj
                        TR(pt[:, jj * 128:jj * 128 + n], xt[0:n, c * 128:(c + 1) * 128], ident[0:n, 0:n], [kx, "cst"], [kp])
                    CP("act", B["xT"][:, q * 4:(q + 1) * 4, c0:c0 + n], pt.rearrange("p (a b) -> p a b", a=4)[:, :, 0:n], [kp], ["xT2"])
        task(None, reload_x)

        for cb in range(4):
            def comp_o(slot, sk, cb=cb):
                wv_ = wview(slot, 16, 512)
                for i in range(4):
                    res = proj(128, lambda kc: wv_[:, kc, i * 128:(i + 1) * 128], lambda kc, c0, c1: hnT[:, kc, c0:c1], 16, Tn, ["hnT", sk + ("a" if i < 2 else "b")])
                    for (o, kp, c0, c1) in res:
                        CP("act", B["mixT"][:, cb * 4 + i, c0:c1], o, [kp], ["mixT"])
            task(wloads_plain(w_o, cb * 512, 512, 16), comp_o)

        def add_norm(widx, srcname, sqbuf):
            fm_rstd(B[srcname], srcname, Tn, sqbuf, B["rstd"], D)
            for kc in range(16):
                STT("dve", B[srcname][:, kc, 0:Tn], B[srcname][:, kc, 0:Tn], nw[:, widx, kc:kc + 1], B["rstd"][:, 0:Tn], ALU.mult, ALU.mult,
                    [srcname, "rstd", "nw"], [srcname])
            TT("pool", B["xT"][:, :, 0:Tn], B["xT"][:, :, 0:Tn], B[srcname][:, :, 0:Tn], ALU.add, ["xT2", srcname], ["xT2"])

        def s6b(slot=None, sk=None):
            barrier()
            A.release(B["m_ffn"])
            B["actT"] = A.alloc([44, T], BF16)
            sq = B["actT"][:, 0:16, :]
            add_norm(1, "mixT", sq)
            fm_rstd(B["xT"], "xT2", Tn, sq, B["rstd"], D)
            for kc in range(16):
                STT("dve", hnT[:, kc, 0:Tn], B["xT"][:, kc, 0:Tn], nw[:, 2, kc:kc + 1], B["rstd"][:, 0:Tn], ALU.mult, ALU.mult,
                    ["xT2", "rstd", "nw"], ["hnT"])
            barrier()
            B["sgu"] = A.alloc([4, T], F32)
        task(None, s6b)

        for fb in range(11):
            def comp_g(slot, sk, fb=fb):
                wv_ = wview(slot, 16, 512)
                B["pend"] = []
                for i in range(4):
                    res = proj(128, lambda kc: wv_[:, kc, i * 128:(i + 1) * 128], lambda kc, c0, c1: hnT[:, kc, c0:c1], 16, Tn, ["hnT", sk + ("a" if i < 2 else "b")])
                    for (o, kp, c0, c1) in res:
                        ACT(B["sgu"][:, i, c0:c1], o, AF.Silu, [kp], ["sgu"])
            def comp_u(slot, sk, fb=fb):
                wv_ = wview(slot, 16, 512)
                for i in range(4):
                    res = proj(128, lambda kc: wv_[:, kc, i * 128:(i + 1) * 128], lambda kc, c0, c1: hnT[:, kc, c0:c1], 16, Tn, ["hnT", sk + ("a" if i < 2 else "b")])
                    for (o, kp, c0, c1) in res:
                        TT("dve", B["actT"][:, fb * 4 + i, c0:c1], o, B["sgu"][:, i, c0:c1], ALU.mult, [kp, "sgu"], ["actT"])
            task(wloads_plain(w_gu, fb * 512, 512, 16), comp_g)
            task(wloads_plain(w_gu, DFF + fb * 512, 512, 16), comp_u)
        for cbk in range(16):
            def comp_d(slot, sk, cbk=cbk):
                wv_ = wview(slot, 44, 128)
                res = proj(128, lambda kc: wv_[:, kc, :], lambda kc, c0, c1: B["actT"][:, kc, c0:c1], 44, Tn, ["actT", sk + "a"])
                for (o, kp, c0, c1) in res:
                    CP("act", B["mixT"][:, cbk, c0:c1], o, [kp], ["mixT"])
            task(wloads_plain(w_dn, cbk * 128, 128, 44), comp_d)

        def s8(slot=None, sk=None):
            barrier()
            A.release(B["m_ffn"])
            sq = A.alloc([16, T], BF16)
            add_norm(3, "mixT", sq)
            ytok = [A.alloc([D], F32) for _ in range(2)]
            tiles = [(y_out[y_row0 + t * 128:y_row0 + (t + 1) * 128, :], 128, t * 128) for t in range(4)]
            if has_s:
                tiles.append((ys_out, NS, TP))
            for ti, (dst, n, c0) in enumerate(tiles):
                yt = ytok[ti % 2]; ky = "ytok%d" % (ti % 2)
                for q in range(4):
                    pt, kp = PS7()
                    for jj in range(4):
                        c = q * 4 + jj
                        TR(pt[0:n, jj * 128:(jj + 1) * 128], B["xT"][:, c, c0:c0 + n], ident, ["xT2", "cst"], [kp])
                    CP("act", yt[0:n, q * 512:(q + 1) * 512], pt[0:n, :], [kp], [ky])
                P.dma("sp", dst, yt[0:n, :], reads=[ky], writes=["yout"])
            barrier()
            A.release(pm)
        task(None, s8)

    if "only_p3" not in opts:
        emit_pass(0, False, 0, False, 0, False, kv=False)
        emit_pass(1, False, 512, False, 0, False)

    def apply_flag(slot=None, sk=None):
        TS("dve", hT, hT, flag[:, 0:1], None, ALU.mult, None, ["hT", "flag"], ["hT"])
    task(None, apply_flag)
    if "only_p3" not in opts:
        emit_pass(2, True, 1024, False, 0, True)
    emit_pass(3, True, 1536, "no_s" not in opts, 512, False)

    def final_out(slot=None, sk=None):
        m = A.mark()
        ctok = A.alloc([CONVD], F32)
        for q in range(12):
            pt, kp = PS7()
            for jj in range(4):
                c = q * 4 + jj
                TR(pt[0:3, jj * 128:(jj + 1) * 128], convc[:, c, :], ident, ["convc", "cst"], [kp])
            CP("act", ctok[0:3, q * 512:(q + 1) * 512], pt[0:3, :], [kp], ["ctok"])
        P.dma("sp", cv_out, ctok[0:3, :], reads=["ctok"], writes=["cvo"])
        hto = [A.alloc([512], F32) for _ in range(2)]
        for g in range(NG):
            pt, kp = PS7()
            for jx in range(4):
                TR(pt[:, jx * 128:(jx + 1) * 128], hT[:, g, jx * 128:(jx + 1) * 128], ident, ["hT", "cst"], [kp])
            CP("act", hto[g % 2], pt, [kp], ["hto%d" % (g % 2)])
            P.dma("sp", ss_out[g * 512:(g + 1) * 512, :].rearrange("(j p) n -> p j n", p=128),
                  hto[g % 2].rearrange("p (j n) -> p j n", j=4), reads=["hto%d" % (g % 2)], writes=["sso"])
        A.release(m)
    task(None, final_out)

    wt = [i for i, (l, c) in enumerate(TASKS) if l is not None]
    slot_of = {ti: (n % NSLOT) for n, ti in enumerate(wt)}

    def do_load(ti):
        sl = slot_of[ti]
        for (o, i_, sfx) in TASKS[ti][0](wslots[sl]):
            P.dma("pool", o, i_, writes=["ws%d%s" % (sl, sfx)])

    nxt = 0
    if wt:
        do_load(wt[0]); nxt = 1
    for ti, (l, c) in enumerate(TASKS):
        if l is not None:
            if nxt < len(wt):
                do_load(wt[nxt]); nxt += 1
            sl = slot_of[ti]
            c(wslots[sl], "ws%d" % sl)
        else:
            c()
    P.finalize(st)
    return nc, P


_CACHE = {}


def _host_consts():
    ident = np.eye(128, dtype=np.float32)
    tri = np.triu(np.ones((128, 128), np.float32))
    R = np.zeros((128, 128), np.float32)
    for m in range(128):
        if m % 64 < 32:
            R[m + 32, m] = -1.0
        else:
            R[m - 32, m] = 1.0
    return np.ascontiguousarray(np.stack([ident, tri, np.ones((128, 128), np.float32), R, tri, ident], 1))


def _rope_tables(s0):
    inv = (10000.0 ** (-np.arange(32, dtype=np.float32) / 32)).astype(np.float32)
    cosT = np.zeros((128, 4, TP + NS), np.float32)
    sinT = np.zeros((128, 4, TP + NS), np.float32)
    for pi in range(4):
        pos = (s0 - 1024 + pi * 512 + np.arange(512)).astype(np.float32)
        pos = np.concatenate([pos, np.full((NS,), 16384.0, np.float32)])
        ang = pos[None, :] * inv[:, None]
        c = np.cos(ang).astype(np.float32); sn = np.sin(ang).astype(np.float32)
        idx = np.arange(128) % 32
        cosT[:, pi, :] = c[idx]; sinT[:, pi, :] = sn[idx]
    return cosT, sinT


def kernel(x_prompt, x_sample, cache_win_k, cache_win_v, state_conv, state_ssm,
           norm_mix_pre, norm_mix_post, w_in, attn_sinks, w_attn_branch, conv_w, conv_b,
           dt_bias, a_log, d_skip, ssm_norm, w_ssm_branch, w_out,
           norm_ffn_pre, norm_ffn_post, w_gate_up, w_down):
    f = lambda a: np.ascontiguousarray(np.asarray(a, dtype=np.float32))
    if "nc" not in _CACHE:
        _CACHE["nc"] = build_program()[0]
    nc = _CACHE["nc"]
    xp = f(x_prompt); xs = f(x_sample)
    nws = np.stack([f(norm_mix_pre)[0], f(norm_mix_post)[0], f(norm_ffn_pre)[0], f(norm_ffn_post)[0]], 0)
    nw_l = np.ascontiguousarray(nws.reshape(4, 16, 128).transpose(2, 0, 1))
    cw = f(conv_w)[0]
    convw_l = np.ascontiguousarray(cw.reshape(4, 48, 128).transpose(2, 1, 0))
    convb_l = np.ascontiguousarray(f(conv_b)[0].reshape(48, 128).T)
    dtb_l = np.ascontiguousarray(f(dt_bias)[0].reshape(8, 8).T)
    dsk_l = np.ascontiguousarray(np.repeat(f(d_skip)[0], 64).reshape(32, 128).T)
    snw_l = np.ascontiguousarray(f(ssm_norm)[0].reshape(32, 128).T)
    consts = _host_consts()
    ii = np.arange(128)[:, None]; jj = np.arange(128)[None, :]
    mprev = np.where(jj > ii, 0.0, NEG).astype(np.float32); mcur = np.where(jj <= ii, 0.0, NEG).astype(np.float32)
    m_std = np.concatenate([mprev, mcur], 1)
    m_none = np.concatenate([np.full((128, 128), NEG, np.float32), mcur], 1)
    shared = {"w_in": f(w_in)[0], "w_ab": f(w_attn_branch)[0], "w_sb": f(w_ssm_branch)[0], "w_o": f(w_out)[0],
              "w_gu": f(w_gate_up)[0], "w_dn": f(w_down)[0], "nw": nw_l, "convw": convw_l, "convb": convb_l,
              "dtb": dtb_l, "alog": f(a_log), "dsk": dsk_l, "snw": snw_l, "sinks": f(attn_sinks), "consts": consts}
    in_maps = []
    for c in range(8):
        b = c // 2; hf = c % 2
        xa = np.zeros((2048, D), np.float32)
        if hf == 1:
            xa[0:1024] = xp[b, 0:1024]
        xa[1024:2048] = xp[b, hf * 1024:(hf + 1) * 1024]
        cosT, sinT = _rope_tables(hf * 1024)
        m = dict(shared)
        m.update({"x_all": xa, "x_smp": np.ascontiguousarray(xs[c * NS:(c + 1) * NS, 0, :]),
                  "ck": np.ascontiguousarray(f(cache_win_k)[0, c * NS:(c + 1) * NS]), "cv": np.ascontiguousarray(f(cache_win_v)[0, c * NS:(c + 1) * NS]),
                  "sconv": np.ascontiguousarray(f(state_conv)[0, c * NS:(c + 1) * NS]), "sssm": np.ascontiguousarray(f(state_ssm)[0, c * NS:(c + 1) * NS]),
                  "cosT": cosT, "sinT": sinT,
                  "masks": np.ascontiguousarray(np.stack([m_std, m_std if hf == 1 else m_none], 1)),
                  "flag": np.full((128, 1), float(hf), np.float32)})
        in_maps.append(m)
    res = run_bass_kernel_spmd(nc, in_maps, core_ids=list(range(8))).results
    y_p = np.zeros((4, 2048, D), np.float32)
    for c in range(8):
        y_p[c // 2, (c % 2) * 1024:(c % 2 + 1) * 1024] = res[c]["y_own"]
    y_s = np.concatenate([res[c]["y_smp"] for c in range(8)], 0).reshape(128, 1, D)
    odd = [1, 3, 5, 7]
    wk = np.stack([res[c]["wk"].reshape(128, 8, 64) for c in odd], 0)[None]
    wv = np.stack([res[c]["wv"].reshape(128, 8, 64) for c in odd], 0)[None]
    cvp = np.stack([res[c]["convo"] for c in odd], 0)[None]
    ssp = np.stack([res[c]["ssmo"].reshape(64, 64, 128) for c in odd], 0)[None]
    wks = np.concatenate([res[c]["wks"].reshape(NS, 128, 8, 64) for c in range(8)], 0)[None]
    wvs = np.concatenate([res[c]["wvs"].reshape(NS, 128, 8, 64) for c in range(8)], 0)[None]
    cvs = np.concatenate([res[c]["convs"] for c in range(8)], 0)[None]
    sss = np.concatenate([res[c]["ssms"] for c in range(8)], 0)[None]
    return (y_p, y_s, wk, wv, cvp, ssp, wks, wvs, cvs, sss)
```

```python
import numpy as np
import concourse.bass as bass
import concourse.mybir as mybir

F32 = mybir.dt.float32
BF16 = mybir.dt.bfloat16
AF = mybir.ActivationFunctionType
ALU = mybir.AluOpType
AX = mybir.AxisListType

KQ = 12
STRICT_SAME_ENGINE = False


class _Op(object):
    __slots__ = ("eng", "fn", "reads", "writes", "dma", "deps", "sig", "ev", "qidx", "waits")


class Prog(object):
    ENGS = ("pe", "act", "dve", "pool", "sp")

    def __init__(self, nc):
        self.nc = nc
        self.ops = []
        self.nbar = 0

    def op(self, eng, fn, reads=(), writes=()):
        o = _Op()
        o.eng = eng; o.fn = fn; o.reads = tuple(reads); o.writes = tuple(writes)
        o.dma = False; o.sig = False; o.ev = None; o.qidx = -1
        self.ops.append(o)
        return o

    def dma(self, q, out, in_, reads=(), writes=()):
        o = _Op()
        o.eng = q
        o.fn = (lambda e, out=out, in_=in_: e.dma_start(out=out, in_=in_))
        o.reads = tuple(reads); o.writes = tuple(writes)
        o.dma = True; o.sig = True; o.ev = None; o.qidx = -1
        self.ops.append(o)
        return o

    def barrier(self, tiny):
        n = self.nbar
        self.nbar += 1
        dq = [("dq", q, i) for q in ("sp", "act", "pool") for i in range(KQ)]
        for e in self.ENGS:
            o = self.op(e, tiny[e], reads=(dq + ["scr"] if e == "sp" else ["scr"]), writes=[("bar1", n, e)])
            if e == "sp":
                o.dma = True; o.sig = True
        for e in self.ENGS:
            o = self.op(e, tiny[e], reads=[("bar1", n, e2) for e2 in self.ENGS], writes=[("bar2", n, e)])
            if e == "sp":
                o.dma = True; o.sig = True

    def finalize(self, stack):
        nc = self.nc
        ops = self.ops
        esem = {e: stack.enter_context(nc.semaphore("se_" + e)) for e in self.ENGS}
        dsem = {q: [stack.enter_context(nc.semaphore("sd_%s%d" % (q, i))) for i in range(KQ)]
                for q in ("sp", "act", "pool")}
        semobj = {}
        for e in self.ENGS:
            semobj[("e", e)] = esem[e]
        for q in dsem:
            for i in range(KQ):
                semobj[("d", q, i)] = dsem[q][i]

        last_w = {}
        readers = {}
        dq_hist = {"sp": [], "act": [], "pool": []}
        for j, op in enumerate(ops):
            deps = {}
            for k in op.reads:
                i = last_w.get(k)
                if i is not None:
                    deps[i] = True
            for k in op.writes:
                i = last_w.get(k)
                if i is not None:
                    o = ops[i]
                    if o.dma or op.dma or o.eng != op.eng or (STRICT_SAME_ENGINE and op.eng != "pe"):
                        deps[i] = True
                for i in readers.get(k, {}).values():
                    o = ops[i]
                    if o.dma or op.dma or o.eng != op.eng or (STRICT_SAME_ENGINE and op.eng != "pe"):
                        deps[i] = True
            if op.dma:
                h = dq_hist[op.eng]
                op.qidx = len(h)
                op.writes = op.writes + (("dq", op.eng, op.qidx % KQ),)
                if op.qidx >= KQ:
                    deps[h[op.qidx - KQ]] = True
                h.append(j)
            deps.pop(j, None)
            op.deps = sorted(deps, reverse=True)
            rid = (op.eng, op.qidx % KQ) if op.dma else op.eng
            for k in op.reads:
                readers.setdefault(k, {})[rid] = j
            for k in op.writes:
                last_w[k] = j
                readers[k] = {}
        for op in ops:
            for i in op.deps:
                ops[i].sig = True
        cnt = {e: 0 for e in self.ENGS}
        for op in ops:
            if op.dma:
                op.ev = (("d", op.eng, op.qidx % KQ), 16 * (op.qidx // KQ + 1))
            elif op.sig:
                cnt[op.eng] += 1
                op.ev = (("e", op.eng), cnt[op.eng])
        know = {e: {} for e in self.ENGS}
        snap = {}
        nw = 0
        for op in ops:
            kn = know[op.eng]
            waits = []
            for i in op.deps:
                sid, val = ops[i].ev
                if kn.get(sid, 0) >= val:
                    continue
                waits.append((sid, val))
                for s, v in snap[(sid, val)].items():
                    if kn.get(s, 0) < v:
                        kn[s] = v
            op.waits = waits
            nw += len(waits)
            if op.sig:
                s = dict(kn)
                s[op.ev[0]] = op.ev[1]
                snap[op.ev] = s
        self.stats = dict(n_ops=len(ops), n_waits=nw, cnt=dict(cnt),
                          ndma={q: len(h) for q, h in dq_hist.items()})
        final_waits = []
        for q, h in dq_hist.items():
            for j in h[-KQ:]:
                final_waits.append(ops[j].ev)

        block = stack.enter_context(nc.Block())

        def emit(e, eng, extra=None):
            for op in ops:
                if op.eng != eng:
                    continue
                for sid, val in op.waits:
                    e.wait_ge(semobj[sid], val)
                ins = op.fn(e)
                if op.sig:
                    if op.dma:
                        ins.then_inc(semobj[op.ev[0]], 16)
                    else:
                        ins.then_inc(semobj[op.ev[0]], 1)
            if extra:
                for sid, val in extra:
                    e.wait_ge(semobj[sid], val)

        @block.tensor
        def _(e):
            emit(e, "pe")

        @block.scalar
        def _(e):
            emit(e, "act")

        @block.vector
        def _(e):
            emit(e, "dve")

        @block.gpsimd
        def _(e):
            emit(e, "pool")

        @block.sync
        def _(e):
            emit(e, "sp", final_waits)


from contextlib import ExitStack
from concourse.bass_utils import run_bass_kernel_spmd

D = 2048; NH = 32; NKV = 8; HD = 64; DI = 4096; NSH = 64; NST = 128; NG = 8
CONVD = 6144; DFF = 5632; EPS = 1e-6
OQ = 0; OK_ = 2048; OV = 2560; OZ = 3072; OX = 7168; OB = 11264; OC = 12288; ODT = 13312; OGA = 13376; OGS = 15424
INP = 17472
TP = 512
NS = 16
NEG = -30000.0
SB_BYTES = 212480


def _prod(s):
    r = 1
    for v in s:
        r *= v
    return r


class Arena(object):
    def __init__(self, nc, stack, name, nbytes):
        self.t = stack.enter_context(nc.sbuf_tensor(name, [128, nbytes // 4], F32))
        self.ap = self.t[:]
        self.cap = nbytes
        self.top = 0

    def alloc(self, free_shape, dtype, parts=128):
        n = _prod(free_shape)
        nb = n * (4 if dtype == F32 else 2)
        nb = (nb + 63) // 64 * 64
        off = self.top
        self.top += nb
        assert self.top <= self.cap, ("SBUF arena overflow", self.top, self.cap)
        v = self.ap[:, off // 4:(off + nb) // 4]
        if dtype != F32:
            v = v.bitcast(dtype)
        v = v[:, 0:n]
        if len(free_shape) == 2:
            v = v.rearrange("p (a b) -> p a b", a=free_shape[0])
        elif len(free_shape) == 3:
            v = v.rearrange("p (a b c) -> p a b c", a=free_shape[0], b=free_shape[1])
        return v

    def mark(self):
        return self.top

    def release(self, m):
        self.top = m


def build_program(dbg=None, opts=()):
    nc = bass.Bass("TRN2", target_bir_lowering=False)
    st = ExitStack()
    P = Prog(nc)

    def din(name, shape):
        return nc.dram_tensor(name, list(shape), F32, kind="ExternalInput").ap()

    def dout(name, shape):
        return nc.dram_tensor(name, list(shape), F32, kind="ExternalOutput").ap()

    x_all = din("x_all", [2048, D])
    x_smp = din("x_smp", [NS, D])
    ck = din("ck", [NS, 128, NKV, HD]); cv = din("cv", [NS, 128, NKV, HD])
    sconv = din("sconv", [NS, 3, CONVD]); sssm = din("sssm", [NS, NSH, HD, NST])
    w_in = din("w_in", [D, INP]); w_ab = din("w_ab", [D, D]); w_sb = din("w_sb", [DI, D])
    w_o = din("w_o", [D, D]); w_gu = din("w_gu", [D, 2 * DFF]); w_dn = din("w_dn", [DFF, D])
    nw_in = din("nw", [128, 4, 16])
    convw_in = din("convw", [128, 48, 4]); convb_in = din("convb", [128, 48])
    dtb_in = din("dtb", [8, 8]); alog_in = din("alog", [1, 64]); dsk_in = din("dsk", [128, 32])
    snw_in = din("snw", [128, 32]); sink_in = din("sinks", [1, 32])
    cos_in = din("cosT", [128, 4, TP + NS]); sin_in = din("sinT", [128, 4, TP + NS])
    cst_in = din("consts", [128, 6, 128])
    msk_in = din("masks", [128, 2, 256])
    flag_in = din("flag", [128, 1])

    y_out = dout("y_own", [1024, D]); ys_out = dout("y_smp", [NS, D])
    wk_out = dout("wk", [128, 512]); wv_out = dout("wv", [128, 512])
    cv_out = dout("convo", [3, CONVD]); ss_out = dout("ssmo", [DI, NST])
    wks_out = dout("wks", [NS, 128, 512]); wvs_out = dout("wvs", [NS, 128, 512])
    cvs_out = dout("convs", [NS, 3, CONVD]); sss_out = dout("ssms", [NS, NSH, HD, NST])
    scr_x = nc.dram_tensor("scr_x", [NS, CONVD], F32).ap()
    scr_dt = nc.dram_tensor("scr_dt", [NS, 64], F32).ap()
    scr_y = nc.dram_tensor("scr_y", [NS, DI], F32).ap()
    scr_q = nc.dram_tensor("scr_q", [NS, D], F32).ap()
    scr_k = nc.dram_tensor("scr_k", [NS, 512], F32).ap()
    scr_v = nc.dram_tensor("scr_v", [NS, 512], F32).ap()
    scr_o = nc.dram_tensor("scr_o", [NS, D], F32).ap()
    dbg_out = {}
    if dbg:
        for k, shp in dbg.items():
            dbg_out[k] = dout("dbg_" + k, shp)

    A = Arena(nc, st, "arena", SB_BYTES)
    cst = A.alloc([6, 128], F32)
    ident = cst[:, 0, :]; triU = cst[:, 1, :]; ones_f = cst[:, 2, :]; rotR = cst[:, 3, :]; ssdmask = cst[:, 4, :]
    cst_b = A.alloc([6, 128], BF16)
    ident_b = cst_b[:, 0, :]; ones_b = cst_b[:, 2, :]; ssdmask_b = cst_b[:, 4, :]
    masks = A.alloc([2, 256], F32)
    nw = A.alloc([4, 16], F32)
    convw = A.alloc([48, 4], F32); convb = A.alloc([48], F32)
    dtb = A.alloc([8], F32); dsk = A.alloc([32], F32); snw = A.alloc([32], F32)
    negA = A.alloc([64], F32)
    sink8 = A.alloc([32], F32)
    flag = A.alloc([1], F32)
    cosT = A.alloc([544], F32); sinT = A.alloc([544], F32)
    scr = A.alloc([128], F32)
    kTc = A.alloc([4, 128], BF16); Vc = A.alloc([512], BF16)
    convc = A.alloc([48, 3], F32)
    hT = A.alloc([NG, 512], F32)
    NSLOT = 2
    wslots = [A.alloc([8192], BF16) for _ in range(NSLOT)]
    hnT = A.alloc([16, 544], BF16)
    base_mark = A.mark()

    ps = [st.enter_context(nc.psum_tensor("ps%d" % i, [128, 512], F32))[:] for i in range(8)]
    psn = [0]

    def PS():
        i = psn[0] % 8
        psn[0] += 1
        return ps[i], "ps%d" % i

    tiny = {
        "pe": lambda e: e.matmul(ps[7][0:1, 0:1], lhsT=cst_b[0:1, 0, 0:1], rhs=cst_b[0:1, 0, 0:1], start=True, stop=True),
        "act": lambda e: e.activation(out=scr[0:1, 0:1], in_=scr[0:1, 16:17], func=AF.Copy),
        "dve": lambda e: e.memset(scr[0:1, 32:33], 0.0),
        "pool": lambda e: e.memset(scr[0:1, 48:49], 0.0),
        "sp": lambda e: e.dma_start(out=scr[0:1, 64:72], in_=scr[0:1, 96:104]),
    }

    def barrier():
        P.barrier(tiny)

    P.dma("sp", cst, cst_in, writes=["cst"])
    P.dma("pool", cst_b, cst_in, writes=["cst_b"])
    P.dma("sp", masks, msk_in, writes=["masks"])
    P.dma("sp", nw, nw_in, writes=["nw"])
    P.dma("sp", convw, convw_in, writes=["convw"]); P.dma("sp", convb, convb_in, writes=["convb"])
    P.dma("sp", dtb[0:8, :], dtb_in, writes=["dtb"]); P.dma("sp", dsk, dsk_in, writes=["dsk"])
    P.dma("sp", snw, snw_in, writes=["snw"]); P.dma("sp", flag, flag_in, writes=["flag"])
    P.dma("sp", negA, alog_in.partition_broadcast(128), writes=["negA"])
    P.dma("sp", sink8, sink_in.partition_broadcast(128), writes=["sink8"])
    P.op("act", lambda e: e.activation(out=negA, in_=negA, func=AF.Exp), reads=["negA"], writes=["negA"])
    P.op("dve", lambda e: e.tensor_scalar(out=negA, in0=negA, scalar1=-1.0, scalar2=None, op0=ALU.mult), reads=["negA"], writes=["negA"])
    P.op("dve", lambda e: e.tensor_scalar(out=sink8, in0=sink8, scalar1=8.0, scalar2=None, op0=ALU.mult), reads=["sink8"], writes=["sink8"])
    P.op("dve", lambda e: e.memset(hT, 0.0), writes=["hT"])
    P.op("dve", lambda e: e.memset(scr, 0.0), writes=["scr"])
    P.op("dve", lambda e: e.memset(kTc, 0.0), writes=["kTc"])
    P.op("dve", lambda e: e.memset(Vc, 0.0), writes=["Vc"])
    P.op("dve", lambda e: e.memset(convc, 0.0), writes=["convc"])


    T = 544
    TASKS = []

    CUR = [TASKS]

    def task(loads, compute):
        CUR[0].append((loads, compute))

    pss_n = [0]

    def PSS():
        i = psn[0] % 7
        psn[0] += 1
        return ps[i][:, 0:16], "ps%d" % i

    def PS7():
        i = psn[0] % 7
        psn[0] += 1
        return ps[i], "ps%d" % i

    def ranges(Tn):
        return [(0, TP)] if Tn == TP else [(0, Tn // 2), (Tn // 2, Tn)]

    def op(eng, fn, reads, writes):
        return P.op(eng, fn, reads=reads, writes=writes)

    def ACT(out, in_, func, reads, writes, **kw):
        op("act", lambda e: e.activation(out=out, in_=in_, func=func, **kw), reads, writes)

    def TT(eng, out, in0, in1, o, reads, writes):
        op(eng, lambda e: e.tensor_tensor(out=out, in0=in0, in1=in1, op=o), reads, writes)

    def TS(eng, out, in0, s1, s2, o0, o1, reads, writes):
        if o1 is None:
            op(eng, lambda e: e.tensor_scalar(out=out, in0=in0, scalar1=s1, scalar2=None, op0=o0), reads, writes)
        else:
            op(eng, lambda e: e.tensor_scalar(out=out, in0=in0, scalar1=s1, scalar2=s2, op0=o0, op1=o1), reads, writes)

    def STT(eng, out, in0, sc, in1, o0, o1, reads, writes):
        op(eng, lambda e: e.scalar_tensor_tensor(out=out, in0=in0, scalar=sc, in1=in1, op0=o0, op1=o1), reads, writes)

    def MM(out, lhsT, rhs, start, stop, reads, writes):
        op("pe", lambda e: e.matmul(out, lhsT=lhsT, rhs=rhs, start=start, stop=stop), reads, writes)

    def TR(out, in_, idn, reads, writes):
        op("pe", lambda e: e.transpose(out, in_, idn), reads, writes)

    def CP(eng, out, in_, reads, writes):
        if eng == "act":
            ACT(out, in_, AF.Copy, reads, writes)
        else:
            op(eng, lambda e: e.tensor_copy(out=out, in_=in_), reads, writes)

    stg = A.alloc([8, 16], F32)
    stg_n = [0]

    def proj(M, lhsT_of, rhs_of, nk, Tn, rd):
        res = []
        for (c0, c1) in ranges(Tn):
            if c1 - c0 > 16:
                pt, kp = PS7()
            else:
                pt, kp = PSS()
            o = pt[0:M, 0:c1 - c0]
            for kc in range(nk):
                MM(o, lhsT_of(kc), rhs_of(kc, c0, c1), kc == 0, kc == nk - 1, rd, [kp])
            if c1 - c0 <= 16:
                i = stg_n[0] % 8
                stg_n[0] += 1
                so = stg[0:M, i, 0:c1 - c0]
                CP("act", so, o, [kp], ["stg%d" % i])
                res.append((so, "stg%d" % i, c0, c1))
            else:
                res.append((o, kp, c0, c1))
        return res

    def wview(slot, nk, cw):
        return slot[:, 0:nk * cw].rearrange("p (k c) -> p k c", k=nk)

    def wloads_plain(W, c0, cw, nk, col_off=0, tot=None, sfx_default="a"):
        tot = tot or cw
        def f(slot):
            v = wview(slot, nk, tot)
            src = W[:, c0:c0 + cw].rearrange("(k p) c -> p k c", p=128)
            out = []
            step = max(1, 4096 // (cw * 4) * 4) if cw * 4 < 1024 else 4
            step = min(nk, max(4, step))
            if cw >= 256 and col_off == 0 and tot == cw:
                hw = cw // 2
                for (ca, sfx) in ((0, "a"), (hw, "b")):
                    for k0 in range(0, nk, nk // 2):
                        out.append((v[:, k0:k0 + nk // 2, ca:ca + hw], src[:, k0:k0 + nk // 2, ca:ca + hw], sfx))
                return out
            for k0 in range(0, nk, step):
                k1 = min(nk, k0 + step)
                out.append((v[:, k0:k1, col_off:col_off + cw], src[:, k0:k1, :], sfx_default))
            return out
        return f

    def fm_rstd(src, ksrc, Tn, sq, rstd, nfeat):
        ACT(sq[:, :, 0:Tn], src[:, :, 0:Tn], AF.Square, [ksrc], ["sq"])
        res = proj(128, lambda kc: ones_b, lambda kc, c0, c1: sq[:, kc, c0:c1], 16, Tn, ["sq", "cst_b"])
        for (o, kp, c0, c1) in res:
            TS("dve", rstd[:, c0:c1], o, 1.0 / nfeat, EPS, ALU.mult, ALU.add, [kp], ["rstd"])
        ACT(rstd[:, 0:Tn], rstd[:, 0:Tn], AF.Ln, ["rstd"], ["rstd"])
        ACT(rstd[:, 0:Tn], rstd[:, 0:Tn], AF.Exp, ["rstd"], ["rstd"], scale=-0.5)

    def emit_pass(pi, full, x_row0, has_s, y_row0, first_block_mask, dbgtag=None, kv=True):
        Tn = TP + (NS if has_s else 0)
        last = (pi == 3)
        pm = A.mark()

        def s01(slot=None, sk=None):
            m = A.mark()
            xT = A.alloc([16, T], F32); sq = A.alloc([16, T], BF16); rstd = A.alloc([T], F32)
            xtok = [A.alloc([D], F32) for _ in range(2)]
            tiles = [(x_all[x_row0 + t * 128:x_row0 + (t + 1) * 128, :], 128, t * 128) for t in range(4)]
            if has_s:
                tiles.append((x_smp, NS, TP))
            P.dma("sp", cosT[:, 0:TP + NS], cos_in[:, pi, :], writes=["cosT"])
            P.dma("sp", sinT[:, 0:TP + NS], sin_in[:, pi, :], writes=["sinT"])
            for ti, (src, n, c0) in enumerate(tiles):
                xt = xtok[ti % 2]; kx = "xtok%d" % (ti % 2)
                P.dma("sp", xt[0:n, :], src, writes=[kx])
                for q in range(4):
                    pt, kp = PS7()
                    for jj in range(4):
                        c = q * 4 + jj
                        TR(pt[:, jj * 128:jj * 128 + n], xt[0:n, c * 128:(c + 1) * 128], ident[0:n, 0:n], [kx, "cst"], [kp])
                    CP("act", xT[:, q * 4:(q + 1) * 4, c0:c0 + n],
                       pt.rearrange("p (a b) -> p a b", a=4)[:, :, 0:n], [kp], ["xT"])
            fm_rstd(xT, "xT", Tn, sq, rstd, D)
            for kc in range(16):
                STT("dve", hnT[:, kc, 0:Tn], xT[:, kc, 0:Tn], nw[:, 0, kc:kc + 1], rstd[:, 0:Tn],
                    ALU.mult, ALU.mult, ["xT", "rstd", "nw"], ["hnT"])
            if dbgtag and "hnT" in dbg_out:
                hd = A.alloc([16, T], F32)
                CP("dve", hd, hnT, ["hnT"], ["hd"])
                P.dma("sp", dbg_out["hnT"], hd, reads=["hd"])
            barrier()
            A.release(m)
        task(None, s01)

        B = {}

        att_list = []; ssd_list = []
        CUR[0] = att_list

        def s2_alloc():
            barrier()
            A.release(B["m_ssd"])
            B["attnT"] = A.alloc([16, T], BF16)
            B["m_att"] = A.mark()
            B["qT"] = A.alloc([16, T], BF16)
            B["kT"] = A.alloc([4, 128 + T], BF16)
            B["kfl"] = A.alloc([4, T], F32)
            B["Vt"] = A.alloc([5, 512], BF16)
            B["Vlast"] = A.alloc([512], F32)
            B["vs_tok"] = A.alloc([512], F32)
            B["qsf"] = A.alloc([16, NS], F32)
            for nm in ("qf", "qr", "t1", "t2"):
                B[nm] = A.alloc([T], F32)
            CP("dve", B["kT"][:, :, 0:128], kTc, ["kTc"], ["kT"])
            CP("dve", B["Vt"][:, 0, :], Vc, ["Vc"], ["Vt0"])
        task(None, s2_alloc)

        def rope_chunk(res, kq):
            qf, qr, t1, t2 = B["qf"], B["qr"], B["t1"], B["t2"]
            for (o, kp, c0, c1) in res:
                CP("act", qf[:, c0:c1], o, [kp], ["qf"])
            if "no_rope" in opts:
                CP("dve", qr[:, 0:Tn], qf[:, 0:Tn], ["qf"], ["qr"])
                return
            for (c0, c1) in ranges(Tn):
                pt, kp2 = PS7() if c1 - c0 > 16 else PSS()
                o2 = pt[:, 0:c1 - c0]
                for a0 in range(c0, c1, 256):
                    a1 = min(c1, a0 + 256)
                    MM(o2[:, a0 - c0:a1 - c0], rotR, qf[:, a0:a1], True, True, ["qf", "cst"], [kp2])
                if c1 - c0 <= 16:
                    i = stg_n[0] % 8
                    stg_n[0] += 1
                    CP("act", stg[:, i, 0:c1 - c0], o2, [kp2], ["stg%d" % i])
                    TT("dve", t2[:, c0:c1], stg[:, i, 0:c1 - c0], sinT[:, c0:c1], ALU.mult, ["stg%d" % i, "sinT"], ["t2"])
                else:
                    TT("dve", t2[:, c0:c1], o2, sinT[:, c0:c1], ALU.mult, [kp2, "sinT"], ["t2"])
            TT("pool", t1[:, 0:Tn], qf[:, 0:Tn], cosT[:, 0:Tn], ALU.mult, ["qf", "cosT"], ["t1"])
            TT("pool", qr[:, 0:Tn], t1[:, 0:Tn], t2[:, 0:Tn], ALU.add, ["t1", "t2"], ["qr"])

        if full and "no_q" not in opts:
            for jb in range(4):
                def loads_q(slot, jb=jb):
                    v = wview(slot, 16, 512).rearrange("p k (i h d) -> p k i h d", i=4, h=2)
                    src = w_in[:, OQ + jb * 512:OQ + (jb + 1) * 512].rearrange("(k p) (h i d) -> p k i h d", p=128, h=2, i=4)
                    out = []
                    for i in range(4):
                        for h in range(2):
                            out.append((v[:, :, i, h, :], src[:, :, i, h, :], "a" if i < 2 else "b"))
                    return out

                def comp_q(slot, sk, jb=jb):
                    wv_ = wview(slot, 16, 512)
                    for i in range(4):
                        c = 4 * jb + i
                        res = proj(128, lambda kc: wv_[:, kc, i * 128:(i + 1) * 128], lambda kc, c0, c1: hnT[:, kc, c0:c1], 16, Tn, ["hnT", sk + ("a" if i < 2 else "b")])
                        rope_chunk(res, "q")
                        CP("act", B["qT"][:, c, 0:Tn], B["qr"][:, 0:Tn], ["qr"], ["qT"])
                        if has_s:
                            CP("dve", B["qsf"][:, c, :], B["qr"][:, TP:Tn], ["qr"], ["qsf"])
                task(loads_q, comp_q)

        def comp_k(slot, sk):
            wv_ = wview(slot, 16, 512)
            for c in range(4):
                res = proj(128, lambda kc: wv_[:, kc, c * 128:(c + 1) * 128], lambda kc, c0, c1: hnT[:, kc, c0:c1], 16, Tn, ["hnT", sk + ("a" if c < 2 else "b")])
                rope_chunk(res, "k")
                CP("act", B["kT"][:, c, 128:128 + Tn], B["qr"][:, 0:Tn], ["qr"], ["kT"])
                CP("dve", B["kfl"][:, c, 0:Tn], B["qr"][:, 0:Tn], ["qr"], ["kfl"])
        if "no_k" not in opts:
            task(wloads_plain(w_in, OK_, 512, 16), comp_k)

        def comp_v(slot, sk):
            wv_ = wview(slot, 16, 512)
            for tt in range(4):
                pt, kp = PS7()
                for kc in range(16):
                    MM(pt, hnT[:, kc, tt * 128:(tt + 1) * 128], wv_[:, kc, :], kc == 0, kc == 15, ["hnT", sk + "a", sk + "b"], [kp])
                CP("act", B["Vt"][:, 1 + tt, :], pt, [kp], ["Vt%d" % (1 + tt)])
                if tt == 3 and "no_vlast" not in opts:
                    CP("act", B["Vlast"], pt, [kp], ["Vlast"])
            if has_s:
                pt, kp = PS7()
                for kc in range(16):
                    MM(pt[0:NS, :], hnT[:, kc, TP:Tn], wv_[:, kc, :], kc == 0, kc == 15, ["hnT", sk + "a", sk + "b"], [kp])
                CP("act", B["vs_tok"][0:NS, :], pt[0:NS, :], [kp], ["vs_tok"])
        if "no_v" not in opts:
            task(wloads_plain(w_in, OV, 512, 16), comp_v)

        def s3(slot=None, sk=None):
            m = A.mark()
            s_sb = A.alloc([2, 256], F32); Pb = A.alloc([2, 256], BF16); PT = A.alloc([2, 2, 128], BF16)
            rmx = A.alloc([2], F32); ngm = A.alloc([2], F32); rsm = A.alloc([2], F32); es = A.alloc([2], F32)
            atok = A.alloc([D], BF16)
            qT, kT, Vt = B["qT"], B["kT"], B["Vt"]
            for blk in range(4):
                mk = masks[:, 1, :] if (blk == 0 and first_block_mask) else masks[:, 0, :]
                for hp in range(16):
                    pS, kS = PS7()
                    hs = (2 * hp, 2 * hp + 1)
                    for u, h in enumerate(hs):
                        kvh = h // 4; g = h % 4; jj = kvh // 2; half = kvh % 2
                        cq = 4 * jj + g
                        MM(pS[:, u * 256:(u + 1) * 256], qT[half * 64:(half + 1) * 64, cq, blk * 128:(blk + 1) * 128],
                           kT[half * 64:(half + 1) * 64, jj, blk * 128:blk * 128 + 256], True, True, ["qT", "kT"], [kS])
                    TT("dve", s_sb, pS.rearrange("p (a b) -> p a b", a=2), mk.unsqueeze(1).to_broadcast([128, 2, 256]), ALU.add,
                       [kS, "masks"], ["s_sb"])
                    op("dve", lambda e: e.tensor_reduce(out=rmx, in_=s_sb, axis=AX.X, op=ALU.max), ["s_sb"], ["rmx"])
                    TT("dve", rmx, rmx, sink8[:, 2 * hp:2 * hp + 2], ALU.max, ["rmx", "sink8"], ["rmx"])
                    TS("dve", ngm, rmx, -0.125, None, ALU.mult, None, ["rmx"], ["ngm"])
                    for u in range(2):
                        ACT(Pb[:, u, :], s_sb[:, u, :], AF.Exp, ["s_sb", "ngm"], ["Pb", "rsm"], scale=0.125, bias=ngm[:, u:u + 1], accum_out=rsm[:, u:u + 1])
                    TT("dve", es, sink8[:, 2 * hp:2 * hp + 2], rmx, ALU.subtract, ["rmx", "sink8"], ["es"])
                    ACT(es, es, AF.Exp, ["es"], ["es"], scale=0.125)
                    TT("dve", es, es, rsm, ALU.add, ["es", "rsm"], ["es"])
                    op("dve", lambda e: e.reciprocal(out=es, in_=es), ["es"], ["es"])
                    pT, kT_ = PS7()
                    pTb = pT.bitcast(BF16)
                    for u in range(2):
                        for kb in range(2):
                            TR(pTb[:, (u * 2 + kb) * 128:(u * 2 + kb + 1) * 128], Pb[:, u, kb * 128:(kb + 1) * 128], ident_b, ["Pb", "cst_b"], [kT_])
                    CP("act", PT, pTb[:, 0:512].rearrange("p (a b c) -> p a b c", a=2, b=2), [kT_], ["PT"])
                    pO, kO = PS7()
                    for u, h in enumerate(hs):
                        kvh = h // 4
                        for kb in range(2):
                            MM(pO[:, u * 64:(u + 1) * 64], PT[:, u, kb, :], Vt[:, blk + kb, kvh * 64:(kvh + 1) * 64], kb == 0, kb == 1,
                               ["PT", "Vt%d" % (blk + kb)], [kO])
                    for u, h in enumerate(hs):
                        ACT(atok[:, h * 64:(h + 1) * 64], pO[:, u * 64:(u + 1) * 64], AF.Copy, [kO, "es"], ["atok"], scale=es[:, u:u + 1])
                for q2 in range(2):
                    pT, kT_ = PS7()
                    pTb = pT.bitcast(BF16)
                    for c8 in range(8):
                        c = q2 * 8 + c8
                        TR(pTb[:, c8 * 128:(c8 + 1) * 128], atok[:, c * 128:(c + 1) * 128], ident_b, ["atok", "cst_b"], [kT_])
                    CP("act", B["attnT"][:, q2 * 8:(q2 + 1) * 8, blk * 128:(blk + 1) * 128], pTb.rearrange("p (a b) -> p a b", a=8), [kT_], ["attnT"])
            A.release(m)
            if dbgtag and "attnT" in dbg_out:
                hd = A.alloc([8, T], F32)
                for hh in range(2):
                    CP("dve", hd, B["attnT"][:, hh * 8:(hh + 1) * 8, :], ["attnT"], ["hd"])
                    P.dma("sp", dbg_out["attnT"][:, hh * 8:(hh + 1) * 8, :], hd[:, :, 0:TP + NS], reads=["hd"], writes=["hdo"])
        if full and "no_s3" not in opts:
            task(None, s3)

        def s3_carry(slot=None, sk=None):
            CP("dve", kTc, B["kT"][:, :, TP:TP + 128], ["kT"], ["kTc"])
            CP("dve", Vc, B["Vt"][:, 4, :], ["Vt4"], ["Vc"])
            if last:
                P.dma("sp", wv_out, B["Vlast"], reads=["Vlast"], writes=["wvo"])
                pt, kp = PS7()
                for c in range(4):
                    TR(pt[:, c * 128:(c + 1) * 128], B["kfl"][:, c, TP - 128:TP], ident, ["kfl", "cst"], [kp])
                CP("act", B["qf"][:, 0:512], pt, [kp], ["qf"])
                P.dma("sp", wk_out, B["qf"][:, 0:512], reads=["qf"], writes=["wko"])
        task(None, s3_carry)

        def smp_attn(slot=None, sk=None):
            qtok = A.alloc([D], F32); ktok = A.alloc([512], F32)
            for q4 in range(4):
                pt, kp = PS7()
                for jj in range(4):
                    TR(pt[0:NS, jj * 128:(jj + 1) * 128], B["qsf"][:, q4 * 4 + jj, :], ident, ["qsf", "cst"], [kp])
                for jj in range(4):
                    CP("act", qtok[0:NS, (8 * q4 + jj) * 64:(8 * q4 + jj + 1) * 64], pt[0:NS, jj * 128:jj * 128 + 64], [kp], ["qtok"])
                    CP("act", qtok[0:NS, (8 * q4 + 4 + jj) * 64:(8 * q4 + 5 + jj) * 64], pt[0:NS, jj * 128 + 64:(jj + 1) * 128], [kp], ["qtok"])
            pt, kp = PS7()
            for c in range(4):
                TR(pt[0:NS, c * 128:(c + 1) * 128], B["kfl"][:, c, TP:Tn], ident, ["kfl", "cst"], [kp])
            CP("act", ktok[0:NS, :], pt[0:NS, :], [kp], ["ktok"])
            P.dma("sp", scr_q, qtok[0:NS, :], reads=["qtok"], writes=["scr_q"])
            P.dma("sp", scr_k, ktok[0:NS, :], reads=["ktok"], writes=["scr_k"])
            P.dma("sp", scr_v, B["vs_tok"][0:NS, :], reads=["vs_tok"], writes=["scr_v"])
            P.dma("sp", wks_out[:, 127, :], ktok[0:NS, :], reads=["ktok"], writes=["wks1"])
            P.dma("sp", wvs_out[:, 127, :], B["vs_tok"][0:NS, :], reads=["vs_tok"], writes=["wvs1"])
            P.dma("sp", wks_out[:, 0:127, :], ck[:, 1:128, :, :].rearrange("b s k d -> b s (k d)"), writes=["wks0"])
            P.dma("sp", wvs_out[:, 0:127, :], cv[:, 1:128, :, :].rearrange("b s k d -> b s (k d)"), writes=["wvs0"])
            barrier()
            A.release(B["m_att"])
            qb = A.alloc([256], F32); kb_ = A.alloc([64], F32); vb = A.alloc([64], F32); sk8 = A.alloc([4], F32)
            P.dma("sp", qb, scr_q.rearrange("b (k f) -> (b k) f", k=8), reads=["scr_q"], writes=["qb"])
            P.dma("sp", kb_, scr_k.rearrange("b (k f) -> (b k) f", k=8), reads=["scr_k"], writes=["kb_"])
            P.dma("sp", vb, scr_v.rearrange("b (k f) -> (b k) f", k=8), reads=["scr_v"], writes=["vb"])
            for b in range(NS):
                P.dma("pool", sk8[b * 8:(b + 1) * 8, :], sink_in[0:1, :].rearrange("o (k g) -> (o k) g", g=4), writes=["sk8"])
            TS("dve", sk8, sk8, 8.0, None, ALU.mult, None, ["sk8"], ["sk8"])
            cbuf = [A.alloc([64, 64], F32) for _ in range(2)]
            prod = A.alloc([64, 64], F32)
            sc = A.alloc([4, 128], F32); pp = A.alloc([4, 128], F32)
            rmx = A.alloc([4], F32); ngm = A.alloc([4], F32); rsm = A.alloc([4], F32); es = A.alloc([4], F32)
            oacc = A.alloc([4, 64], F32); opart = A.alloc([4, 64], F32)
            for hf in range(2):
                cb_ = cbuf[hf]; kc_ = "cbuf%d" % hf
                for b in range(NS):
                    P.dma("pool", cb_[b * 8:(b + 1) * 8, :, :], ck[b, hf * 64:(hf + 1) * 64, :, :].rearrange("s k d -> k s d"), writes=[kc_])
                if hf == 0:
                    CP("dve", cb_[:, 0, :], kb_, ["kb_", kc_], [kc_])
                for g in range(4):
                    TT("pool" if g % 2 == 0 else "dve", prod, cb_, qb[:, g * 64:(g + 1) * 64].unsqueeze(1).to_broadcast([128, 64, 64]), ALU.mult, [kc_, "qb"], ["sprod"])
                    op("dve", lambda e, g=g, hf=hf: e.tensor_reduce(out=sc[:, g, hf * 64:(hf + 1) * 64], in_=prod, axis=AX.X, op=ALU.add), ["sprod"], ["ssc"])
            op("dve", lambda e: e.tensor_reduce(out=rmx, in_=sc, axis=AX.X, op=ALU.max), ["ssc"], ["srmx"])
            TT("dve", rmx, rmx, sk8, ALU.max, ["srmx", "sk8"], ["srmx"])
            TS("dve", ngm, rmx, -0.125, None, ALU.mult, None, ["srmx"], ["sngm"])
            for g in range(4):
                ACT(pp[:, g, :], sc[:, g, :], AF.Exp, ["ssc", "sngm"], ["spp", "srsm"], scale=0.125, bias=ngm[:, g:g + 1], accum_out=rsm[:, g:g + 1])
            TT("dve", es, sk8, rmx, ALU.subtract, ["sk8", "srmx"], ["ses"])
            ACT(es, es, AF.Exp, ["ses"], ["ses"], scale=0.125)
            TT("dve", es, es, rsm, ALU.add, ["ses", "srsm"], ["ses"])
            op("dve", lambda e: e.reciprocal(out=es, in_=es), ["ses"], ["ses"])
            for hf in range(2):
                cb_ = cbuf[hf]; kc_ = "cbuf%d" % hf
                for b in range(NS):
                    P.dma("pool", cb_[b * 8:(b + 1) * 8, :, :], cv[b, hf * 64:(hf + 1) * 64, :, :].rearrange("s k d -> k s d"), writes=[kc_])
                if hf == 0:
                    CP("dve", cb_[:, 0, :], vb, ["vb", kc_], [kc_])
                for g in range(4):
                    TT("pool" if g % 2 == 0 else "dve", prod, cb_, pp[:, g, hf * 64:(hf + 1) * 64].unsqueeze(2).to_broadcast([128, 64, 64]), ALU.mult, [kc_, "spp"], ["sprod"])
                    dst = oacc if hf == 0 else opart
                    op("dve", lambda e, g=g, dst=dst: e.tensor_reduce(out=dst[:, g, :], in_=prod.rearrange("p s d -> p d s"), axis=AX.X, op=ALU.add),
                       ["sprod"], ["soacc" if hf == 0 else "sopart"])
            TT("dve", oacc, oacc, opart, ALU.add, ["soacc", "sopart"], ["soacc"])
            TT("dve", oacc, oacc, es.unsqueeze(2).to_broadcast([128, 4, 64]), ALU.mult, ["soacc", "ses"], ["soacc"])
            P.dma("sp", scr_o.rearrange("b (k f) -> (b k) f", k=8), oacc.rearrange("p g d -> p (g d)"), reads=["soacc"], writes=["scr_o"])
            otok = A.alloc([D], F32)
            P.dma("sp", otok[0:NS, :], scr_o, reads=["scr_o"], writes=["otok"])
            pt, kp = PS7()
            ptb = pt
            for c in range(16):
                TR(pt[:, c * NS:(c + 1) * NS], otok[0:NS, c * 128:(c + 1) * 128], ident[0:NS, 0:NS], ["otok", "cst"], [kp])
            CP("act", B["attnT"][:, :, TP:Tn], pt[:, 0:16 * NS].rearrange("p (a b) -> p a b", a=16), [kp], ["attnT"])
        if has_s and full and "no_smp_attn" not in opts:
            task(None, smp_attn)

        CUR[0] = ssd_list

        def s4_alloc(slot=None, sk=None):
            B["m_pass"] = A.mark()
            B["ynT"] = A.alloc([32, T], BF16)
            if has_s:
                B["xpre_s"] = A.alloc([48, NS], F32); B["zs_s"] = A.alloc([32, NS], F32); B["dtT_s"] = A.alloc([8, NS], F32)
            B["m_ssd"] = A.mark()
            for nm in ("zs", "xs", "yT"):
                B[nm] = A.alloc([4, T], F32)
            B["xpre"] = A.alloc([4, 3 + T], F32)
            B["bcpre"] = A.alloc([2, 3 + T], F32)
            B["bcs"] = A.alloc([2, T], F32)
            B["BCT"] = A.alloc([2, T], BF16)
            B["dtT"] = A.alloc([T], F32)
            B["gsq"] = A.alloc([4, T], BF16)
            B["rstd2"] = A.alloc([T], F32)
            B["acc"] = A.alloc([T], F32)
            B["Xpad"] = [A.alloc([8, 128], BF16) for _ in range(2)]
            B["Xd"] = [A.alloc([512], BF16) for _ in range(2)]; B["Btok"] = [A.alloc([128], BF16) for _ in range(2)]
            for nm in ("dtk", "dA", "acum", "tot", "dec", "cd", "ndA", "nacum"):
                B[nm] = [A.alloc([8], F32) for _ in range(2)]
            B["dAtri"] = A.alloc([8, 128], F32)
            B["L"] = A.alloc([8, 128], F32); B["MT"] = A.alloc([8, 128], BF16)
            B["cbT"] = A.alloc([128], F32); B["Ebc"] = A.alloc([4, 128], F32)
            B["hTb"] = A.alloc([512], BF16); B["ytmp"] = A.alloc([4, 128], F32); B["htmp"] = A.alloc([512], F32)
            op("pool", lambda e: e.memset(B["Xpad"][0], 0.0), [], ["Xpad0"])
            op("pool", lambda e: e.memset(B["Xpad"][1], 0.0), [], ["Xpad1"])
        task(None, s4_alloc)

        def conv_silu(pre, c_in, ch, dst, kpre, kdst):
            acc = B["acc"]
            ACT(acc[:, 0:TP], pre[:, 0:TP], AF.Copy, [kpre, "convw"], ["acc"], scale=convw[:, ch, 0:1])
            for k in range(1, 4):
                STT("dve", acc[:, 0:TP], pre[:, k:k + TP], convw[:, ch, k:k + 1], acc[:, 0:TP], ALU.mult, ALU.add, [kpre, "acc", "convw"], ["acc"])
            ACT(dst[:, 0:TP], acc[:, 0:TP], AF.Silu, ["acc", "convb"], [kdst], bias=convb[:, ch:ch + 1])

        for g in range(NG):
            if full:
                def comp_z(slot, sk, g=g):
                    wv_ = wview(slot, 16, 512)
                    for i in range(4):
                        res = proj(128, lambda kc: wv_[:, kc, i * 128:(i + 1) * 128], lambda kc, c0, c1: hnT[:, kc, c0:c1], 16, Tn, ["hnT", sk + ("a" if i < 2 else "b")])
                        for (o, kp, c0, c1) in res:
                            ACT(B["zs"][:, i, c0:c1], o, AF.Silu, [kp], ["zs"])
                        if has_s:
                            CP("pool", B["zs_s"][:, 4 * g + i, :], B["zs"][:, i, TP:Tn], ["zs"], ["zs_s"])
                task(wloads_plain(w_in, OZ + g * 512, 512, 16), comp_z)

            def comp_x(slot, sk, g=g):
                wv_ = wview(slot, 16, 512)
                xpre = B["xpre"]
                CP("dve", xpre[:, :, 0:3], convc[:, 4 * g:4 * g + 4, :], ["convc"], ["xpre"])
                for i in range(4):
                    res = proj(128, lambda kc: wv_[:, kc, i * 128:(i + 1) * 128], lambda kc, c0, c1: hnT[:, kc, c0:c1], 16, Tn, ["hnT", sk + ("a" if i < 2 else "b")])
                    for (o, kp, c0, c1) in res:
                        CP("act", xpre[:, i, 3 + c0:3 + c1], o, [kp], ["xpre"])
                for i in range(4):
                    conv_silu(xpre[:, i, :], None, 4 * g + i, B["xs"][:, i, :], "xpre", "xs")
                CP("dve", convc[:, 4 * g:4 * g + 4, :], xpre[:, :, TP:TP + 3], ["xpre"], ["convc"])
                if has_s:
                    CP("pool", B["xpre_s"][:, 4 * g:4 * g + 4, :], xpre[:, :, 3 + TP:3 + Tn], ["xpre"], ["xpre_s"])
            task(wloads_plain(w_in, OX + g * 512, 512, 16), comp_x)

            def loads_bcdt(slot, g=g):
                f1 = wloads_plain(w_in, OB + g * 128, 128, 16, 0, 264, "a")(slot)
                f2 = wloads_plain(w_in, OC + g * 128, 128, 16, 128, 264, "b")(slot)
                f3 = wloads_plain(w_in, ODT + g * 8, 8, 16, 256, 264, "b")(slot)
                return f1 + f2 + f3

            def comp_ssd(slot, sk, g=g):
                wv_ = wview(slot, 16, 264)
                bcpre, bcs, BCT, dtT = B["bcpre"], B["bcs"], B["BCT"], B["dtT"]
                CP("dve", bcpre[:, 0, 0:3], convc[:, 32 + g, :], ["convc"], ["bcpre"])
                CP("dve", bcpre[:, 1, 0:3], convc[:, 40 + g, :], ["convc"], ["bcpre"])
                for i in range(2):
                    if i == 1 and not full and not kv:
                        continue
                    res = proj(128, lambda kc: wv_[:, kc, i * 128:(i + 1) * 128], lambda kc, c0, c1: hnT[:, kc, c0:c1], 16, Tn, ["hnT", sk + ("a" if i == 0 else "b")])
                    for (o, kp, c0, c1) in res:
                        CP("act", bcpre[:, i, 3 + c0:3 + c1], o, [kp], ["bcpre"])
                res = proj(8, lambda kc: wv_[:, kc, 256:264], lambda kc, c0, c1: hnT[:, kc, c0:c1], 16, Tn, ["hnT", sk + "b"])
                for (o, kp, c0, c1) in res:
                    ACT(dtT[0:8, c0:c1], o, AF.Exp, [kp, "dtb"], ["dtT"], bias=dtb[0:8, g:g + 1])
                ACT(dtT[0:8, 0:Tn], dtT[0:8, 0:Tn], AF.Ln, ["dtT"], ["dtT"], bias=1.0)
                for i in range(2 if full else 1):
                    conv_silu(bcpre[:, i, :], None, (32 if i == 0 else 40) + g, bcs[:, i, :], "bcpre", "bcs")
                CP("dve", convc[:, 32 + g, :], bcpre[:, 0, TP:TP + 3], ["bcpre"], ["convc"])
                if full or kv:
                    CP("dve", convc[:, 40 + g, :], bcpre[:, 1, TP:TP + 3], ["bcpre"], ["convc"])
                if full:
                    CP("act", BCT[:, :, 0:TP], bcs[:, :, 0:TP], ["bcs"], ["BCT"])
                if has_s:
                    CP("pool", B["xpre_s"][:, 32 + g, :], bcpre[:, 0, 3 + TP:3 + Tn], ["bcpre"], ["xpre_s"])
                    CP("pool", B["xpre_s"][:, 40 + g, :], bcpre[:, 1, 3 + TP:3 + Tn], ["bcpre"], ["xpre_s"])
                    CP("pool", B["dtT_s"][0:8, g, :], dtT[0:8, TP:Tn], ["dtT"], ["dtT_s"])
                hTg = hT[:, g, :]
                L, MT, cbT, Ebc, hTb, ytmp, htmp, dAtri = [B[n] for n in ("L", "MT", "cbT", "Ebc", "hTb", "ytmp", "htmp", "dAtri")]
                for c in range(4):
                    cs = slice(c * 128, (c + 1) * 128)
                    pr_ = c % 2
                    dtk, dA, acum, tot, dec, cd, ndA, nacum = [B[n][pr_] for n in ("dtk", "dA", "acum", "tot", "dec", "cd", "ndA", "nacum")]
                    Xpad, Xd, Btok = B["Xpad"][pr_], B["Xd"][pr_], B["Btok"][pr_]
                    pt, kp = PS7()
                    TR(pt[:, 0:8], dtT[0:8, cs], ident[0:8, 0:8], ["dtT", "cst"], [kp])
                    CP("act", dtk, pt[:, 0:8], [kp], ["dtk%d" % pr_])
                    TT("dve", dA, dtk, negA[:, g * 8:(g + 1) * 8], ALU.mult, ["dtk%d" % pr_, "negA"], ["dA%d" % pr_])
                    pa, kpa = PS7()
                    MM(pa[:, 0:8], triU, dA, True, True, ["dA%d" % pr_, "cst"], [kpa])
                    MM(pa[:, 8:16], ones_f, dA, True, True, ["dA%d" % pr_, "cst"], [kpa])
                    CP("act", acum, pa[:, 0:8], [kpa], ["acum%d" % pr_])
                    CP("act", tot, pa[:, 8:16], [kpa], ["tot%d" % pr_])
                    TT("dve", dec, tot, acum, ALU.subtract, ["tot%d" % pr_, "acum%d" % pr_], ["dec%d" % pr_])
                    ACT(dec, dec, AF.Exp, ["dec%d" % pr_], ["dec%d" % pr_])
                    TT("dve", dec, dec, dtk, ALU.mult, ["dec%d" % pr_, "dtk%d" % pr_], ["dec%d" % pr_])
                    ACT(cd, tot, AF.Exp, ["tot%d" % pr_], ["cd%d" % pr_])
                    px, kpx = PS7()
                    for i in range(4):
                        TR(px[:, i * 128:(i + 1) * 128], B["xs"][:, i, cs], ident, ["xs", "cst"], [kpx])
                    px3 = px.rearrange("p (r d) -> p r d", r=8)
                    TT("dve", Xd.rearrange("p (r d) -> p r d", r=8), px3, dec.unsqueeze(2).to_broadcast([128, 8, 64]), ALU.mult, [kpx, "dec%d" % pr_], ["Xd%d" % pr_])
                    if full:
                        for par in range(2):
                            TT("dve", Xpad[:, par::2, par * 64:(par + 1) * 64], px3[:, par::2, :],
                               dtk[:, par::2].unsqueeze(2).to_broadcast([128, 4, 64]), ALU.mult, [kpx, "dtk%d" % pr_], ["Xpad%d" % pr_])
                    pb, kpb = PS7()
                    TR(pb[:, 0:128], bcs[:, 0, cs], ident, ["bcs", "cst"], [kpb])
                    CP("act", Btok, pb[:, 0:128], [kpb], ["Btok%d" % pr_])
                    if full:
                        TT("pool", dAtri, triU.unsqueeze(1).to_broadcast([128, 8, 128]), dA.unsqueeze(2).to_broadcast([128, 8, 128]), ALU.mult,
                           ["dA%d" % pr_, "cst"], ["dAtri"])
                        pA = []
                        for hf in range(2):
                            p_, k_ = PS7()
                            for q4 in range(2):
                                r0 = hf * 4 + q4 * 2
                                MM(p_[:, q4 * 256:(q4 + 1) * 256], ones_f, dAtri[:, r0:r0 + 2, :].rearrange("p a b -> p (a b)"), True, True, ["dAtri", "cst"], [k_])
                            pA.append((p_, k_))
                        TS("dve", nacum, acum, -1.0, None, ALU.mult, None, ["acum%d" % pr_], ["nacum%d" % pr_])
                        pc, kpc = PS7()
                        MM(pc[:, 0:128], BCT[:, 0, cs], BCT[:, 1, cs], True, True, ["BCT"], [kpc])
                        TT("dve", cbT, pc[:, 0:128], ssdmask, ALU.mult, [kpc, "cst"], ["cbT"])
                        for r in range(8):
                            p_, k_ = pA[r // 4]
                            a_ = p_[:, (r % 4) * 128:(r % 4 + 1) * 128]
                            STT("dve", L[:, r, :], a_, nacum[:, r:r + 1], ssdmask, ALU.add, ALU.mult, [k_, "nacum%d" % pr_, "cst"], ["L"])
                        ACT(L, L, AF.Exp, ["L"], ["L"])
                        TT("pool", MT, L, cbT.unsqueeze(1).to_broadcast([128, 8, 128]), ALU.mult, ["L", "cbT"], ["MT"])
                        for jx in range(4):
                            for par in range(2):
                                r = 2 * jx + par
                                p_, k_ = pA[r // 4]
                                CP("dve", Ebc[par * 64:(par + 1) * 64, jx, :], p_[par * 64:(par + 1) * 64, (r % 4) * 128:(r % 4 + 1) * 128], [k_], ["Ebc"])
                        ACT(Ebc, Ebc, AF.Exp, ["Ebc"], ["Ebc"])
                        CP("act", hTb, hTg, ["hT"], ["hTb"])
                        po, kpo = PS7()
                        for jx in range(4):
                            MM(po[:, jx * 128:(jx + 1) * 128], hTb[:, jx * 128:(jx + 1) * 128], BCT[:, 1, cs], True, True, ["hTb", "BCT"], [kpo])
                        TT("dve", ytmp, po.rearrange("p (a b) -> p a b", a=4), Ebc, ALU.mult, [kpo, "Ebc"], ["ytmp"])
                        pd, kpd = PS7()
                        for jx in range(4):
                            for par in range(2):
                                r = 2 * jx + par
                                MM(pd[:, jx * 128:(jx + 1) * 128], Xpad[:, r, :], MT[:, r, :], par == 0, par == 1, ["Xpad%d" % pr_, "MT"], [kpd])
                        TT("dve", ytmp, pd.rearrange("p (a b) -> p a b", a=4), ytmp, ALU.add, [kpd, "ytmp"], ["ytmp"])
                        for jx in range(4):
                            STT("dve", B["yT"][:, jx, cs], B["xs"][:, jx, cs], dsk[:, 4 * g + jx:4 * g + jx + 1], ytmp[:, jx, :], ALU.mult, ALU.add,
                                ["xs", "ytmp", "dsk"], ["yT"])
                    pst, kps = PS7()
                    MM(pst, Btok, Xd, True, True, ["Btok%d" % pr_, "Xd%d" % pr_], [kps])
                    TT("dve", htmp.rearrange("p (r d) -> p r d", r=8), hTg.rearrange("p (r d) -> p r d", r=8),
                       cd.unsqueeze(2).to_broadcast([128, 8, 64]), ALU.mult, ["hT", "cd%d" % pr_], ["htmp"])
                    TT("dve", hTg, htmp, pst, ALU.add, ["htmp", kps], ["hT"])
                if full:
                    yT, zs, gsq, rstd2 = B["yT"], B["zs"], B["gsq"], B["rstd2"]
                    TT("pool", yT[:, :, 0:TP], yT[:, :, 0:TP], zs[:, :, 0:TP], ALU.mult, ["yT", "zs"], ["yT"])
                    ACT(gsq[:, :, 0:TP], yT[:, :, 0:TP], AF.Square, ["yT"], ["gsq"])
                    res = proj(128, lambda kc: ones_b, lambda kc, c0, c1: gsq[:, kc, c0:c1], 4, TP, ["gsq", "cst_b"])
                    for (o, kp, c0, c1) in res:
                        TS("dve", rstd2[:, c0:c1], o, 1.0 / 512, EPS, ALU.mult, ALU.add, [kp], ["rstd2"])
                    ACT(rstd2[:, 0:TP], rstd2[:, 0:TP], AF.Ln, ["rstd2"], ["rstd2"])
                    ACT(rstd2[:, 0:TP], rstd2[:, 0:TP], AF.Exp, ["rstd2"], ["rstd2"], scale=-0.5)
                    for jx in range(4):
                        STT("dve", B["ynT"][:, 4 * g + jx, 0:TP], yT[:, jx, 0:TP], snw[:, 4 * g + jx:4 * g + jx + 1], rstd2[:, 0:TP], ALU.mult, ALU.mult,
                            ["yT", "rstd2", "snw"], ["ynT"])
            task(loads_bcdt, comp_ssd)


        def smp_ssm(slot=None, sk=None):
            barrier()
            A.release(B["m_ssd"])
            xpre_s, zs_s, dtT_s = B["xpre_s"], B["zs_s"], B["dtT_s"]
            ST = A.alloc([48, 48], F32)
            sct = A.alloc([1536], F32)
            sc2 = sconv.rearrange("b k c -> (b k) c")
            for q in range(4):
                P.dma("sp", sct[0:48, :], sc2[:, q * 1536:(q + 1) * 1536], writes=["sct"])
                for j4 in range(3):
                    pt, kp = PS7()
                    for jj in range(4):
                        lc = j4 * 4 + jj
                        TR(pt[:, jj * 48:(jj + 1) * 48], sct[0:48, lc * 128:(lc + 1) * 128], ident[0:48, 0:48], ["sct", "cst"], [kp])
                    CP("act", ST[:, q * 12 + j4 * 4:q * 12 + j4 * 4 + 4, :], pt[:, 0:192].rearrange("p (a b) -> p a b", a=4), [kp], ["ST"])
            STv = ST.rearrange("p c (b k) -> p c b k", k=3)
            acc = A.alloc([48, NS], F32); tmp = A.alloc([48, NS], F32); xc_s = A.alloc([48, NS], F32)
            TT("pool", acc, STv[:, :, :, 0], convw[:, :, 0:1].to_broadcast([128, 48, NS]), ALU.mult, ["ST", "convw"], ["sacc"])
            for k in (1, 2):
                TT("pool", tmp, STv[:, :, :, k], convw[:, :, k:k + 1].to_broadcast([128, 48, NS]), ALU.mult, ["ST", "convw"], ["stmp"])
                TT("pool", acc, acc, tmp, ALU.add, ["sacc", "stmp"], ["sacc"])
            TT("pool", tmp, xpre_s, convw[:, :, 3:4].to_broadcast([128, 48, NS]), ALU.mult, ["xpre_s", "convw"], ["stmp"])
            TT("pool", acc, acc, tmp, ALU.add, ["sacc", "stmp"], ["sacc"])
            TT("pool", acc, acc, convb.unsqueeze(2).to_broadcast([128, 48, NS]), ALU.add, ["sacc", "convb"], ["sacc"])
            ACT(xc_s, acc, AF.Silu, ["sacc"], ["xc_s"])
            if "ssm_stop1" in opts:
                return
            P.dma("sp", cvs_out[:, 0:2, :], sconv[:, 1:3, :], writes=["cvs01"])
            tok = A.alloc([CONVD], F32)
            for (srcT, ksrc, dst, kd) in ((xpre_s, "xpre_s", cvs_out[:, 2, :], "cvs2"), (xc_s, "xc_s", scr_x, "scr_x")):
                for q in range(12):
                    pt, kp = PS7()
                    for jj in range(4):
                        TR(pt[0:NS, jj * 128:(jj + 1) * 128], srcT[:, q * 4 + jj, :], ident, [ksrc, "cst"], [kp])
                    CP("act", tok[0:NS, q * 512:(q + 1) * 512], pt[0:NS, :], [kp], ["stok"])
                P.dma("sp", dst, tok[0:NS, :], reads=["stok"], writes=[kd])
            dtt = A.alloc([64], F32)
            pt, kp = PS7()
            for g in range(NG):
                TR(pt[0:NS, g * 8:(g + 1) * 8], dtT_s[0:8, g, :], ident[0:8, 0:8], ["dtT_s", "cst"], [kp])
            CP("act", dtt[0:NS, :], pt[0:NS, 0:64], [kp], ["dtt"])
            P.dma("sp", scr_dt, dtt[0:NS, :], reads=["dtt"], writes=["scr_dt"])
            if "ssm_stop2" in opts:
                return
            Xg = A.alloc([64], F32); Bg = A.alloc([128], F32); Cg = A.alloc([128], F32)
            dtg = A.alloc([1], F32); ag = A.alloc([1], F32); da = A.alloc([1], F32)
            yg = [A.alloc([64], F32) for _ in range(2)]
            hb = [A.alloc([8, 128], F32) for _ in range(2)]
            tm = A.alloc([8, 128], F32); pr = A.alloc([8, 128], F32)
            for g in range(NG):
                P.dma("pool", Xg, scr_x[:, g * 512:(g + 1) * 512].rearrange("b (r p) -> b r p", r=8), reads=["scr_x"], writes=["Xg"])
                P.dma("pool", Bg, scr_x[:, OB - OX + g * 128:OB - OX + (g + 1) * 128].unsqueeze(1).to_broadcast([NS, 8, 128]), reads=["scr_x"], writes=["Bg"])
                P.dma("pool", Cg, scr_x[:, OC - OX + g * 128:OC - OX + (g + 1) * 128].unsqueeze(1).to_broadcast([NS, 8, 128]), reads=["scr_x"], writes=["Cg"])
                P.dma("pool", dtg, scr_dt[:, g * 8:(g + 1) * 8].unsqueeze(2), reads=["scr_dt"], writes=["dtg"])
                P.dma("pool", ag, alog_in[:, g * 8:(g + 1) * 8].unsqueeze(2).to_broadcast([NS, 8, 1]), writes=["ag"])
                ACT(ag, ag, AF.Exp, ["ag"], ["ag"])
                TT("dve", da, dtg, ag, ALU.mult, ["dtg", "ag"], ["da"])
                ACT(da, da, AF.Exp, ["da"], ["da"], scale=-1.0)
                TS("dve", Xg, Xg, dtg[:, 0:1], None, ALU.mult, None, ["Xg", "dtg"], ["Xg"])
                y_ = yg[g % 2]; ky = "yg%d" % (g % 2)
                for pc in range(8):
                    h = hb[pc % 2]; kh = "hb%d" % (pc % 2)
                    hsrc = sssm[:, g * 8:(g + 1) * 8, pc * 8:(pc + 1) * 8, :].rearrange("b r p n -> b r (p n)")
                    hdst = sss_out[:, g * 8:(g + 1) * 8, pc * 8:(pc + 1) * 8, :].rearrange("b r p n -> b r (p n)")
                    if "ssm_noload" not in opts:
                        P.dma("sp", h.rearrange("q a b -> q (a b)"), hsrc, writes=[kh])
                    if "ssm_nocomp" in opts:
                        if "ssm_nostore" not in opts:
                            P.dma("act", hdst, h.rearrange("q a b -> q (a b)"), reads=[kh], writes=["ssso"])
                        continue
                    TT("pool", tm, Xg[:, pc * 8:(pc + 1) * 8].unsqueeze(2).to_broadcast([128, 8, 128]), Bg.unsqueeze(1).to_broadcast([128, 8, 128]), ALU.mult,
                       ["Xg", "Bg"], ["tm"])
                    STT("dve", h, h, da[:, 0:1], tm, ALU.mult, ALU.add, [kh, "da", "tm"], [kh])
                    if "ssm_nostore" not in opts:
                        P.dma("act", hdst, h.rearrange("q a b -> q (a b)"), reads=[kh], writes=["ssso"])
                    TT("pool" if pc % 2 == 0 else "dve", pr, h, Cg.unsqueeze(1).to_broadcast([128, 8, 128]), ALU.mult, [kh, "Cg"], ["pr"])
                    op("dve", lambda e, y_=y_, pc=pc: e.tensor_reduce(out=y_[:, pc * 8:(pc + 1) * 8], in_=pr, axis=AX.X, op=ALU.add), ["pr"], [ky])
                P.dma("sp", scr_y[:, g * 512:(g + 1) * 512].rearrange("b (r p) -> b r p", r=8), y_, reads=[ky], writes=["scr_y"])
            if "ssm_stop3" in opts:
                return
            ytk = A.alloc([DI], F32)
            P.dma("sp", ytk[0:NS, :], scr_y, reads=["scr_y"], writes=["ytk"])
            yTs = A.alloc([32, NS], F32); gq = A.alloc([32, 32], BF16)[:, :, 0:NS]; rs = A.alloc([8, NS], F32)
            for q in range(8):
                pt, kp = PS7()
                for jj in range(4):
                    TR(pt[:, jj * NS:(jj + 1) * NS], ytk[0:NS, (q * 4 + jj) * 128:(q * 4 + jj + 1) * 128], ident[0:NS, 0:NS], ["ytk", "cst"], [kp])
                CP("act", yTs[:, q * 4:(q + 1) * 4, :], pt[:, 0:4 * NS].rearrange("p (a b) -> p a b", a=4), [kp], ["yTs"])
            if "ssm_stop4" in opts:
                return
            TT("pool", tmp[:, 0:32, :], xc_s[:, 0:32, :], dsk.unsqueeze(2).to_broadcast([128, 32, NS]), ALU.mult, ["xc_s", "dsk"], ["stmp"])
            TT("pool", yTs, yTs, tmp[:, 0:32, :], ALU.add, ["yTs", "stmp"], ["yTs"])
            TT("pool", yTs, yTs, zs_s, ALU.mult, ["yTs", "zs_s"], ["yTs"])
            if "ssm_stop5" in opts:
                return
            ACT(gq, yTs, AF.Square, ["yTs"], ["gq"])
            if "ssm_stop6" in opts:
                return
            for g in range(NG):
                pt_, kp = PS7()
                o = pt_[:, 0:NS]
                for jx in range(4):
                    MM(o, ones_b, gq[:, 4 * g + jx, :], jx == 0, jx == 3, ["gq", "cst_b"], [kp])
                ACT(rs[:, g, :], o, AF.Copy, [kp], ["rs"], scale=1.0 / 512)
            TS("dve", rs, rs, EPS, None, ALU.add, None, ["rs"], ["rs"])
            ACT(rs, rs, AF.Ln, ["rs"], ["rs"])
            ACT(rs, rs, AF.Exp, ["rs"], ["rs"], scale=-0.5)
            if "ssm_stop7" in opts:
                return
            for c in range(32):
                STT("dve", B["ynT"][:, c, TP:Tn], yTs[:, c, :], snw[:, c:c + 1], rs[:, c // 4, :], ALU.mult, ALU.mult, ["yTs", "rs", "snw"], ["ynT"])
        if has_s and "no_smp_ssm" not in opts:
            task(None, smp_ssm)

        CUR[0] = TASKS
        TASKS.extend(ssd_list)
        if full or kv:
            TASKS.extend(att_list)
        if not full:
            def p_end(slot=None, sk=None):
                barrier()
                A.release(B["m_pass"])
            task(None, p_end)
            return

        def s5_alloc(slot=None, sk=None):
            barrier()
            A.release(B["m_att"])
            B["mergedT"] = A.alloc([16, T], BF16)
            B["sg"] = A.alloc([2, T], F32)
            B["mt"] = A.alloc([2, T], F32)
        task(None, s5_alloc)

        for cb in range(4):
            st5 = {}

            def mk_acc(name, nk, cw, sub):
                def comp(slot, sk, cb=cb):
                    wv_ = wview(slot, nk, cw)
                    src = {"ab": B["attnT"], "sb": B["ynT"], "ga": hnT, "gs": hnT}[name[0:2]]
                    ksrc = {"ab": "attnT", "sb": "ynT", "ga": "hnT", "gs": "hnT"}[name[0:2]]
                    return wv_, src, ksrc
                return comp

            def comp_merge_blk(slots, cb=cb):
                pass

        for cb in range(4):
            def comp_ga(slot, sk, cb=cb):
                wv_ = wview(slot, 16, 512)
                for i in range(4):
                    res = proj(128, lambda kc: wv_[:, kc, i * 128:(i + 1) * 128], lambda kc, c0, c1: hnT[:, kc, c0:c1], 16, Tn, ["hnT", sk + ("a" if i < 2 else "b")])
                    for (o, kp, c0, c1) in res:
                        ACT(B["sgA"][:, i, c0:c1], o, AF.Sigmoid, [kp], ["sgA"])
            def comp_gs(slot, sk, cb=cb):
                wv_ = wview(slot, 16, 512)
                for i in range(4):
                    res = proj(128, lambda kc: wv_[:, kc, i * 128:(i + 1) * 128], lambda kc, c0, c1: hnT[:, kc, c0:c1], 16, Tn, ["hnT", sk + ("a" if i < 2 else "b")])
                    for (o, kp, c0, c1) in res:
                        ACT(B["sgS"][:, i, c0:c1], o, AF.Sigmoid, [kp], ["sgS"])
            def comp_ab(slot, sk, cb=cb):
                wv_ = wview(slot, 16, 512)
                for i in range(4):
                    res = proj(128, lambda kc: wv_[:, kc, i * 128:(i + 1) * 128], lambda kc, c0, c1: B["attnT"][:, kc, c0:c1], 16, Tn, ["attnT", sk + ("a" if i < 2 else "b")])
                    for (o, kp, c0, c1) in res:
                        TT("dve", B["mtmp"][:, i, c0:c1], o, B["sgA"][:, i, c0:c1], ALU.mult, [kp, "sgA"], ["mtmp"])
            def comp_sb(slot, sk, cb=cb, half=0):
                pass
            if cb == 0:
                def s5b(slot=None, sk=None):
                    B["sgA"] = A.alloc([4, T], F32); B["sgS"] = A.alloc([4, T], F32); B["mtmp"] = A.alloc([4, T], F32)
                task(None, s5b)
            task(wloads_plain(w_in, OGA + cb * 512, 512, 16), comp_ga)
            task(wloads_plain(w_in, OGS + cb * 512, 512, 16), comp_gs)
            task(wloads_plain(w_ab, cb * 512, 512, 16), comp_ab)
            for hf in range(2):
                def comp_sbh(slot, sk, cb=cb, hf=hf):
                    wv_ = wview(slot, 32, 256)
                    for i2 in range(2):
                        i = hf * 2 + i2
                        res = proj(128, lambda kc: wv_[:, kc, i2 * 128:(i2 + 1) * 128], lambda kc, c0, c1: B["ynT"][:, kc, c0:c1], 32, Tn, ["ynT", sk + ("a" if i2 == 0 else "b")])
                        for (o, kp, c0, c1) in res:
                            TT("dve", B["sgS"][:, i, c0:c1], o, B["sgS"][:, i, c0:c1], ALU.mult, [kp, "sgS"], ["sgS"])
                        TT("pool", B["mergedT"][:, cb * 4 + i, 0:Tn], B["sgS"][:, i, 0:Tn], B["mtmp"][:, i, 0:Tn], ALU.add, ["sgS", "mtmp"], ["mergedT"])
                task(wloads_plain(w_sb, cb * 512 + hf * 256, 256, 32), comp_sbh)

        if has_s and "smp" in dbg_out:
            def dbg_smp(slot=None, sk=None):
                d1 = A.alloc([16, NS], F32); d2 = A.alloc([32, NS], F32)
                CP("dve", d1, B["attnT"][:, :, TP:Tn], ["attnT"], ["d1"])
                CP("dve", d2, B["ynT"][:, :, TP:Tn], ["ynT"], ["d2"])
                P.dma("sp", dbg_out["smp"][:, 0:16, :], d1, reads=["d1"], writes=["dbgo1"])
                P.dma("sp", dbg_out["smp"][:, 16:48, :], d2, reads=["d2"], writes=["dbgo2"])
            task(None, dbg_smp)

        def s6_alloc(slot=None, sk=None):
            barrier()
            A.release(B["m_pass"])
            B["mixT"] = A.alloc([16, T], F32)
            B["xT"] = A.alloc([16, T], F32)
            B["rstd"] = A.alloc([T], F32)
            B["m_ffn"] = A.mark()
            CP("dve", hnT[:, :, 0:Tn], B["mergedT"][:, :, 0:Tn], ["mergedT"], ["hnT"])
            barrier()
        task(None, s6_alloc)

        def reload_x(slot=None, sk=None):
            xtok = B["xtok2"] = [A.alloc([D], F32) for _ in range(2)]
            tiles = [(x_all[x_row0 + t * 128:x_row0 + (t + 1) * 128, :], 128, t * 128) for t in range(4)]
            if has_s:
                tiles.append((x_smp, NS, TP))
            for ti, (src, n, c0) in enumerate(tiles):
                xt = xtok[ti % 2]; kx = "xtokb%d" % (ti % 2)
                P.dma("sp", xt[0:n, :], src, writes=[kx])
                for q in range(4):
                    pt, kp = PS7()
                    for jj in range(4):
                        c = q * 4 + jj
                        TR(pt[:, jj * 128:jj * 128 + n], xt[0:n, c * 128:(c + 1) * 128], ident[0:n, 0:n], [kx, "cst"], [kp])
                    CP("act", B["xT"][:, q * 4:(q + 1) * 4, c0:c0 + n], pt.rearrange("p (a b) -> p a b", a=4)[:, :, 0:n], [kp], ["xT2"])
        task(None, reload_x)

        for cb in range(4):
            def comp_o(slot, sk, cb=cb):
                wv_ = wview(slot, 16, 512)
                for i in range(4):
                    res = proj(128, lambda kc: wv_[:, kc, i * 128:(i + 1) * 128], lambda kc, c0, c1: hnT[:, kc, c0:c1], 16, Tn, ["hnT", sk + ("a" if i < 2 else "b")])
                    for (o, kp, c0, c1) in res:
                        CP("act", B["mixT"][:, cb * 4 + i, c0:c1], o, [kp], ["mixT"])
            task(wloads_plain(w_o, cb * 512, 512, 16), comp_o)

        def add_norm(widx, srcname, sqbuf):
            fm_rstd(B[srcname], srcname, Tn, sqbuf, B["rstd"], D)
            for kc in range(16):
                STT("dve", B[srcname][:, kc, 0:Tn], B[srcname][:, kc, 0:Tn], nw[:, widx, kc:kc + 1], B["rstd"][:, 0:Tn], ALU.mult, ALU.mult,
                    [srcname, "rstd", "nw"], [srcname])
            TT("pool", B["xT"][:, :, 0:Tn], B["xT"][:, :, 0:Tn], B[srcname][:, :, 0:Tn], ALU.add, ["xT2", srcname], ["xT2"])

        def s6b(slot=None, sk=None):
            barrier()
            A.release(B["m_ffn"])
            B["actT"] = A.alloc([44, T], BF16)
            sq = B["actT"][:, 0:16, :]
            add_norm(1, "mixT", sq)
            fm_rstd(B["xT"], "xT2", Tn, sq, B["rstd"], D)
            for kc in range(16):
                STT("dve", hnT[:, kc, 0:Tn], B["xT"][:, kc, 0:Tn], nw[:, 2, kc:kc + 1], B["rstd"][:, 0:Tn], ALU.mult, ALU.mult,
                    ["xT2", "rstd", "nw"], ["hnT"])
            barrier()
            B["sgu"] = A.alloc([4, T], F32)
        task(None, s6b)

        for fb in range(11):
            def comp_g(slot, sk, fb=fb):
                wv_ = wview(slot, 16, 512)
                B["pend"] = []
                for i in range(4):
                    res = proj(128, lambda kc: wv_[:, kc, i * 128:(i + 1) * 128], lambda kc, c0, c1: hnT[:, kc, c0:c1], 16, Tn, ["hnT", sk + ("a" if i < 2 else "b")])
                    for (o, kp, c0, c1) in res:
                        ACT(B["sgu"][:, i, c0:c1], o, AF.Silu, [kp], ["sgu"])
            def comp_u(slot, sk, fb=fb):
                wv_ = wview(slot, 16, 512)
                for i in range(4):
                    res = proj(128, lambda kc: wv_[:, kc, i * 128:(i + 1) * 128], lambda kc, c0, c1: hnT[:, kc, c0:c1], 16, Tn, ["hnT", sk + ("a" if i < 2 else "b")])
                    for (o, kp, c0, c1) in res:
                        TT("dve", B["actT"][:, fb * 4 + i, c0:c1], o, B["sgu"][:, i, c0:c1], ALU.mult, [kp, "sgu"], ["actT"])
            task(wloads_plain(w_gu, fb * 512, 512, 16), comp_g)
            task(wloads_plain(w_gu, DFF + fb * 512, 512, 16), comp_u)
        for cbk in range(16):
            def comp_d(slot, sk, cbk=cbk):
                wv_ = wview(slot, 44, 128)
                res = proj(128, lambda kc: wv_[:, kc, :], lambda kc, c0, c1: B["actT"][:, kc, c0:c1], 44, Tn, ["actT", sk + "a"])
                for (o, kp, c0, c1) in res:
                    CP("act", B["mixT"][:, cbk, c0:c1], o, [kp], ["mixT"])
            task(wloads_plain(w_dn, cbk * 128, 128, 44), comp_d)

        def s8(slot=None, sk=None):
            barrier()
            A.release(B["m_ffn"])
            sq = A.alloc([16, T], BF16)
            add_norm(3, "mixT", sq)
            ytok = [A.alloc([D], F32) for _ in range(2)]
            tiles = [(y_out[y_row0 + t * 128:y_row0 + (t + 1) * 128, :], 128, t * 128) for t in range(4)]
            if has_s:
                tiles.append((ys_out, NS, TP))
            for ti, (dst, n, c0) in enumerate(tiles):
                yt = ytok[ti % 2]; ky = "ytok%d" % (ti % 2)
                for q in range(4):
                    pt, kp = PS7()
                    for jj in range(4):
                        c = q * 4 + jj
                        TR(pt[0:n, jj * 128:(jj + 1) * 128], B["xT"][:, c, c0:c0 + n], ident, ["xT2", "cst"], [kp])
                    CP("act", yt[0:n, q * 512:(q + 1) * 512], pt[0:n, :], [kp], [ky])
                P.dma("sp", dst, yt[0:n, :], reads=[ky], writes=["yout"])
            barrier()
            A.release(pm)
        task(None, s8)

    if "only_p3" not in opts:
        emit_pass(0, False, 0, False, 0, False, kv=False)
        emit_pass(1, False, 512, False, 0, False)

    def apply_flag(slot=None, sk=None):
        TS("dve", hT, hT, flag[:, 0:1], None, ALU.mult, None, ["hT", "flag"], ["hT"])
    task(None, apply_flag)
    if "only_p3" not in opts:
        emit_pass(2, True, 1024, False, 0, True)
    emit_pass(3, True, 1536, "no_s" not in opts, 512, False)

    def final_out(slot=None, sk=None):
        m = A.mark()
        ctok = A.alloc([CONVD], F32)
        for q in range(12):
            pt, kp = PS7()
            for jj in range(4):
                c = q * 4 + jj
                TR(pt[0:3, jj * 128:(jj + 1) * 128], convc[:, c, :], ident, ["convc", "cst"], [kp])
            CP("act", ctok[0:3, q * 512:(q + 1) * 512], pt[0:3, :], [kp], ["ctok"])
        P.dma("sp", cv_out, ctok[0:3, :], reads=["ctok"], writes=["cvo"])
        hto = [A.alloc([512], F32) for _ in range(2)]
        for g in range(NG):
            pt, kp = PS7()
            for jx in range(4):
                TR(pt[:, jx * 128:(jx + 1) * 128], hT[:, g, jx * 128:(jx + 1) * 128], ident, ["hT", "cst"], [kp])
            CP("act", hto[g % 2], pt, [kp], ["hto%d" % (g % 2)])
            P.dma("sp", ss_out[g * 512:(g + 1) * 512, :].rearrange("(j p) n -> p j n", p=128),
                  hto[g % 2].rearrange("p (j n) -> p j n", j=4), reads=["hto%d" % (g % 2)], writes=["sso"])
        A.release(m)
    task(None, final_out)

    wt = [i for i, (l, c) in enumerate(TASKS) if l is not None]
    slot_of = {ti: (n % NSLOT) for n, ti in enumerate(wt)}

    def do_load(ti):
        sl = slot_of[ti]
        for (o, i_, sfx) in TASKS[ti][0](wslots[sl]):
            P.dma("pool", o, i_, writes=["ws%d%s" % (sl, sfx)])

    nxt = 0
    if wt:
        do_load(wt[0]); nxt = 1
    for ti, (l, c) in enumerate(TASKS):
        if l is not None:
            if nxt < len(wt):
                do_load(wt[nxt]); nxt += 1
            sl = slot_of[ti]
            c(wslots[sl], "ws%d" % sl)
        else:
            c()
    P.finalize(st)
    return nc, P


_CACHE = {}


def _host_consts():
    ident = np.eye(128, dtype=np.float32)
    tri = np.triu(np.ones((128, 128), np.float32))
    R = np.zeros((128, 128), np.float32)
    for m in range(128):
        if m % 64 < 32:
            R[m + 32, m] = -1.0
        else:
            R[m - 32, m] = 1.0
    return np.ascontiguousarray(np.stack([ident, tri, np.ones((128, 128), np.float32), R, tri, ident], 1))


def _rope_tables(s0):
    inv = (10000.0 ** (-np.arange(32, dtype=np.float32) / 32)).astype(np.float32)
    cosT = np.zeros((128, 4, TP + NS), np.float32)
    sinT = np.zeros((128, 4, TP + NS), np.float32)
    for pi in range(4):
        pos = (s0 - 1024 + pi * 512 + np.arange(512)).astype(np.float32)
        pos = np.concatenate([pos, np.full((NS,), 16384.0, np.float32)])
        ang = pos[None, :] * inv[:, None]
        c = np.cos(ang).astype(np.float32); sn = np.sin(ang).astype(np.float32)
        idx = np.arange(128) % 32
        cosT[:, pi, :] = c[idx]; sinT[:, pi, :] = sn[idx]
    return cosT, sinT


def kernel(x_prompt, x_sample, cache_win_k, cache_win_v, state_conv, state_ssm,
           norm_mix_pre, norm_mix_post, w_in, attn_sinks, w_attn_branch, conv_w, conv_b,
           dt_bias, a_log, d_skip, ssm_norm, w_ssm_branch, w_out,
           norm_ffn_pre, norm_ffn_post, w_gate_up, w_down):
    f = lambda a: np.ascontiguousarray(np.asarray(a, dtype=np.float32))
    if "nc" not in _CACHE:
        _CACHE["nc"] = build_program()[0]
    nc = _CACHE["nc"]
    xp = f(x_prompt); xs = f(x_sample)
    nws = np.stack([f(norm_mix_pre)[0], f(norm_mix_post)[0], f(norm_ffn_pre)[0], f(norm_ffn_post)[0]], 0)
    nw_l = np.ascontiguousarray(nws.reshape(4, 16, 128).transpose(2, 0, 1))
    cw = f(conv_w)[0]
    convw_l = np.ascontiguousarray(cw.reshape(4, 48, 128).transpose(2, 1, 0))
    convb_l = np.ascontiguousarray(f(conv_b)[0].reshape(48, 128).T)
    dtb_l = np.ascontiguousarray(f(dt_bias)[0].reshape(8, 8).T)
    dsk_l = np.ascontiguousarray(np.repeat(f(d_skip)[0], 64).reshape(32, 128).T)
    snw_l = np.ascontiguousarray(f(ssm_norm)[0].reshape(32, 128).T)
    consts = _host_consts()
    ii = np.arange(128)[:, None]; jj = np.arange(128)[None, :]
    mprev = np.where(jj > ii, 0.0, NEG).astype(np.float32); mcur = np.where(jj <= ii, 0.0, NEG).astype(np.float32)
    m_std = np.concatenate([mprev, mcur], 1)
    m_none = np.concatenate([np.full((128, 128), NEG, np.float32), mcur], 1)
    shared = {"w_in": f(w_in)[0], "w_ab": f(w_attn_branch)[0], "w_sb": f(w_ssm_branch)[0], "w_o": f(w_out)[0],
              "w_gu": f(w_gate_up)[0], "w_dn": f(w_down)[0], "nw": nw_l, "convw": convw_l, "convb": convb_l,
              "dtb": dtb_l, "alog": f(a_log), "dsk": dsk_l, "snw": snw_l, "sinks": f(attn_sinks), "consts": consts}
    in_maps = []
    for c in range(8):
        b = c // 2; hf = c % 2
        xa = np.zeros((2048, D), np.float32)
        if hf == 1:
            xa[0:1024] = xp[b, 0:1024]
        xa[1024:2048] = xp[b, hf * 1024:(hf + 1) * 1024]
        cosT, sinT = _rope_tables(hf * 1024)
        m = dict(shared)
        m.update({"x_all": xa, "x_smp": np.ascontiguousarray(xs[c * NS:(c + 1) * NS, 0, :]),
                  "ck": np.ascontiguousarray(f(cache_win_k)[0, c * NS:(c + 1) * NS]), "cv": np.ascontiguousarray(f(cache_win_v)[0, c * NS:(c + 1) * NS]),
                  "sconv": np.ascontiguousarray(f(state_conv)[0, c * NS:(c + 1) * NS]), "sssm": np.ascontiguousarray(f(state_ssm)[0, c * NS:(c + 1) * NS]),
                  "cosT": cosT, "sinT": sinT,
                  "masks": np.ascontiguousarray(np.stack([m_std, m_std if hf == 1 else m_none], 1)),
                  "flag": np.full((128, 1), float(hf), np.float32)})
        in_maps.append(m)
    res = run_bass_kernel_spmd(nc, in_maps, core_ids=list(range(8))).results
    y_p = np.zeros((4, 2048, D), np.float32)
    for c in range(8):
        y_p[c // 2, (c % 2) * 1024:(c % 2 + 1) * 1024] = res[c]["y_own"]
    y_s = np.concatenate([res[c]["y_smp"] for c in range(8)], 0).reshape(128, 1, D)
    odd = [1, 3, 5, 7]
    wk = np.stack([res[c]["wk"].reshape(128, 8, 64) for c in odd], 0)[None]
    wv = np.stack([res[c]["wv"].reshape(128, 8, 64) for c in odd], 0)[None]
    cvp = np.stack([res[c]["convo"] for c in odd], 0)[None]
    ssp = np.stack([res[c]["ssmo"].reshape(64, 64, 128) for c in odd], 0)[None]
    wks = np.concatenate([res[c]["wks"].reshape(NS, 128, 8, 64) for c in range(8)], 0)[None]
    wvs = np.concatenate([res[c]["wvs"].reshape(NS, 128, 8, 64) for c in range(8)], 0)[None]
    cvs = np.concatenate([res[c]["convs"] for c in range(8)], 0)[None]
    sss = np.concatenate([res[c]["ssms"] for c in range(8)], 0)[None]
    return (y_p, y_s, wk, wv, cvp, ssp, wks, wvs, cvs, sss)
```

```python
import numpy as np
import concourse.bass as bass
import concourse.mybir as mybir

F32 = mybir.dt.float32
BF16 = mybir.dt.bfloat16
AF = mybir.ActivationFunctionType
ALU = mybir.AluOpType
AX = mybir.AxisListType

KQ = 12
STRICT_SAME_ENGINE = False


class _Op(object):
    __slots__ = ("eng", "fn", "reads", "writes", "dma", "deps", "sig", "ev", "qidx", "waits")


class Prog(object):
    ENGS = ("pe", "act", "dve", "pool", "sp")

    def __init__(self, nc):
        self.nc = nc
        self.ops = []
        self.nbar = 0

    def op(self, eng, fn, reads=(), writes=()):
        o = _Op()
        o.eng = eng; o.fn = fn; o.reads = tuple(reads); o.writes = tuple(writes)
        o.dma = False; o.sig = False; o.ev = None; o.qidx = -1
        self.ops.append(o)
        return o

    def dma(self, q, out, in_, reads=(), writes=()):
        o = _Op()
        o.eng = q
        o.fn = (lambda e, out=out, in_=in_: e.dma_start(out=out, in_=in_))
        o.reads = tuple(reads); o.writes = tuple(writes)
        o.dma = True; o.sig = True; o.ev = None; o.qidx = -1
        self.ops.append(o)
        return o

    def barrier(self, tiny):
        n = self.nbar
        self.nbar += 1
        dq = [("dq", q, i) for q in ("sp", "act", "pool") for i in range(KQ)]
        for e in self.ENGS:
            o = self.op(e, tiny[e], reads=(dq + ["scr"] if e == "sp" else ["scr"]), writes=[("bar1", n, e)])
            if e == "sp":
                o.dma = True; o.sig = True
        for e in self.ENGS:
            o = self.op(e, tiny[e], reads=[("bar1", n, e2) for e2 in self.ENGS], writes=[("bar2", n, e)])
            if e == "sp":
                o.dma = True; o.sig = True

    def finalize(self, stack):
        nc = self.nc
        ops = self.ops
        esem = {e: stack.enter_context(nc.semaphore("se_" + e)) for e in self.ENGS}
        dsem = {q: [stack.enter_context(nc.semaphore("sd_%s%d" % (q, i))) for i in range(KQ)]
                for q in ("sp", "act", "pool")}
        semobj = {}
        for e in self.ENGS:
            semobj[("e", e)] = esem[e]
        for q in dsem:
            for i in range(KQ):
                semobj[("d", q, i)] = dsem[q][i]

        last_w = {}
        readers = {}
        dq_hist = {"sp": [], "act": [], "pool": []}
        for j, op in enumerate(ops):
            deps = {}
            for k in op.reads:
                i = last_w.get(k)
                if i is not None:
                    deps[i] = True
            for k in op.writes:
                i = last_w.get(k)
                if i is not None:
                    o = ops[i]
                    if o.dma or op.dma or o.eng != op.eng or (STRICT_SAME_ENGINE and op.eng != "pe"):
                        deps[i] = True
                for i in readers.get(k, {}).values():
                    o = ops[i]
                    if o.dma or op.dma or o.eng != op.eng or (STRICT_SAME_ENGINE and op.eng != "pe"):
                        deps[i] = True
            if op.dma:
                h = dq_hist[op.eng]
                op.qidx = len(h)
                op.writes = op.writes + (("dq", op.eng, op.qidx % KQ),)
                if op.qidx >= KQ:
                    deps[h[op.qidx - KQ]] = True
                h.append(j)
            deps.pop(j, None)
            op.deps = sorted(deps, reverse=True)
            rid = (op.eng, op.qidx % KQ) if op.dma else op.eng
            for k in op.reads:
                readers.setdefault(k, {})[rid] = j
            for k in op.writes:
                last_w[k] = j
                readers[k] = {}
        for op in ops:
            for i in op.deps:
                ops[i].sig = True
        cnt = {e: 0 for e in self.ENGS}
        for op in ops:
            if op.dma:
                op.ev = (("d", op.eng, op.qidx % KQ), 16 * (op.qidx // KQ + 1))
            elif op.sig:
                cnt[op.eng] += 1
                op.ev = (("e", op.eng), cnt[op.eng])
        know = {e: {} for e in self.ENGS}
        snap = {}
        nw = 0
        for op in ops:
            kn = know[op.eng]
            waits = []
            for i in op.deps:
                sid, val = ops[i].ev
                if kn.get(sid, 0) >= val:
                    continue
                waits.append((sid, val))
                for s, v in snap[(sid, val)].items():
                    if kn.get(s, 0) < v:
                        kn[s] = v
            op.waits = waits
            nw += len(waits)
            if op.sig:
                s = dict(kn)
                s[op.ev[0]] = op.ev[1]
                snap[op.ev] = s
        self.stats = dict(n_ops=len(ops), n_waits=nw, cnt=dict(cnt),
                          ndma={q: len(h) for q, h in dq_hist.items()})
        final_waits = []
        for q, h in dq_hist.items():
            for j in h[-KQ:]:
                final_waits.append(ops[j].ev)

        block = stack.enter_context(nc.Block())

        def emit(e, eng, extra=None):
            for op in ops:
                if op.eng != eng:
                    continue
                for sid, val in op.waits:
                    e.wait_ge(semobj[sid], val)
                ins = op.fn(e)
                if op.sig:
                    if op.dma:
                        ins.then_inc(semobj[op.ev[0]], 16)
                    else:
                        ins.then_inc(semobj[op.ev[0]], 1)
            if extra:
                for sid, val in extra:
                    e.wait_ge(semobj[sid], val)

        @block.tensor
        def _(e):
            emit(e, "pe")

        @block.scalar
        def _(e):
            emit(e, "act")

        @block.vector
        def _(e):
            emit(e, "dve")

        @block.gpsimd
        def _(e):
            emit(e, "pool")

        @block.sync
        def _(e):
            emit(e, "sp", final_waits)


from contextlib import ExitStack
from concourse.bass_utils import run_bass_kernel_spmd

D = 2048; NH = 32; NKV = 8; HD = 64; DI = 4096; NSH = 64; NST = 128; NG = 8
CONVD = 6144; DFF = 5632; EPS = 1e-6
OQ = 0; OK_ = 2048; OV = 2560; OZ = 3072; OX = 7168; OB = 11264; OC = 12288; ODT = 13312; OGA = 13376; OGS = 15424
INP = 17472
TP = 512
NS = 16
NEG = -30000.0
SB_BYTES = 212480


def _prod(s):
    r = 1
    for v in s:
        r *= v
    return r


class Arena(object):
    def __init__(self, nc, stack, name, nbytes):
        self.t = stack.enter_context(nc.sbuf_tensor(name, [128, nbytes // 4], F32))
        self.ap = self.t[:]
        self.cap = nbytes
        self.top = 0

    def alloc(self, free_shape, dtype, parts=128):
        n = _prod(free_shape)
        nb = n * (4 if dtype == F32 else 2)
        nb = (nb + 63) // 64 * 64
        off = self.top
        self.top += nb
        assert self.top <= self.cap, ("SBUF arena overflow", self.top, self.cap)
        v = self.ap[:, off // 4:(off + nb) // 4]
        if dtype != F32:
            v = v.bitcast(dtype)
        v = v[:, 0:n]
        if len(free_shape) == 2:
            v = v.rearrange("p (a b) -> p a b", a=free_shape[0])
        elif len(free_shape) == 3:
            v = v.rearrange("p (a b c) -> p a b c", a=free_shape[0], b=free_shape[1])
        return v

    def mark(self):
        return self.top

    def release(self, m):
        self.top = m


def build_program(dbg=None, opts=()):
    nc = bass.Bass("TRN2", target_bir_lowering=False)
    st = ExitStack()
    P = Prog(nc)

    def din(name, shape):
        return nc.dram_tensor(name, list(shape), F32, kind="ExternalInput").ap()

    def dout(name, shape):
        return nc.dram_tensor(name, list(shape), F32, kind="ExternalOutput").ap()

    x_all = din("x_all", [2048, D])
    x_smp = din("x_smp", [NS, D])
    ck = din("ck", [NS, 128, NKV, HD]); cv = din("cv", [NS, 128, NKV, HD])
    sconv = din("sconv", [NS, 3, CONVD]); sssm = din("sssm", [NS, NSH, HD, NST])
    w_in = din("w_in", [D, INP]); w_ab = din("w_ab", [D, D]); w_sb = din("w_sb", [DI, D])
    w_o = din("w_o", [D, D]); w_gu = din("w_gu", [D, 2 * DFF]); w_dn = din("w_dn", [DFF, D])
    nw_in = din("nw", [128, 4, 16])
    convw_in = din("convw", [128, 48, 4]); convb_in = din("convb", [128, 48])
    dtb_in = din("dtb", [8, 8]); alog_in = din("alog", [1, 64]); dsk_in = din("dsk", [128, 32])
    snw_in = din("snw", [128, 32]); sink_in = din("sinks", [1, 32])
    cos_in = din("cosT", [128, 4, TP + NS]); sin_in = din("sinT", [128, 4, TP + NS])
    cst_in = din("consts", [128, 6, 128])
    msk_in = din("masks", [128, 2, 256])
    flag_in = din("flag", [128, 1])

    y_out = dout("y_own", [1024, D]); ys_out = dout("y_smp", [NS, D])
    wk_out = dout("wk", [128, 512]); wv_out = dout("wv", [128, 512])
    cv_out = dout("convo", [3, CONVD]); ss_out = dout("ssmo", [DI, NST])
    wks_out = dout("wks", [NS, 128, 512]); wvs_out = dout("wvs", [NS, 128, 512])
    cvs_out = dout("convs", [NS, 3, CONVD]); sss_out = dout("ssms", [NS, NSH, HD, NST])
    scr_x = nc.dram_tensor("scr_x", [NS, CONVD], F32).ap()
    scr_dt = nc.dram_tensor("scr_dt", [NS, 64], F32).ap()
    scr_y = nc.dram_tensor("scr_y", [NS, DI], F32).ap()
    scr_q = nc.dram_tensor("scr_q", [NS, D], F32).ap()
    scr_k = nc.dram_tensor("scr_k", [NS, 512], F32).ap()
    scr_v = nc.dram_tensor("scr_v", [NS, 512], F32).ap()
    scr_o = nc.dram_tensor("scr_o", [NS, D], F32).ap()
    dbg_out = {}
    if dbg:
        for k, shp in dbg.items():
            dbg_out[k] = dout("dbg_" + k, shp)

    A = Arena(nc, st, "arena", SB_BYTES)
    cst = A.alloc([6, 128], F32)
    ident = cst[:, 0, :]; triU = cst[:, 1, :]; ones_f = cst[:, 2, :]; rotR = cst[:, 3, :]; ssdmask = cst[:, 4, :]
    cst_b = A.alloc([6, 128], BF16)
    ident_b = cst_b[:, 0, :]; ones_b = cst_b[:, 2, :]; ssdmask_b = cst_b[:, 4, :]
    masks = A.alloc([2, 256], F32)
    nw = A.alloc([4, 16], F32)
    convw = A.alloc([48, 4], F32); convb = A.alloc([48], F32)
    dtb = A.alloc([8], F32); dsk = A.alloc([32], F32); snw = A.alloc([32], F32)
    negA = A.alloc([64], F32)
    sink8 = A.alloc([32], F32)
    flag = A.alloc([1], F32)
    cosT = A.alloc([544], F32); sinT = A.alloc([544], F32)
    scr = A.alloc([128], F32)
    kTc = A.alloc([4, 128], BF16); Vc = A.alloc([512], BF16)
    convc = A.alloc([48, 3], F32)
    hT = A.alloc([NG, 512], F32)
    NSLOT = 2
    wslots = [A.alloc([8192], BF16) for _ in range(NSLOT)]
    hnT = A.alloc([16, 544], BF16)
    base_mark = A.mark()

    ps = [st.enter_context(nc.psum_tensor("ps%d" % i, [128, 512], F32))[:] for i in range(8)]
    psn = [0]

    def PS():
        i = psn[0] % 8
        psn[0] += 1
        return ps[i], "ps%d" % i

    tiny = {
        "pe": lambda e: e.matmul(ps[7][0:1, 0:1], lhsT=cst_b[0:1, 0, 0:1], rhs=cst_b[0:1, 0, 0:1], start=True, stop=True),
        "act": lambda e: e.activation(out=scr[0:1, 0:1], in_=scr[0:1, 16:17], func=AF.Copy),
        "dve": lambda e: e.memset(scr[0:1, 32:33], 0.0),
        "pool": lambda e: e.memset(scr[0:1, 48:49], 0.0),
        "sp": lambda e: e.dma_start(out=scr[0:1, 64:72], in_=scr[0:1, 96:104]),
    }

    def barrier():
        P.barrier(tiny)

    P.dma("sp", cst, cst_in, writes=["cst"])
    P.dma("pool", cst_b, cst_in, writes=["cst_b"])
    P.dma("sp", masks, msk_in, writes=["masks"])
    P.dma("sp", nw, nw_in, writes=["nw"])
    P.dma("sp", convw, convw_in, writes=["convw"]); P.dma("sp", convb, convb_in, writes=["convb"])
    P.dma("sp", dtb[0:8, :], dtb_in, writes=["dtb"]); P.dma("sp", dsk, dsk_in, writes=["dsk"])
    P.dma("sp", snw, snw_in, writes=["snw"]); P.dma("sp", flag, flag_in, writes=["flag"])
    P.dma("sp", negA, alog_in.partition_broadcast(128), writes=["negA"])
    P.dma("sp", sink8, sink_in.partition_broadcast(128), writes=["sink8"])
    P.op("act", lambda e: e.activation(out=negA, in_=negA, func=AF.Exp), reads=["negA"], writes=["negA"])
    P.op("dve", lambda e: e.tensor_scalar(out=negA, in0=negA, scalar1=-1.0, scalar2=None, op0=ALU.mult), reads=["negA"], writes=["negA"])
    P.op("dve", lambda e: e.tensor_scalar(out=sink8, in0=sink8, scalar1=8.0, scalar2=None, op0=ALU.mult), reads=["sink8"], writes=["sink8"])
    P.op("dve", lambda e: e.memset(hT, 0.0), writes=["hT"])
    P.op("dve", lambda e: e.memset(scr, 0.0), writes=["scr"])
    P.op("dve", lambda e: e.memset(kTc, 0.0), writes=["kTc"])
    P.op("dve", lambda e: e.memset(Vc, 0.0), writes=["Vc"])
    P.op("dve", lambda e: e.memset(convc, 0.0), writes=["convc"])


    T = 544
    TASKS = []

    CUR = [TASKS]

    def task(loads, compute):
        CUR[0].append((loads, compute))

    pss_n = [0]

    def PSS():
        i = psn[0] % 7
        psn[0] += 1
        return ps[i][:, 0:16], "ps%d" % i

    def PS7():
        i = psn[0] % 7
        psn[0] += 1
        return ps[i], "ps%d" % i

    def ranges(Tn):
        return [(0, TP)] if Tn == TP else [(0, Tn // 2), (Tn // 2, Tn)]

    def op(eng, fn, reads, writes):
        return P.op(eng, fn, reads=reads, writes=writes)

    def ACT(out, in_, func, reads, writes, **kw):
        op("act", lambda e: e.activation(out=out, in_=in_, func=func, **kw), reads, writes)

    def TT(eng, out, in0, in1, o, reads, writes):
        op(eng, lambda e: e.tensor_tensor(out=out, in0=in0, in1=in1, op=o), reads, writes)

    def TS(eng, out, in0, s1, s2, o0, o1, reads, writes):
        if o1 is None:
            op(eng, lambda e: e.tensor_scalar(out=out, in0=in0, scalar1=s1, scalar2=None, op0=o0), reads, writes)
        else:
            op(eng, lambda e: e.tensor_scalar(out=out, in0=in0, scalar1=s1, scalar2=s2, op0=o0, op1=o1), reads, writes)

    def STT(eng, out, in0, sc, in1, o0, o1, reads, writes):
        op(eng, lambda e: e.scalar_tensor_tensor(out=out, in0=in0, scalar=sc, in1=in1, op0=o0, op1=o1), reads, writes)

    def MM(out, lhsT, rhs, start, stop, reads, writes):
        op("pe", lambda e: e.matmul(out, lhsT=lhsT, rhs=rhs, start=start, stop=stop), reads, writes)

    def TR(out, in_, idn, reads, writes):
        op("pe", lambda e: e.transpose(out, in_, idn), reads, writes)

    def CP(eng, out, in_, reads, writes):
        if eng == "act":
            ACT(out, in_, AF.Copy, reads, writes)
        else:
            op(eng, lambda e: e.tensor_copy(out=out, in_=in_), reads, writes)

    stg = A.alloc([8, 16], F32)
    stg_n = [0]

    def proj(M, lhsT_of, rhs_of, nk, Tn, rd):
        res = []
        for (c0, c1) in ranges(Tn):
            if c1 - c0 > 16:
                pt, kp = PS7()
            else:
                pt, kp = PSS()
            o = pt[0:M, 0:c1 - c0]
            for kc in range(nk):
                MM(o, lhsT_of(kc), rhs_of(kc, c0, c1), kc == 0, kc == nk - 1, rd, [kp])
            if c1 - c0 <= 16:
                i = stg_n[0] % 8
                stg_n[0] += 1
                so = stg[0:M, i, 0:c1 - c0]
                CP("act", so, o, [kp], ["stg%d" % i])
                res.append((so, "stg%d" % i, c0, c1))
            else:
                res.append((o, kp, c0, c1))
        return res

    def wview(slot, nk, cw):
        return slot[:, 0:nk * cw].rearrange("p (k c) -> p k c", k=nk)

    def wloads_plain(W, c0, cw, nk, col_off=0, tot=None, sfx_default="a"):
        tot = tot or cw
        def f(slot):
            v = wview(slot, nk, tot)
            src = W[:, c0:c0 + cw].rearrange("(k p) c -> p k c", p=128)
            out = []
            step = max(1, 4096 // (cw * 4) * 4) if cw * 4 < 1024 else 4
            step = min(nk, max(4, step))
            if cw >= 256 and col_off == 0 and tot == cw:
                hw = cw // 2
                for (ca, sfx) in ((0, "a"), (hw, "b")):
                    for k0 in range(0, nk, nk // 2):
                        out.append((v[:, k0:k0 + nk // 2, ca:ca + hw], src[:, k0:k0 + nk // 2, ca:ca + hw], sfx))
                return out
            for k0 in range(0, nk, step):
                k1 = min(nk, k0 + step)
                out.append((v[:, k0:k1, col_off:col_off + cw], src[:, k0:k1, :], sfx_default))
            return out
        return f

    def fm_rstd(src, ksrc, Tn, sq, rstd, nfeat):
        ACT(sq[:, :, 0:Tn], src[:, :, 0:Tn], AF.Square, [ksrc], ["sq"])
        res = proj(128, lambda kc: ones_b, lambda kc, c0, c1: sq[:, kc, c0:c1], 16, Tn, ["sq", "cst_b"])
        for (o, kp, c0, c1) in res:
            TS("dve", rstd[:, c0:c1], o, 1.0 / nfeat, EPS, ALU.mult, ALU.add, [kp], ["rstd"])
        ACT(rstd[:, 0:Tn], rstd[:, 0:Tn], AF.Ln, ["rstd"], ["rstd"])
        ACT(rstd[:, 0:Tn], rstd[:, 0:Tn], AF.Exp, ["rstd"], ["rstd"], scale=-0.5)

    def emit_pass(pi, full, x_row0, has_s, y_row0, first_block_mask, dbgtag=None, kv=True):
        Tn = TP + (NS if has_s else 0)
        last = (pi == 3)
        pm = A.mark()

        def s01(slot=None, sk=None):
            m = A.mark()
            xT = A.alloc([16, T], F32); sq = A.alloc([16, T], BF16); rstd = A.alloc([T], F32)
            xtok = [A.alloc([D], F32) for _ in range(2)]
            tiles = [(x_all[x_row0 + t * 128:x_row0 + (t + 1) * 128, :], 128, t * 128) for t in range(4)]
            if has_s:
                tiles.append((x_smp, NS, TP))
            P.dma("sp", cosT[:, 0:TP + NS], cos_in[:, pi, :], writes=["cosT"])
            P.dma("sp", sinT[:, 0:TP + NS], sin_in[:, pi, :], writes=["sinT"])
            for ti, (src, n, c0) in enumerate(tiles):
                xt = xtok[ti % 2]; kx = "xtok%d" % (ti % 2)
                P.dma("sp", xt[0:n, :], src, writes=[kx])
                for q in range(4):
                    pt, kp = PS7()
                    for jj in range(4):
                        c = q * 4 + jj
                        TR(pt[:, jj * 128:jj * 128 + n], xt[0:n, c * 128:(c + 1) * 128], ident[0:n, 0:n], [kx, "cst"], [kp])
                    CP("act", xT[:, q * 4:(q + 1) * 4, c0:c0 + n],
                       pt.rearrange("p (a b) -> p a b", a=4)[:, :, 0:n], [kp], ["xT"])
            fm_rstd(xT, "xT", Tn, sq, rstd, D)
            for kc in range(16):
                STT("dve", hnT[:, kc, 0:Tn], xT[:, kc, 0:Tn], nw[:, 0, kc:kc + 1], rstd[:, 0:Tn],
                    ALU.mult, ALU.mult, ["xT", "rstd", "nw"], ["hnT"])
            if dbgtag and "hnT" in dbg_out:
                hd = A.alloc([16, T], F32)
                CP("dve", hd, hnT, ["hnT"], ["hd"])
                P.dma("sp", dbg_out["hnT"], hd, reads=["hd"])
            barrier()
            A.release(m)
        task(None, s01)

        B = {}

        att_list = []; ssd_list = []
        CUR[0] = att_list

        def s2_alloc():
            barrier()
            A.release(B["m_ssd"])
            B["attnT"] = A.alloc([16, T], BF16)
            B["m_att"] = A.mark()
            B["qT"] = A.alloc([16, T], BF16)
            B["kT"] = A.alloc([4, 128 + T], BF16)
            B["kfl"] = A.alloc([4, T], F32)
            B["Vt"] = A.alloc([5, 512], BF16)
            B["Vlast"] = A.alloc([512], F32)
            B["vs_tok"] = A.alloc([512], F32)
            B["qsf"] = A.alloc([16, NS], F32)
            for nm in ("qf", "qr", "t1", "t2"):
                B[nm] = A.alloc([T], F32)
            CP("dve", B["kT"][:, :, 0:128], kTc, ["kTc"], ["kT"])
            CP("dve", B["Vt"][:, 0, :], Vc, ["Vc"], ["Vt0"])
        task(None, s2_alloc)

        def rope_chunk(res, kq):
            qf, qr, t1, t2 = B["qf"], B["qr"], B["t1"], B["t2"]
            for (o, kp, c0, c1) in res:
                CP("act", qf[:, c0:c1], o, [kp], ["qf"])
            if "no_rope" in opts:
                CP("dve", qr[:, 0:Tn], qf[:, 0:Tn], ["qf"], ["qr"])
                return
            for (c0, c1) in ranges(Tn):
                pt, kp2 = PS7() if c1 - c0 > 16 else PSS()
                o2 = pt[:, 0:c1 - c0]
                for a0 in range(c0, c1, 256):
                    a1 = min(c1, a0 + 256)
                    MM(o2[:, a0 - c0:a1 - c0], rotR, qf[:, a0:a1], True, True, ["qf", "cst"], [kp2])
                if c1 - c0 <= 16:
                    i = stg_n[0] % 8
                    stg_n[0] += 1
                    CP("act", stg[:, i, 0:c1 - c0], o2, [kp2], ["stg%d" % i])
                    TT("dve", t2[:, c0:c1], stg[:, i, 0:c1 - c0], sinT[:, c0:c1], ALU.mult, ["stg%d" % i, "sinT"], ["t2"])
                else:
                    TT("dve", t2[:, c0:c1], o2, sinT[:, c0:c1], ALU.mult, [kp2, "sinT"], ["t2"])
            TT("pool", t1[:, 0:Tn], qf[:, 0:Tn], cosT[:, 0:Tn], ALU.mult, ["qf", "cosT"], ["t1"])
            TT("pool", qr[:, 0:Tn], t1[:, 0:Tn], t2[:, 0:Tn], ALU.add, ["t1", "t2"], ["qr"])

        if full and "no_q" not in opts:
            for jb in range(4):
                def loads_q(slot, jb=jb):
                    v = wview(slot, 16, 512).rearrange("p k (i h d) -> p k i h d", i=4, h=2)
                    src = w_in[:, OQ + jb * 512:OQ + (jb + 1) * 512].rearrange("(k p) (h i d) -> p k i h d", p=128, h=2, i=4)
                    out = []
                    for i in range(4):
                        for h in range(2):
                            out.append((v[:, :, i, h, :], src[:, :, i, h, :], "a" if i < 2 else "b"))
                    return out

                def comp_q(slot, sk, jb=jb):
                    wv_ = wview(slot, 16, 512)
                    for i in range(4):
                        c = 4 * jb + i
                        res = proj(128, lambda kc: wv_[:, kc, i * 128:(i + 1) * 128], lambda kc, c0, c1: hnT[:, kc, c0:c1], 16, Tn, ["hnT", sk + ("a" if i < 2 else "b")])
                        rope_chunk(res, "q")
                        CP("act", B["qT"][:, c, 0:Tn], B["qr"][:, 0:Tn], ["qr"], ["qT"])
                        if has_s:
                            CP("dve", B["qsf"][:, c, :], B["qr"][:, TP:Tn], ["qr"], ["qsf"])
                task(loads_q, comp_q)

        def comp_k(slot, sk):
            wv_ = wview(slot, 16, 512)
            for c in range(4):
                res = proj(128, lambda kc: wv_[:, kc, c * 128:(c + 1) * 128], lambda kc, c0, c1: hnT[:, kc, c0:c1], 16, Tn, ["hnT", sk + ("a" if c < 2 else "b")])
                rope_chunk(res, "k")
                CP("act", B["kT"][:, c, 128:128 + Tn], B["qr"][:, 0:Tn], ["qr"], ["kT"])
                CP("dve", B["kfl"][:, c, 0:Tn], B["qr"][:, 0:Tn], ["qr"], ["kfl"])
        if "no_k" not in opts:
            task(wloads_plain(w_in, OK_, 512, 16), comp_k)

        def comp_v(slot, sk):
            wv_ = wview(slot, 16, 512)
            for tt in range(4):
                pt, kp = PS7()
                for kc in range(16):
                    MM(pt, hnT[:, kc, tt * 128:(tt + 1) * 128], wv_[:, kc, :], kc == 0, kc == 15, ["hnT", sk + "a", sk + "b"], [kp])
                CP("act", B["Vt"][:, 1 + tt, :], pt, [kp], ["Vt%d" % (1 + tt)])
                if tt == 3 and "no_vlast" not in opts:
                    CP("act", B["Vlast"], pt, [kp], ["Vlast"])
            if has_s:
                pt, kp = PS7()
                for kc in range(16):
                    MM(pt[0:NS, :], hnT[:, kc, TP:Tn], wv_[:, kc, :], kc == 0, kc == 15, ["hnT", sk + "a", sk + "b"], [kp])
                CP("act", B["vs_tok"][0:NS, :], pt[0:NS, :], [kp], ["vs_tok"])
        if "no_v" not in opts:
            task(wloads_plain(w_in, OV, 512, 16), comp_v)

        def s3(slot=None, sk=None):
            m = A.mark()
            s_sb = A.alloc([2, 256], F32); Pb = A.alloc([2, 256], BF16); PT = A.alloc([2, 2, 128], BF16)
            rmx = A.alloc([2], F32); ngm = A.alloc([2], F32); rsm = A.alloc([2], F32); es = A.alloc([2], F32)
            atok = A.alloc([D], BF16)
            qT, kT, Vt = B["qT"], B["kT"], B["Vt"]
            for blk in range(4):
                mk = masks[:, 1, :] if (blk == 0 and first_block_mask) else masks[:, 0, :]
                for hp in range(16):
                    pS, kS = PS7()
                    hs = (2 * hp, 2 * hp + 1)
                    for u, h in enumerate(hs):
                        kvh = h // 4; g = h % 4; jj = kvh // 2; half = kvh % 2
                        cq = 4 * jj + g
                        MM(pS[:, u * 256:(u + 1) * 256], qT[half * 64:(half + 1) * 64, cq, blk * 128:(blk + 1) * 128],
                           kT[half * 64:(half + 1) * 64, jj, blk * 128:blk * 128 + 256], True, True, ["qT", "kT"], [kS])
                    TT("dve", s_sb, pS.rearrange("p (a b) -> p a b", a=2), mk.unsqueeze(1).to_broadcast([128, 2, 256]), ALU.add,
                       [kS, "masks"], ["s_sb"])
                    op("dve", lambda e: e.tensor_reduce(out=rmx, in_=s_sb, axis=AX.X, op=ALU.max), ["s_sb"], ["rmx"])
                    TT("dve", rmx, rmx, sink8[:, 2 * hp:2 * hp + 2], ALU.max, ["rmx", "sink8"], ["rmx"])
                    TS("dve", ngm, rmx, -0.125, None, ALU.mult, None, ["rmx"], ["ngm"])
                    for u in range(2):
                        ACT(Pb[:, u, :], s_sb[:, u, :], AF.Exp, ["s_sb", "ngm"], ["Pb", "rsm"], scale=0.125, bias=ngm[:, u:u + 1], accum_out=rsm[:, u:u + 1])
                    TT("dve", es, sink8[:, 2 * hp:2 * hp + 2], rmx, ALU.subtract, ["rmx", "sink8"], ["es"])
                    ACT(es, es, AF.Exp, ["es"], ["es"], scale=0.125)
                    TT("dve", es, es, rsm, ALU.add, ["es", "rsm"], ["es"])
                    op("dve", lambda e: e.reciprocal(out=es, in_=es), ["es"], ["es"])
                    pT, kT_ = PS7()
                    pTb = pT.bitcast(BF16)
                    for u in range(2):
                        for kb in range(2):
                            TR(pTb[:, (u * 2 + kb) * 128:(u * 2 + kb + 1) * 128], Pb[:, u, kb * 128:(kb + 1) * 128], ident_b, ["Pb", "cst_b"], [kT_])
                    CP("act", PT, pTb[:, 0:512].rearrange("p (a b c) -> p a b c", a=2, b=2), [kT_], ["PT"])
                    pO, kO = PS7()
                    for u, h in enumerate(hs):
                        kvh = h // 4
                        for kb in range(2):
                            MM(pO[:, u * 64:(u + 1) * 64], PT[:, u, kb, :], Vt[:, blk + kb, kvh * 64:(kvh + 1) * 64], kb == 0, kb == 1,
                               ["PT", "Vt%d" % (blk + kb)], [kO])
                    for u, h in enumerate(hs):
                        ACT(atok[:, h * 64:(h + 1) * 64], pO[:, u * 64:(u + 1) * 64], AF.Copy, [kO, "es"], ["atok"], scale=es[:, u:u + 1])
                for q2 in range(2):
                    pT, kT_ = PS7()
                    pTb = pT.bitcast(BF16)
                    for c8 in range(8):
                        c = q2 * 8 + c8
                        TR(pTb[:, c8 * 128:(c8 + 1) * 128], atok[:, c * 128:(c + 1) * 128], ident_b, ["atok", "cst_b"], [kT_])
                    CP("act", B["attnT"][:, q2 * 8:(q2 + 1) * 8, blk * 128:(blk + 1) * 128], pTb.rearrange("p (a b) -> p a b", a=8), [kT_], ["attnT"])
            A.release(m)
            if dbgtag and "attnT" in dbg_out:
                hd = A.alloc([8, T], F32)
                for hh in range(2):
                    CP("dve", hd, B["attnT"][:, hh * 8:(hh + 1) * 8, :], ["attnT"], ["hd"])
                    P.dma("sp", dbg_out["attnT"][:, hh * 8:(hh + 1) * 8, :], hd[:, :, 0:TP + NS], reads=["hd"], writes=["hdo"])
        if full and "no_s3" not in opts:
            task(None, s3)

        def s3_carry(slot=None, sk=None):
            CP("dve", kTc, B["kT"][:, :, TP:TP + 128], ["kT"], ["kTc"])
            CP("dve", Vc, B["Vt"][:, 4, :], ["Vt4"], ["Vc"])
            if last:
                P.dma("sp", wv_out, B["Vlast"], reads=["Vlast"], writes=["wvo"])
                pt, kp = PS7()
                for c in range(4):
                    TR(pt[:, c * 128:(c + 1) * 128], B["kfl"][:, c, TP - 128:TP], ident, ["kfl", "cst"], [kp])
                CP("act", B["qf"][:, 0:512], pt, [kp], ["qf"])
                P.dma("sp", wk_out, B["qf"][:, 0:512], reads=["qf"], writes=["wko"])
        task(None, s3_carry)

        def smp_attn(slot=None, sk=None):
            qtok = A.alloc([D], F32); ktok = A.alloc([512], F32)
            for q4 in range(4):
                pt, kp = PS7()
                for jj in range(4):
                    TR(pt[0:NS, jj * 128:(jj + 1) * 128], B["qsf"][:, q4 * 4 + jj, :], ident, ["qsf", "cst"], [kp])
                for jj in range(4):
                    CP("act", qtok[0:NS, (8 * q4 + jj) * 64:(8 * q4 + jj + 1) * 64], pt[0:NS, jj * 128:jj * 128 + 64], [kp], ["qtok"])
                    CP("act", qtok[0:NS, (8 * q4 + 4 + jj) * 64:(8 * q4 + 5 + jj) * 64], pt[0:NS, jj * 128 + 64:(jj + 1) * 128], [kp], ["qtok"])
            pt, kp = PS7()
            for c in range(4):
                TR(pt[0:NS, c * 128:(c + 1) * 128], B["kfl"][:, c, TP:Tn], ident, ["kfl", "cst"], [kp])
            CP("act", ktok[0:NS, :], pt[0:NS, :], [kp], ["ktok"])
            P.dma("sp", scr_q, qtok[0:NS, :], reads=["qtok"], writes=["scr_q"])
            P.dma("sp", scr_k, ktok[0:NS, :], reads=["ktok"], writes=["scr_k"])
            P.dma("sp", scr_v, B["vs_tok"][0:NS, :], reads=["vs_tok"], writes=["scr_v"])
            P.dma("sp", wks_out[:, 127, :], ktok[0:NS, :], reads=["ktok"], writes=["wks1"])
            P.dma("sp", wvs_out[:, 127, :], B["vs_tok"][0:NS, :], reads=["vs_tok"], writes=["wvs1"])
            P.dma("sp", wks_out[:, 0:127, :], ck[:, 1:128, :, :].rearrange("b s k d -> b s (k d)"), writes=["wks0"])
            P.dma("sp", wvs_out[:, 0:127, :], cv[:, 1:128, :, :].rearrange("b s k d -> b s (k d)"), writes=["wvs0"])
            barrier()
            A.release(B["m_att"])
            qb = A.alloc([256], F32); kb_ = A.alloc([64], F32); vb = A.alloc([64], F32); sk8 = A.alloc([4], F32)
            P.dma("sp", qb, scr_q.rearrange("b (k f) -> (b k) f", k=8), reads=["scr_q"], writes=["qb"])
            P.dma("sp", kb_, scr_k.rearrange("b (k f) -> (b k) f", k=8), reads=["scr_k"], writes=["kb_"])
            P.dma("sp", vb, scr_v.rearrange("b (k f) -> (b k) f", k=8), reads=["scr_v"], writes=["vb"])
            for b in range(NS):
                P.dma("pool", sk8[b * 8:(b + 1) * 8, :], sink_in[0:1, :].rearrange("o (k g) -> (o k) g", g=4), writes=["sk8"])
            TS("dve", sk8, sk8, 8.0, None, ALU.mult, None, ["sk8"], ["sk8"])
            cbuf = [A.alloc([64, 64], F32) for _ in range(2)]
            prod = A.alloc([64, 64], F32)
            sc = A.alloc([4, 128], F32); pp = A.alloc([4, 128], F32)
            rmx = A.alloc([4], F32); ngm = A.alloc([4], F32); rsm = A.alloc([4], F32); es = A.alloc([4], F32)
            oacc = A.alloc([4, 64], F32); opart = A.alloc([4, 64], F32)
            for hf in range(2):
                cb_ = cbuf[hf]; kc_ = "cbuf%d" % hf
                for b in range(NS):
                    P.dma("pool", cb_[b * 8:(b + 1) * 8, :, :], ck[b, hf * 64:(hf + 1) * 64, :, :].rearrange("s k d -> k s d"), writes=[kc_])
                if hf == 0:
                    CP("dve", cb_[:, 0, :], kb_, ["kb_", kc_], [kc_])
                for g in range(4):
                    TT("pool" if g % 2 == 0 else "dve", prod, cb_, qb[:, g * 64:(g + 1) * 64].unsqueeze(1).to_broadcast([128, 64, 64]), ALU.mult, [kc_, "qb"], ["sprod"])
                    op("dve", lambda e, g=g, hf=hf: e.tensor_reduce(out=sc[:, g, hf * 64:(hf + 1) * 64], in_=prod, axis=AX.X, op=ALU.add), ["sprod"], ["ssc"])
            op("dve", lambda e: e.tensor_reduce(out=rmx, in_=sc, axis=AX.X, op=ALU.max), ["ssc"], ["srmx"])
            TT("dve", rmx, rmx, sk8, ALU.max, ["srmx", "sk8"], ["srmx"])
            TS("dve", ngm, rmx, -0.125, None, ALU.mult, None, ["srmx"], ["sngm"])
            for g in range(4):
                ACT(pp[:, g, :], sc[:, g, :], AF.Exp, ["ssc", "sngm"], ["spp", "srsm"], scale=0.125, bias=ngm[:, g:g + 1], accum_out=rsm[:, g:g + 1])
            TT("dve", es, sk8, rmx, ALU.subtract, ["sk8", "srmx"], ["ses"])
            ACT(es, es, AF.Exp, ["ses"], ["ses"], scale=0.125)
            TT("dve", es, es, rsm, ALU.add, ["ses", "srsm"], ["ses"])
            op("dve", lambda e: e.reciprocal(out=es, in_=es), ["ses"], ["ses"])
            for hf in range(2):
                cb_ = cbuf[hf]; kc_ = "cbuf%d" % hf
                for b in range(NS):
                    P.dma("pool", cb_[b * 8:(b + 1) * 8, :, :], cv[b, hf * 64:(hf + 1) * 64, :, :].rearrange("s k d -> k s d"), writes=[kc_])
                if hf == 0:
                    CP("dve", cb_[:, 0, :], vb, ["vb", kc_], [kc_])
                for g in range(4):
                    TT("pool" if g % 2 == 0 else "dve", prod, cb_, pp[:, g, hf * 64:(hf + 1) * 64].unsqueeze(2).to_broadcast([128, 64, 64]), ALU.mult, [kc_, "spp"], ["sprod"])
                    dst = oacc if hf == 0 else opart
                    op("dve", lambda e, g=g, dst=dst: e.tensor_reduce(out=dst[:, g, :], in_=prod.rearrange("p s d -> p d s"), axis=AX.X, op=ALU.add),
                       ["sprod"], ["soacc" if hf == 0 else "sopart"])
            TT("dve", oacc, oacc, opart, ALU.add, ["soacc", "sopart"], ["soacc"])
            TT("dve", oacc, oacc, es.unsqueeze(2).to_broadcast([128, 4, 64]), ALU.mult, ["soacc", "ses"], ["soacc"])
            P.dma("sp", scr_o.rearrange("b (k f) -> (b k) f", k=8), oacc.rearrange("p g d -> p (g d)"), reads=["soacc"], writes=["scr_o"])
            otok = A.alloc([D], F32)
            P.dma("sp", otok[0:NS, :], scr_o, reads=["scr_o"], writes=["otok"])
            pt, kp = PS7()
            ptb = pt
            for c in range(16):
                TR(pt[:, c * NS:(c + 1) * NS], otok[0:NS, c * 128:(c + 1) * 128], ident[0:NS, 0:NS], ["otok", "cst"], [kp])
            CP("act", B["attnT"][:, :, TP:Tn], pt[:, 0:16 * NS].rearrange("p (a b) -> p a b", a=16), [kp], ["attnT"])
        if has_s and full and "no_smp_attn" not in opts:
            task(None, smp_attn)

        CUR[0] = ssd_list

        def s4_alloc(slot=None, sk=None):
            B["m_pass"] = A.mark()
            B["ynT"] = A.alloc([32, T], BF16)
            if has_s:
                B["xpre_s"] = A.alloc([48, NS], F32); B["zs_s"] = A.alloc([32, NS], F32); B["dtT_s"] = A.alloc([8, NS], F32)
            B["m_ssd"] = A.mark()
            for nm in ("zs", "xs", "yT"):
                B[nm] = A.alloc([4, T], F32)
            B["xpre"] = A.alloc([4, 3 + T], F32)
            B["bcpre"] = A.alloc([2, 3 + T], F32)
            B["bcs"] = A.alloc([2, T], F32)
            B["BCT"] = A.alloc([2, T], BF16)
            B["dtT"] = A.alloc([T], F32)
            B["gsq"] = A.alloc([4, T], BF16)
            B["rstd2"] = A.alloc([T], F32)
            B["acc"] = A.alloc([T], F32)
            B["Xpad"] = [A.alloc([8, 128], BF16) for _ in range(2)]
            B["Xd"] = [A.alloc([512], BF16) for _ in range(2)]; B["Btok"] = [A.alloc([128], BF16) for _ in range(2)]
            for nm in ("dtk", "dA", "acum", "tot", "dec", "cd", "ndA", "nacum"):
                B[nm] = [A.alloc([8], F32) for _ in range(2)]
            B["dAtri"] = A.alloc([8, 128], F32)
            B["L"] = A.alloc([8, 128], F32); B["MT"] = A.alloc([8, 128], BF16)
            B["cbT"] = A.alloc([128], F32); B["Ebc"] = A.alloc([4, 128], F32)
            B["hTb"] = A.alloc([512], BF16); B["ytmp"] = A.alloc([4, 128], F32); B["htmp"] = A.alloc([512], F32)
            op("pool", lambda e: e.memset(B["Xpad"][0], 0.0), [], ["Xpad0"])
            op("pool", lambda e: e.memset(B["Xpad"][1], 0.0), [], ["Xpad1"])
        task(None, s4_alloc)

        def conv_silu(pre, c_in, ch, dst, kpre, kdst):
            acc = B["acc"]
            ACT(acc[:, 0:TP], pre[:, 0:TP], AF.Copy, [kpre, "convw"], ["acc"], scale=convw[:, ch, 0:1])
            for k in range(1, 4):
                STT("dve", acc[:, 0:TP], pre[:, k:k + TP], convw[:, ch, k:k + 1], acc[:, 0:TP], ALU.mult, ALU.add, [kpre, "acc", "convw"], ["acc"])
            ACT(dst[:, 0:TP], acc[:, 0:TP], AF.Silu, ["acc", "convb"], [kdst], bias=convb[:, ch:ch + 1])

        for g in range(NG):
            if full:
                def comp_z(slot, sk, g=g):
                    wv_ = wview(slot, 16, 512)
                    for i in range(4):
                        res = proj(128, lambda kc: wv_[:, kc, i * 128:(i + 1) * 128], lambda kc, c0, c1: hnT[:, kc, c0:c1], 16, Tn, ["hnT", sk + ("a" if i < 2 else "b")])
                        for (o, kp, c0, c1) in res:
                            ACT(B["zs"][:, i, c0:c1], o, AF.Silu, [kp], ["zs"])
                        if has_s:
                            CP("pool", B["zs_s"][:, 4 * g + i, :], B["zs"][:, i, TP:Tn], ["zs"], ["zs_s"])
                task(wloads_plain(w_in, OZ + g * 512, 512, 16), comp_z)

            def comp_x(slot, sk, g=g):
                wv_ = wview(slot, 16, 512)
                xpre = B["xpre"]
                CP("dve", xpre[:, :, 0:3], convc[:, 4 * g:4 * g + 4, :], ["convc"], ["xpre"])
                for i in range(4):
                    res = proj(128, lambda kc: wv_[:, kc, i * 128:(i + 1) * 128], lambda kc, c0, c1: hnT[:, kc, c0:c1], 16, Tn, ["hnT", sk + ("a" if i < 2 else "b")])
                    for (o, kp, c0, c1) in res:
                        CP("act", xpre[:, i, 3 + c0:3 + c1], o, [kp], ["xpre"])
                for i in range(4):
                    conv_silu(xpre[:, i, :], None, 4 * g + i, B["xs"][:, i, :], "xpre", "xs")
                CP("dve", convc[:, 4 * g:4 * g + 4, :], xpre[:, :, TP:TP + 3], ["xpre"], ["convc"])
                if has_s:
                    CP("pool", B["xpre_s"][:, 4 * g:4 * g + 4, :], xpre[:, :, 3 + TP:3 + Tn], ["xpre"], ["xpre_s"])
            task(wloads_plain(w_in, OX + g * 512, 512, 16), comp_x)

            def loads_bcdt(slot, g=g):
                f1 = wloads_plain(w_in, OB + g * 128, 128, 16, 0, 264, "a")(slot)
                f2 = wloads_plain(w_in, OC + g * 128, 128, 16, 128, 264, "b")(slot)
                f3 = wloads_plain(w_in, ODT + g * 8, 8, 16, 256, 264, "b")(slot)
                return f1 + f2 + f3

            def comp_ssd(slot, sk, g=g):
                wv_ = wview(slot, 16, 264)
                bcpre, bcs, BCT, dtT = B["bcpre"], B["bcs"], B["BCT"], B["dtT"]
                CP("dve", bcpre[:, 0, 0:3], convc[:, 32 + g, :], ["convc"], ["bcpre"])
                CP("dve", bcpre[:, 1, 0:3], convc[:, 40 + g, :], ["convc"], ["bcpre"])
                for i in range(2):
                    if i == 1 and not full and not kv:
                        continue
                    res = proj(128, lambda kc: wv_[:, kc, i * 128:(i + 1) * 128], lambda kc, c0, c1: hnT[:, kc, c0:c1], 16, Tn, ["hnT", sk + ("a" if i == 0 else "b")])
                    for (o, kp, c0, c1) in res:
                        CP("act", bcpre[:, i, 3 + c0:3 + c1], o, [kp], ["bcpre"])
                res = proj(8, lambda kc: wv_[:, kc, 256:264], lambda kc, c0, c1: hnT[:, kc, c0:c1], 16, Tn, ["hnT", sk + "b"])
                for (o, kp, c0, c1) in res:
                    ACT(dtT[0:8, c0:c1], o, AF.Exp, [kp, "dtb"], ["dtT"], bias=dtb[0:8, g:g + 1])
                ACT(dtT[0:8, 0:Tn], dtT[0:8, 0:Tn], AF.Ln, ["dtT"], ["dtT"], bias=1.0)
                for i in range(2 if full else 1):
                    conv_silu(bcpre[:, i, :], None, (32 if i == 0 else 40) + g, bcs[:, i, :], "bcpre", "bcs")
                CP("dve", convc[:, 32 + g, :], bcpre[:, 0, TP:TP + 3], ["bcpre"], ["convc"])
                if full or kv:
                    CP("dve", convc[:, 40 + g, :], bcpre[:, 1, TP:TP + 3], ["bcpre"], ["convc"])
                if full:
                    CP("act", BCT[:, :, 0:TP], bcs[:, :, 0:TP], ["bcs"], ["BCT"])
                if has_s:
                    CP("pool", B["xpre_s"][:, 32 + g, :], bcpre[:, 0, 3 + TP:3 + Tn], ["bcpre"], ["xpre_s"])
                    CP("pool", B["xpre_s"][:, 40 + g, :], bcpre[:, 1, 3 + TP:3 + Tn], ["bcpre"], ["xpre_s"])
                    CP("pool", B["dtT_s"][0:8, g, :], dtT[0:8, TP:Tn], ["dtT"], ["dtT_s"])
                hTg = hT[:, g, :]
                L, MT, cbT, Ebc, hTb, ytmp, htmp, dAtri = [B[n] for n in ("L", "MT", "cbT", "Ebc", "hTb", "ytmp", "htmp", "dAtri")]
                for c in range(4):
                    cs = slice(c * 128, (c + 1) * 128)
                    pr_ = c % 2
                    dtk, dA, acum, tot, dec, cd, ndA, nacum = [B[n][pr_] for n in ("dtk", "dA", "acum", "tot", "dec", "cd", "ndA", "nacum")]
                    Xpad, Xd, Btok = B["Xpad"][pr_], B["Xd"][pr_], B["Btok"][pr_]
                    pt, kp = PS7()
                    TR(pt[:, 0:8], dtT[0:8, cs], ident[0:8, 0:8], ["dtT", "cst"], [kp])
                    CP("act", dtk, pt[:, 0:8], [kp], ["dtk%d" % pr_])
                    TT("dve", dA, dtk, negA[:, g * 8:(g + 1) * 8], ALU.mult, ["dtk%d" % pr_, "negA"], ["dA%d" % pr_])
                    pa, kpa = PS7()
                    MM(pa[:, 0:8], triU, dA, True, True, ["dA%d" % pr_, "cst"], [kpa])
                    MM(pa[:, 8:16], ones_f, dA, True, True, ["dA%d" % pr_, "cst"], [kpa])
                    CP("act", acum, pa[:, 0:8], [kpa], ["acum%d" % pr_])
                    CP("act", tot, pa[:, 8:16], [kpa], ["tot%d" % pr_])
                    TT("dve", dec, tot, acum, ALU.subtract, ["tot%d" % pr_, "acum%d" % pr_], ["dec%d" % pr_])
                    ACT(dec, dec, AF.Exp, ["dec%d" % pr_], ["dec%d" % pr_])
                    TT("dve", dec, dec, dtk, ALU.mult, ["dec%d" % pr_, "dtk%d" % pr_], ["dec%d" % pr_])
                    ACT(cd, tot, AF.Exp, ["tot%d" % pr_], ["cd%d" % pr_])
                    px, kpx = PS7()
                    for i in range(4):
                        TR(px[:, i * 128:(i + 1) * 128], B["xs"][:, i, cs], ident, ["xs", "cst"], [kpx])
                    px3 = px.rearrange("p (r d) -> p r d", r=8)
                    TT("dve", Xd.rearrange("p (r d) -> p r d", r=8), px3, dec.unsqueeze(2).to_broadcast([128, 8, 64]), ALU.mult, [kpx, "dec%d" % pr_], ["Xd%d" % pr_])
                    if full:
                        for par in range(2):
                            TT("dve", Xpad[:, par::2, par * 64:(par + 1) * 64], px3[:, par::2, :],
                               dtk[:, par::2].unsqueeze(2).to_broadcast([128, 4, 64]), ALU.mult, [kpx, "dtk%d" % pr_], ["Xpad%d" % pr_])
                    pb, kpb = PS7()
                    TR(pb[:, 0:128], bcs[:, 0, cs], ident, ["bcs", "cst"], [kpb])
                    CP("act", Btok, pb[:, 0:128], [kpb], ["Btok%d" % pr_])
                    if full:
                        TT("pool", dAtri, triU.unsqueeze(1).to_broadcast([128, 8, 128]), dA.unsqueeze(2).to_broadcast([128, 8, 128]), ALU.mult,
                           ["dA%d" % pr_, "cst"], ["dAtri"])
                        pA = []
                        for hf in range(2):
                            p_, k_ = PS7()
                            for q4 in range(2):
                                r0 = hf * 4 + q4 * 2
                                MM(p_[:, q4 * 256:(q4 + 1) * 256], ones_f, dAtri[:, r0:r0 + 2, :].rearrange("p a b -> p (a b)"), True, True, ["dAtri", "cst"], [k_])
                            pA.append((p_, k_))
                        TS("dve", nacum, acum, -1.0, None, ALU.mult, None, ["acum%d" % pr_], ["nacum%d" % pr_])
                        pc, kpc = PS7()
                        MM(pc[:, 0:128], BCT[:, 0, cs], BCT[:, 1, cs], True, True, ["BCT"], [kpc])
                        TT("dve", cbT, pc[:, 0:128], ssdmask, ALU.mult, [kpc, "cst"], ["cbT"])
                        for r in range(8):
                            p_, k_ = pA[r // 4]
                            a_ = p_[:, (r % 4) * 128:(r % 4 + 1) * 128]
                            STT("dve", L[:, r, :], a_, nacum[:, r:r + 1], ssdmask, ALU.add, ALU.mult, [k_, "nacum%d" % pr_, "cst"], ["L"])
                        ACT(L, L, AF.Exp, ["L"], ["L"])
                        TT("pool", MT, L, cbT.unsqueeze(1).to_broadcast([128, 8, 128]), ALU.mult, ["L", "cbT"], ["MT"])
                        for jx in range(4):
                            for par in range(2):
                                r = 2 * jx + par
                                p_, k_ = pA[r // 4]
                                CP("dve", Ebc[par * 64:(par + 1) * 64, jx, :], p_[par * 64:(par + 1) * 64, (r % 4) * 128:(r % 4 + 1) * 128], [k_], ["Ebc"])
                        ACT(Ebc, Ebc, AF.Exp, ["Ebc"], ["Ebc"])
                        CP("act", hTb, hTg, ["hT"], ["hTb"])
                        po, kpo = PS7()
                        for jx in range(4):
                            MM(po[:, jx * 128:(jx + 1) * 128], hTb[:, jx * 128:(jx + 1) * 128], BCT[:, 1, cs], True, True, ["hTb", "BCT"], [kpo])
                        TT("dve", ytmp, po.rearrange("p (a b) -> p a b", a=4), Ebc, ALU.mult, [kpo, "Ebc"], ["ytmp"])
                        pd, kpd = PS7()
                        for jx in range(4):
                            for par in range(2):
                                r = 2 * jx + par
                                MM(pd[:, jx * 128:(jx + 1) * 128], Xpad[:, r, :], MT[:, r, :], par == 0, par == 1, ["Xpad%d" % pr_, "MT"], [kpd])
                        TT("dve", ytmp, pd.rearrange("p (a b) -> p a b", a=4), ytmp, ALU.add, [kpd, "ytmp"], ["ytmp"])
                        for jx in range(4):
                            STT("dve", B["yT"][:, jx, cs], B["xs"][:, jx, cs], dsk[:, 4 * g + jx:4 * g + jx + 1], ytmp[:, jx, :], ALU.mult, ALU.add,
                                ["xs", "ytmp", "dsk"], ["yT"])
                    pst, kps = PS7()
                    MM(pst, Btok, Xd, True, True, ["Btok%d" % pr_, "Xd%d" % pr_], [kps])
                    TT("dve", htmp.rearrange("p (r d) -> p r d", r=8), hTg.rearrange("p (r d) -> p r d", r=8),
                       cd.unsqueeze(2).to_broadcast([128, 8, 64]), ALU.mult, ["hT", "cd%d" % pr_], ["htmp"])
                    TT("dve", hTg, htmp, pst, ALU.add, ["htmp", kps], ["hT"])
                if full:
                    yT, zs, gsq, rstd2 = B["yT"], B["zs"], B["gsq"], B["rstd2"]
                    TT("pool", yT[:, :, 0:TP], yT[:, :, 0:TP], zs[:, :, 0:TP], ALU.mult, ["yT", "zs"], ["yT"])
                    ACT(gsq[:, :, 0:TP], yT[:, :, 0:TP], AF.Square, ["yT"], ["gsq"])
                    res = proj(128, lambda kc: ones_b, lambda kc, c0, c1: gsq[:, kc, c0:c1], 4, TP, ["gsq", "cst_b"])
                    for (o, kp, c0, c1) in res:
                        TS("dve", rstd2[:, c0:c1], o, 1.0 / 512, EPS, ALU.mult, ALU.add, [kp], ["rstd2"])
                    ACT(rstd2[:, 0:TP], rstd2[:, 0:TP], AF.Ln, ["rstd2"], ["rstd2"])
                    ACT(rstd2[:, 0:TP], rstd2[:, 0:TP], AF.Exp, ["rstd2"], ["rstd2"], scale=-0.5)
                    for jx in range(4):
                        STT("dve", B["ynT"][:, 4 * g + jx, 0:TP], yT[:, jx, 0:TP], snw[:, 4 * g + jx:4 * g + jx + 1], rstd2[:, 0:TP], ALU.mult, ALU.mult,
                            ["yT", "rstd2", "snw"], ["ynT"])
            task(loads_bcdt, comp_ssd)


        def smp_ssm(slot=None, sk=None):
            barrier()
            A.release(B["m_ssd"])
            xpre_s, zs_s, dtT_s = B["xpre_s"], B["zs_s"], B["dtT_s"]
            ST = A.alloc([48, 48], F32)
            sct = A.alloc([1536], F32)
            sc2 = sconv.rearrange("b k c -> (b k) c")
            for q in range(4):
                P.dma("sp", sct[0:48, :], sc2[:, q * 1536:(q + 1) * 1536], writes=["sct"])
                for j4 in range(3):
                    pt, kp = PS7()
                    for jj in range(4):
                        lc = j4 * 4 + jj
                        TR(pt[:, jj * 48:(jj + 1) * 48], sct[0:48, lc * 128:(lc + 1) * 128], ident[0:48, 0:48], ["sct", "cst"], [kp])
                    CP("act", ST[:, q * 12 + j4 * 4:q * 12 + j4 * 4 + 4, :], pt[:, 0:192].rearrange("p (a b) -> p a b", a=4), [kp], ["ST"])
            STv = ST.rearrange("p c (b k) -> p c b k", k=3)
            acc = A.alloc([48, NS], F32); tmp = A.alloc([48, NS], F32); xc_s = A.alloc([48, NS], F32)
            TT("pool", acc, STv[:, :, :, 0], convw[:, :, 0:1].to_broadcast([128, 48, NS]), ALU.mult, ["ST", "convw"], ["sacc"])
            for k in (1, 2):
                TT("pool", tmp, STv[:, :, :, k], convw[:, :, k:k + 1].to_broadcast([128, 48, NS]), ALU.mult, ["ST", "convw"], ["stmp"])
                TT("pool", acc, acc, tmp, ALU.add, ["sacc", "stmp"], ["sacc"])
            TT("pool", tmp, xpre_s, convw[:, :, 3:4].to_broadcast([128, 48, NS]), ALU.mult, ["xpre_s", "convw"], ["stmp"])
            TT("pool", acc, acc, tmp, ALU.add, ["sacc", "stmp"], ["sacc"])
            TT("pool", acc, acc, convb.unsqueeze(2).to_broadcast([128, 48, NS]), ALU.add, ["sacc", "convb"], ["sacc"])
            ACT(xc_s, acc, AF.Silu, ["sacc"], ["xc_s"])
            if "ssm_stop1" in opts:
                return
            P.dma("sp", cvs_out[:, 0:2, :], sconv[:, 1:3, :], writes=["cvs01"])
            m_tok = A.mark()
            tok = A.alloc([CONVD], F32)
            for (srcT, ksrc, dst, kd) in ((xpre_s, "xpre_s", cvs_out[:, 2, :], "cvs2"), (xc_s, "xc_s", scr_x, "scr_x")):
                for q in range(12):
                    pt, kp = PS7()
                    for jj in range(4):
                        TR(pt[0:NS, jj * 128:(jj + 1) * 128], srcT[:, q * 4 + jj, :], ident, [ksrc, "cst"], [kp])
                    CP("act", tok[0:NS, q * 512:(q + 1) * 512], pt[0:NS, :], [kp], ["stok"])
                P.dma("sp", dst, tok[0:NS, :], reads=["stok"], writes=[kd])
            dtt = A.alloc([64], F32)
            pt, kp = PS7()
            for g in range(NG):
                TR(pt[0:NS, g * 8:(g + 1) * 8], dtT_s[0:8, g, :], ident[0:8, 0:8], ["dtT_s", "cst"], [kp])
            CP("act", dtt[0:NS, :], pt[0:NS, 0:64], [kp], ["dtt"])
            P.dma("sp", scr_dt, dtt[0:NS, :], reads=["dtt"], writes=["scr_dt"])
            if "ssm_stop2" in opts:
                return
            barrier()
            A.release(m_tok)
            Xg = A.alloc([64], F32); Bg = A.alloc([128], F32); Cg = A.alloc([128], F32)
            dtg = A.alloc([1], F32); ag = A.alloc([1], F32); da = A.alloc([1], F32)
            yg = [A.alloc([64], F32) for _ in range(2)]
            hb = [A.alloc([8, 128], F32) for _ in range(4)]
            tms = [A.alloc([8, 128], F32) for _ in range(2)]; prs = [A.alloc([8, 128], F32) for _ in range(2)]
            for g in range(NG):
                P.dma("pool", Xg, scr_x[:, g * 512:(g + 1) * 512].rearrange("b (r p) -> b r p", r=8), reads=["scr_x"], writes=["Xg"])
                P.dma("pool", Bg, scr_x[:, OB - OX + g * 128:OB - OX + (g + 1) * 128].unsqueeze(1).to_broadcast([NS, 8, 128]), reads=["scr_x"], writes=["Bg"])
                P.dma("pool", Cg, scr_x[:, OC - OX + g * 128:OC - OX + (g + 1) * 128].unsqueeze(1).to_broadcast([NS, 8, 128]), reads=["scr_x"], writes=["Cg"])
                P.dma("pool", dtg, scr_dt[:, g * 8:(g + 1) * 8].unsqueeze(2), reads=["scr_dt"], writes=["dtg"])
                P.dma("pool", ag, alog_in[:, g * 8:(g + 1) * 8].unsqueeze(2).to_broadcast([NS, 8, 1]), writes=["ag"])
                ACT(ag, ag, AF.Exp, ["ag"], ["ag"])
                TT("dve", da, dtg, ag, ALU.mult, ["dtg", "ag"], ["da"])
                ACT(da, da, AF.Exp, ["da"], ["da"], scale=-1.0)
                TS("dve", Xg, Xg, dtg[:, 0:1], None, ALU.mult, None, ["Xg", "dtg"], ["Xg"])
                y_ = yg[g % 2]; ky = "yg%d" % (g % 2)
                for pc in range(8):
                    h = hb[pc % 4]; kh = "hb%d" % (pc % 4)
                    tm = tms[pc % 2]; pr = prs[pc % 2]; ktm = "tm%d" % (pc % 2); kpr = "pr%d" % (pc % 2)
                    hsrc = sssm[:, g * 8:(g + 1) * 8, pc * 8:(pc + 1) * 8, :].rearrange("b r p n -> b r (p n)")
                    hdst = sss_out[:, g * 8:(g + 1) * 8, pc * 8:(pc + 1) * 8, :].rearrange("b r p n -> b r (p n)")
                    if "ssm_noload" not in opts:
                        P.dma("sp", h.rearrange("q a b -> q (a b)"), hsrc, writes=[kh])
                    if "ssm_nocomp" in opts:
                        if "ssm_nostore" not in opts:
                            P.dma("act", hdst, h.rearrange("q a b -> q (a b)"), reads=[kh], writes=["ssso"])
                        continue
                    TT("pool", tm, Xg[:, pc * 8:(pc + 1) * 8].unsqueeze(2).to_broadcast([128, 8, 128]), Bg.unsqueeze(1).to_broadcast([128, 8, 128]), ALU.mult,
                       ["Xg", "Bg"], [ktm])
                    STT("dve", h, h, da[:, 0:1], tm, ALU.mult, ALU.add, [kh, "da", ktm], [kh])
                    if "ssm_nostore" not in opts:
                        P.dma("act", hdst, h.rearrange("q a b -> q (a b)"), reads=[kh], writes=["ssso"])
                    TT("pool" if pc % 2 == 0 else "dve", pr, h, Cg.unsqueeze(1).to_broadcast([128, 8, 128]), ALU.mult, [kh, "Cg"], [kpr])
                    op("dve", lambda e, y_=y_, pc=pc, pr=pr: e.tensor_reduce(out=y_[:, pc * 8:(pc + 1) * 8], in_=pr, axis=AX.X, op=ALU.add), [kpr], [ky])
                P.dma("sp", scr_y[:, g * 512:(g + 1) * 512].rearrange("b (r p) -> b r p", r=8), y_, reads=[ky], writes=["scr_y"])
            if "ssm_stop3" in opts:
                return
            ytk = A.alloc([DI], F32)
            P.dma("sp", ytk[0:NS, :], scr_y, reads=["scr_y"], writes=["ytk"])
            yTs = A.alloc([32, NS], F32); gq = A.alloc([32, 32], BF16)[:, :, 0:NS]; rs = A.alloc([8, NS], F32)
            for q in range(8):
                pt, kp = PS7()
                for jj in range(4):
                    TR(pt[:, jj * NS:(jj + 1) * NS], ytk[0:NS, (q * 4 + jj) * 128:(q * 4 + jj + 1) * 128], ident[0:NS, 0:NS], ["ytk", "cst"], [kp])
                CP("act", yTs[:, q * 4:(q + 1) * 4, :], pt[:, 0:4 * NS].rearrange("p (a b) -> p a b", a=4), [kp], ["yTs"])
            if "ssm_stop4" in opts:
                return
            TT("pool", tmp[:, 0:32, :], xc_s[:, 0:32, :], dsk.unsqueeze(2).to_broadcast([128, 32, NS]), ALU.mult, ["xc_s", "dsk"], ["stmp"])
            TT("pool", yTs, yTs, tmp[:, 0:32, :], ALU.add, ["yTs", "stmp"], ["yTs"])
            TT("pool", yTs, yTs, zs_s, ALU.mult, ["yTs", "zs_s"], ["yTs"])
            if "ssm_stop5" in opts:
                return
            ACT(gq, yTs, AF.Square, ["yTs"], ["gq"])
            if "ssm_stop6" in opts:
                return
            for g in range(NG):
                pt_, kp = PS7()
                o = pt_[:, 0:NS]
                for jx in range(4):
                    MM(o, ones_b, gq[:, 4 * g + jx, :], jx == 0, jx == 3, ["gq", "cst_b"], [kp])
                ACT(rs[:, g, :], o, AF.Copy, [kp], ["rs"], scale=1.0 / 512)
            TS("dve", rs, rs, EPS, None, ALU.add, None, ["rs"], ["rs"])
            ACT(rs, rs, AF.Ln, ["rs"], ["rs"])
            ACT(rs, rs, AF.Exp, ["rs"], ["rs"], scale=-0.5)
            if "ssm_stop7" in opts:
                return
            for c in range(32):
                STT("dve", B["ynT"][:, c, TP:Tn], yTs[:, c, :], snw[:, c:c + 1], rs[:, c // 4, :], ALU.mult, ALU.mult, ["yTs", "rs", "snw"], ["ynT"])
        if has_s and "no_smp_ssm" not in opts:
            task(None, smp_ssm)

        CUR[0] = TASKS
        TASKS.extend(ssd_list)
        if full or kv:
            TASKS.extend(att_list)
        if not full:
            def p_end(slot=None, sk=None):
                barrier()
                A.release(B["m_pass"])
            task(None, p_end)
            return

        def s5_alloc(slot=None, sk=None):
            barrier()
            A.release(B["m_att"])
            B["mergedT"] = A.alloc([16, T], BF16)
            B["sg"] = A.alloc([2, T], F32)
            B["mt"] = A.alloc([2, T], F32)
        task(None, s5_alloc)

        for cb in range(4):
            st5 = {}

            def mk_acc(name, nk, cw, sub):
                def comp(slot, sk, cb=cb):
                    wv_ = wview(slot, nk, cw)
                    src = {"ab": B["attnT"], "sb": B["ynT"], "ga": hnT, "gs": hnT}[name[0:2]]
                    ksrc = {"ab": "attnT", "sb": "ynT", "ga": "hnT", "gs": "hnT"}[name[0:2]]
                    return wv_, src, ksrc
                return comp

            def comp_merge_blk(slots, cb=cb):
                pass

        for cb in range(4):
            def comp_ga(slot, sk, cb=cb):
                wv_ = wview(slot, 16, 512)
                for i in range(4):
                    res = proj(128, lambda kc: wv_[:, kc, i * 128:(i + 1) * 128], lambda kc, c0, c1: hnT[:, kc, c0:c1], 16, Tn, ["hnT", sk + ("a" if i < 2 else "b")])
                    for (o, kp, c0, c1) in res:
                        ACT(B["sgA"][:, i, c0:c1], o, AF.Sigmoid, [kp], ["sgA"])
            def comp_gs(slot, sk, cb=cb):
                wv_ = wview(slot, 16, 512)
                for i in range(4):
                    res = proj(128, lambda kc: wv_[:, kc, i * 128:(i + 1) * 128], lambda kc, c0, c1: hnT[:, kc, c0:c1], 16, Tn, ["hnT", sk + ("a" if i < 2 else "b")])
                    for (o, kp, c0, c1) in res:
                        ACT(B["sgS"][:, i, c0:c1], o, AF.Sigmoid, [kp], ["sgS"])
            def comp_ab(slot, sk, cb=cb):
                wv_ = wview(slot, 16, 512)
                for i in range(4):
                    res = proj(128, lambda kc: wv_[:, kc, i * 128:(i + 1) * 128], lambda kc, c0, c1: B["attnT"][:, kc, c0:c1], 16, Tn, ["attnT", sk + ("a" if i < 2 else "b")])
                    for (o, kp, c0, c1) in res:
                        TT("dve", B["mtmp"][:, i, c0:c1], o, B["sgA"][:, i, c0:c1], ALU.mult, [kp, "sgA"], ["mtmp"])
            def comp_sb(slot, sk, cb=cb, half=0):
                pass
            if cb == 0:
                def s5b(slot=None, sk=None):
                    B["sgA"] = A.alloc([4, T], F32); B["sgS"] = A.alloc([4, T], F32); B["mtmp"] = A.alloc([4, T], F32)
                task(None, s5b)
            task(wloads_plain(w_in, OGA + cb * 512, 512, 16), comp_ga)
            task(wloads_plain(w_in, OGS + cb * 512, 512, 16), comp_gs)
            task(wloads_plain(w_ab, cb * 512, 512, 16), comp_ab)
            for hf in range(2):
                def comp_sbh(slot, sk, cb=cb, hf=hf):
                    wv_ = wview(slot, 32, 256)
                    for i2 in range(2):
                        i = hf * 2 + i2
                        res = proj(128, lambda kc: wv_[:, kc, i2 * 128:(i2 + 1) * 128], lambda kc, c0, c1: B["ynT"][:, kc, c0:c1], 32, Tn, ["ynT", sk + ("a" if i2 == 0 else "b")])
                        for (o, kp, c0, c1) in res:
                            TT("dve", B["sgS"][:, i, c0:c1], o, B["sgS"][:, i, c0:c1], ALU.mult, [kp, "sgS"], ["sgS"])
                        TT("pool", B["mergedT"][:, cb * 4 + i, 0:Tn], B["sgS"][:, i, 0:Tn], B["mtmp"][:, i, 0:Tn], ALU.add, ["sgS", "mtmp"], ["mergedT"])
                task(wloads_plain(w_sb, cb * 512 + hf * 256, 256, 32), comp_sbh)

        if has_s and "smp" in dbg_out:
            def dbg_smp(slot=None, sk=None):
                d1 = A.alloc([16, NS], F32); d2 = A.alloc([32, NS], F32)
                CP("dve", d1, B["attnT"][:, :, TP:Tn], ["attnT"], ["d1"])
                CP("dve", d2, B["ynT"][:, :, TP:Tn], ["ynT"], ["d2"])
                P.dma("sp", dbg_out["smp"][:, 0:16, :], d1, reads=["d1"], writes=["dbgo1"])
                P.dma("sp", dbg_out["smp"][:, 16:48, :], d2, reads=["d2"], writes=["dbgo2"])
            task(None, dbg_smp)

        def s6_alloc(slot=None, sk=None):
            barrier()
            A.release(B["m_pass"])
            B["mixT"] = A.alloc([16, T], F32)
            B["xT"] = A.alloc([16, T], F32)
            B["rstd"] = A.alloc([T], F32)
            B["m_ffn"] = A.mark()
            CP("dve", hnT[:, :, 0:Tn], B["mergedT"][:, :, 0:Tn], ["mergedT"], ["hnT"])
            barrier()
        task(None, s6_alloc)

        def reload_x(slot=None, sk=None):
            xtok = B["xtok2"] = [A.alloc([D], F32) for _ in range(2)]
            tiles = [(x_all[x_row0 + t * 128:x_row0 + (t + 1) * 128, :], 128, t * 128) for t in range(4)]
            if has_s:
                tiles.append((x_smp, NS, TP))
            for ti, (src, n, c0) in enumerate(tiles):
                xt = xtok[ti % 2]; kx = "xtokb%d" % (ti % 2)
                P.dma("sp", xt[0:n, :], src, writes=[kx])
                for q in range(4):
                    pt, kp = PS7()
                    for jj in range(4):
                        c = q * 4 + jj
                        TR(pt[:, jj * 128:jj * 128 + n], xt[0:n, c * 128:(c + 1) * 128], ident[0:n, 0:n], [kx, "cst"], [kp])
                    CP("act", B["xT"][:, q * 4:(q + 1) * 4, c0:c0 + n], pt.rearrange("p (a b) -> p a b", a=4)[:, :, 0:n], [kp], ["xT2"])
        task(None, reload_x)

        for cb in range(4):
            def comp_o(slot, sk, cb=cb):
                wv_ = wview(slot, 16, 512)
                for i in range(4):
                    res = proj(128, lambda kc: wv_[:, kc, i * 128:(i + 1) * 128], lambda kc, c0, c1: hnT[:, kc, c0:c1], 16, Tn, ["hnT", sk + ("a" if i < 2 else "b")])
                    for (o, kp, c0, c1) in res:
                        CP("act", B["mixT"][:, cb * 4 + i, c0:c1], o, [kp], ["mixT"])
            task(wloads_plain(w_o, cb * 512, 512, 16), comp_o)

        def add_norm(widx, srcname, sqbuf):
            fm_rstd(B[srcname], srcname, Tn, sqbuf, B["rstd"], D)
            for kc in range(16):
                STT("dve", B[srcname][:, kc, 0:Tn], B[srcname][:, kc, 0:Tn], nw[:, widx, kc:kc + 1], B["rstd"][:, 0:Tn], ALU.mult, ALU.mult,
                    [srcname, "rstd", "nw"], [srcname])
            TT("pool", B["xT"][:, :, 0:Tn], B["xT"][:, :, 0:Tn], B[srcname][:, :, 0:Tn], ALU.add, ["xT2", srcname], ["xT2"])

        def s6b(slot=None, sk=None):
            barrier()
            A.release(B["m_ffn"])
            B["actT"] = A.alloc([44, T], BF16)
            sq = B["actT"][:, 0:16, :]
            add_norm(1, "mixT", sq)
            fm_rstd(B["xT"], "xT2", Tn, sq, B["rstd"], D)
            for kc in range(16):
                STT("dve", hnT[:, kc, 0:Tn], B["xT"][:, kc, 0:Tn], nw[:, 2, kc:kc + 1], B["rstd"][:, 0:Tn], ALU.mult, ALU.mult,
                    ["xT2", "rstd", "nw"], ["hnT"])
            barrier()
            B["sgu"] = A.alloc([4, T], F32)
        task(None, s6b)

        for fb in range(11):
            def comp_g(slot, sk, fb=fb):
                wv_ = wview(slot, 16, 512)
                B["pend"] = []
                for i in range(4):
                    res = proj(128, lambda kc: wv_[:, kc, i * 128:(i + 1) * 128], lambda kc, c0, c1: hnT[:, kc, c0:c1], 16, Tn, ["hnT", sk + ("a" if i < 2 else "b")])
                    for (o, kp, c0, c1) in res:
                        ACT(B["sgu"][:, i, c0:c1], o, AF.Silu, [kp], ["sgu"])
            def comp_u(slot, sk, fb=fb):
                wv_ = wview(slot, 16, 512)
                for i in range(4):
                    res = proj(128, lambda kc: wv_[:, kc, i * 128:(i + 1) * 128], lambda kc, c0, c1: hnT[:, kc, c0:c1], 16, Tn, ["hnT", sk + ("a" if i < 2 else "b")])
                    for (o, kp, c0, c1) in res:
                        TT("dve", B["actT"][:, fb * 4 + i, c0:c1], o, B["sgu"][:, i, c0:c1], ALU.mult, [kp, "sgu"], ["actT"])
            task(wloads_plain(w_gu, fb * 512, 512, 16), comp_g)
            task(wloads_plain(w_gu, DFF + fb * 512, 512, 16), comp_u)
        for cbk in range(16):
            def comp_d(slot, sk, cbk=cbk):
                wv_ = wview(slot, 44, 128)
                res = proj(128, lambda kc: wv_[:, kc, :], lambda kc, c0, c1: B["actT"][:, kc, c0:c1], 44, Tn, ["actT", sk + "a"])
                for (o, kp, c0, c1) in res:
                    CP("act", B["mixT"][:, cbk, c0:c1], o, [kp], ["mixT"])
            task(wloads_plain(w_dn, cbk * 128, 128, 44), comp_d)

        def s8(slot=None, sk=None):
            barrier()
            A.release(B["m_ffn"])
            sq = A.alloc([16, T], BF16)
            add_norm(3, "mixT", sq)
            ytok = [A.alloc([D], F32) for _ in range(2)]
            tiles = [(y_out[y_row0 + t * 128:y_row0 + (t + 1) * 128, :], 128, t * 128) for t in range(4)]
            if has_s:
                tiles.append((ys_out, NS, TP))
            for ti, (dst, n, c0) in enumerate(tiles):
                yt = ytok[ti % 2]; ky = "ytok%d" % (ti % 2)
                for q in range(4):
                    pt, kp = PS7()
                    for jj in range(4):
                        c = q * 4 + jj
                        TR(pt[0:n, jj * 128:(jj + 1) * 128], B["xT"][:, c, c0:c0 + n], ident, ["xT2", "cst"], [kp])
                    CP("act", yt[0:n, q * 512:(q + 1) * 512], pt[0:n, :], [kp], [ky])
                P.dma("sp", dst, yt[0:n, :], reads=[ky], writes=["yout"])
            barrier()
            A.release(pm)
        task(None, s8)

    if "only_p3" not in opts:
        emit_pass(0, False, 0, False, 0, False, kv=False)
        emit_pass(1, False, 512, False, 0, False)

    def apply_flag(slot=None, sk=None):
        TS("dve", hT, hT, flag[:, 0:1], None, ALU.mult, None, ["hT", "flag"], ["hT"])
    task(None, apply_flag)
    if "only_p3" not in opts:
        emit_pass(2, True, 1024, False, 0, True)
    emit_pass(3, True, 1536, "no_s" not in opts, 512, False)

    def final_out(slot=None, sk=None):
        m = A.mark()
        ctok = A.alloc([CONVD], F32)
        for q in range(12):
            pt, kp = PS7()
            for jj in range(4):
                c = q * 4 + jj
                TR(pt[0:3, jj * 128:(jj + 1) * 128], convc[:, c, :], ident, ["convc", "cst"], [kp])
            CP("act", ctok[0:3, q * 512:(q + 1) * 512], pt[0:3, :], [kp], ["ctok"])
        P.dma("sp", cv_out, ctok[0:3, :], reads=["ctok"], writes=["cvo"])
        hto = [A.alloc([512], F32) for _ in range(2)]
        for g in range(NG):
            pt, kp = PS7()
            for jx in range(4):
                TR(pt[:, jx * 128:(jx + 1) * 128], hT[:, g, jx * 128:(jx + 1) * 128], ident, ["hT", "cst"], [kp])
            CP("act", hto[g % 2], pt, [kp], ["hto%d" % (g % 2)])
            P.dma("sp", ss_out[g * 512:(g + 1) * 512, :].rearrange("(j p) n -> p j n", p=128),
                  hto[g % 2].rearrange("p (j n) -> p j n", j=4), reads=["hto%d" % (g % 2)], writes=["sso"])
        A.release(m)
    task(None, final_out)

    wt = [i for i, (l, c) in enumerate(TASKS) if l is not None]
    slot_of = {ti: (n % NSLOT) for n, ti in enumerate(wt)}

    def do_load(ti):
        sl = slot_of[ti]
        for (o, i_, sfx) in TASKS[ti][0](wslots[sl]):
            P.dma("pool", o, i_, writes=["ws%d%s" % (sl, sfx)])

    nxt = 0
    if wt:
        do_load(wt[0]); nxt = 1
    for ti, (l, c) in enumerate(TASKS):
        if l is not None:
            if nxt < len(wt):
                do_load(wt[nxt]); nxt += 1
            sl = slot_of[ti]
            c(wslots[sl], "ws%d" % sl)
        else:
            c()
    P.finalize(st)
    return nc, P


_CACHE = {}


def _host_consts():
    ident = np.eye(128, dtype=np.float32)
    tri = np.triu(np.ones((128, 128), np.float32))
    R = np.zeros((128, 128), np.float32)
    for m in range(128):
        if m % 64 < 32:
            R[m + 32, m] = -1.0
        else:
            R[m - 32, m] = 1.0
    return np.ascontiguousarray(np.stack([ident, tri, np.ones((128, 128), np.float32), R, tri, ident], 1))


def _rope_tables(s0):
    inv = (10000.0 ** (-np.arange(32, dtype=np.float32) / 32)).astype(np.float32)
    cosT = np.zeros((128, 4, TP + NS), np.float32)
    sinT = np.zeros((128, 4, TP + NS), np.float32)
    for pi in range(4):
        pos = (s0 - 1024 + pi * 512 + np.arange(512)).astype(np.float32)
        pos = np.concatenate([pos, np.full((NS,), 16384.0, np.float32)])
        ang = pos[None, :] * inv[:, None]
        c = np.cos(ang).astype(np.float32); sn = np.sin(ang).astype(np.float32)
        idx = np.arange(128) % 32
        cosT[:, pi, :] = c[idx]; sinT[:, pi, :] = sn[idx]
    return cosT, sinT


def kernel(x_prompt, x_sample, cache_win_k, cache_win_v, state_conv, state_ssm,
           norm_mix_pre, norm_mix_post, w_in, attn_sinks, w_attn_branch, conv_w, conv_b,
           dt_bias, a_log, d_skip, ssm_norm, w_ssm_branch, w_out,
           norm_ffn_pre, norm_ffn_post, w_gate_up, w_down):
    f = lambda a: np.ascontiguousarray(np.asarray(a, dtype=np.float32))
    if "nc" not in _CACHE:
        _CACHE["nc"] = build_program()[0]
    nc = _CACHE["nc"]
    xp = f(x_prompt); xs = f(x_sample)
    nws = np.stack([f(norm_mix_pre)[0], f(norm_mix_post)[0], f(norm_ffn_pre)[0], f(norm_ffn_post)[0]], 0)
    nw_l = np.ascontiguousarray(nws.reshape(4, 16, 128).transpose(2, 0, 1))
    cw = f(conv_w)[0]
    convw_l = np.ascontiguousarray(cw.reshape(4, 48, 128).transpose(2, 1, 0))
    convb_l = np.ascontiguousarray(f(conv_b)[0].reshape(48, 128).T)
    dtb_l = np.ascontiguousarray(f(dt_bias)[0].reshape(8, 8).T)
    dsk_l = np.ascontiguousarray(np.repeat(f(d_skip)[0], 64).reshape(32, 128).T)
    snw_l = np.ascontiguousarray(f(ssm_norm)[0].reshape(32, 128).T)
    consts = _host_consts()
    ii = np.arange(128)[:, None]; jj = np.arange(128)[None, :]
    mprev = np.where(jj > ii, 0.0, NEG).astype(np.float32); mcur = np.where(jj <= ii, 0.0, NEG).astype(np.float32)
    m_std = np.concatenate([mprev, mcur], 1)
    m_none = np.concatenate([np.full((128, 128), NEG, np.float32), mcur], 1)
    shared = {"w_in": f(w_in)[0], "w_ab": f(w_attn_branch)[0], "w_sb": f(w_ssm_branch)[0], "w_o": f(w_out)[0],
              "w_gu": f(w_gate_up)[0], "w_dn": f(w_down)[0], "nw": nw_l, "convw": convw_l, "convb": convb_l,
              "dtb": dtb_l, "alog": f(a_log), "dsk": dsk_l, "snw": snw_l, "sinks": f(attn_sinks), "consts": consts}
    in_maps = []
    for c in range(8):
        b = c // 2; hf = c % 2
        xa = np.zeros((2048, D), np.float32)
        if hf == 1:
            xa[0:1024] = xp[b, 0:1024]
        xa[1024:2048] = xp[b, hf * 1024:(hf + 1) * 1024]
        cosT, sinT = _rope_tables(hf * 1024)
        m = dict(shared)
        m.update({"x_all": xa, "x_smp": np.ascontiguousarray(xs[c * NS:(c + 1) * NS, 0, :]),
                  "ck": np.ascontiguousarray(f(cache_win_k)[0, c * NS:(c + 1) * NS]), "cv": np.ascontiguousarray(f(cache_win_v)[0, c * NS:(c + 1) * NS]),
                  "sconv": np.ascontiguousarray(f(state_conv)[0, c * NS:(c + 1) * NS]), "sssm": np.ascontiguousarray(f(state_ssm)[0, c * NS:(c + 1) * NS]),
                  "cosT": cosT, "sinT": sinT,
                  "masks": np.ascontiguousarray(np.stack([m_std, m_std if hf == 1 else m_none], 1)),
                  "flag": np.full((128, 1), float(hf), np.float32)})
        in_maps.append(m)
    res = run_bass_kernel_spmd(nc, in_maps, core_ids=list(range(8))).results
    y_p = np.zeros((4, 2048, D), np.float32)
    for c in range(8):
        y_p[c // 2, (c % 2) * 1024:(c % 2 + 1) * 1024] = res[c]["y_own"]
    y_s = np.concatenate([res[c]["y_smp"] for c in range(8)], 0).reshape(128, 1, D)
    odd = [1, 3, 5, 7]
    wk = np.stack([res[c]["wk"].reshape(128, 8, 64) for c in odd], 0)[None]
    wv = np.stack([res[c]["wv"].reshape(128, 8, 64) for c in odd], 0)[None]
    cvp = np.stack([res[c]["convo"] for c in odd], 0)[None]
    ssp = np.stack([res[c]["ssmo"].reshape(64, 64, 128) for c in odd], 0)[None]
    wks = np.concatenate([res[c]["wks"].reshape(NS, 128, 8, 64) for c in range(8)], 0)[None]
    wvs = np.concatenate([res[c]["wvs"].reshape(NS, 128, 8, 64) for c in range(8)], 0)[None]
    cvs = np.concatenate([res[c]["convs"] for c in range(8)], 0)[None]
    sss = np.concatenate([res[c]["ssms"] for c in range(8)], 0)[None]
    return (y_p, y_s, wk, wv, cvp, ssp, wks, wvs, cvs, sss)
```

```python
import numpy as np
import concourse.bass as bass
import concourse.mybir as mybir

F32 = mybir.dt.float32
BF16 = mybir.dt.bfloat16
AF = mybir.ActivationFunctionType
ALU = mybir.AluOpType
AX = mybir.AxisListType

KQ = 12
STRICT_SAME_ENGINE = False


class _Op(object):
    __slots__ = ("eng", "fn", "reads", "writes", "dma", "deps", "sig", "ev", "qidx", "waits")


class Prog(object):
    ENGS = ("pe", "act", "dve", "pool", "sp")

    def __init__(self, nc):
        self.nc = nc
        self.ops = []
        self.nbar = 0

    def op(self, eng, fn, reads=(), writes=()):
        o = _Op()
        o.eng = eng; o.fn = fn; o.reads = tuple(reads); o.writes = tuple(writes)
        o.dma = False; o.sig = False; o.ev = None; o.qidx = -1
        self.ops.append(o)
        return o

    def dma(self, q, out, in_, reads=(), writes=()):
        o = _Op()
        o.eng = q
        o.fn = (lambda e, out=out, in_=in_: e.dma_start(out=out, in_=in_))
        o.reads = tuple(reads); o.writes = tuple(writes)
        o.dma = True; o.sig = True; o.ev = None; o.qidx = -1
        self.ops.append(o)
        return o

    def barrier(self, tiny):
        n = self.nbar
        self.nbar += 1
        dq = [("dq", q, i) for q in ("sp", "act", "pool") for i in range(KQ)]
        for e in self.ENGS:
            o = self.op(e, tiny[e], reads=(dq + ["scr"] if e == "sp" else ["scr"]), writes=[("bar1", n, e)])
            if e == "sp":
                o.dma = True; o.sig = True
        for e in self.ENGS:
            o = self.op(e, tiny[e], reads=[("bar1", n, e2) for e2 in self.ENGS], writes=[("bar2", n, e)])
            if e == "sp":
                o.dma = True; o.sig = True

    def finalize(self, stack):
        nc = self.nc
        ops = self.ops
        esem = {e: stack.enter_context(nc.semaphore("se_" + e)) for e in self.ENGS}
        dsem = {q: [stack.enter_context(nc.semaphore("sd_%s%d" % (q, i))) for i in range(KQ)]
                for q in ("sp", "act", "pool")}
        semobj = {}
        for e in self.ENGS:
            semobj[("e", e)] = esem[e]
        for q in dsem:
            for i in range(KQ):
                semobj[("d", q, i)] = dsem[q][i]

        last_w = {}
        readers = {}
        dq_hist = {"sp": [], "act": [], "pool": []}
        for j, op in enumerate(ops):
            deps = {}
            for k in op.reads:
                i = last_w.get(k)
                if i is not None:
                    deps[i] = True
            for k in op.writes:
                i = last_w.get(k)
                if i is not None:
                    o = ops[i]
                    if o.dma or op.dma or o.eng != op.eng or (STRICT_SAME_ENGINE and op.eng != "pe"):
                        deps[i] = True
                for i in readers.get(k, {}).values():
                    o = ops[i]
                    if o.dma or op.dma or o.eng != op.eng or (STRICT_SAME_ENGINE and op.eng != "pe"):
                        deps[i] = True
            if op.dma:
                h = dq_hist[op.eng]
                op.qidx = len(h)
                op.writes = op.writes + (("dq", op.eng, op.qidx % KQ),)
                if op.qidx >= KQ:
                    deps[h[op.qidx - KQ]] = True
                h.append(j)
            deps.pop(j, None)
            op.deps = sorted(deps, reverse=True)
            rid = (op.eng, op.qidx % KQ) if op.dma else op.eng
            for k in op.reads:
                readers.setdefault(k, {})[rid] = j
            for k in op.writes:
                last_w[k] = j
                readers[k] = {}
        for op in ops:
            for i in op.deps:
                ops[i].sig = True
        cnt = {e: 0 for e in self.ENGS}
        for op in ops:
            if op.dma:
                op.ev = (("d", op.eng, op.qidx % KQ), 16 * (op.qidx // KQ + 1))
            elif op.sig:
                cnt[op.eng] += 1
                op.ev = (("e", op.eng), cnt[op.eng])
        know = {e: {} for e in self.ENGS}
        snap = {}
        nw = 0
        for op in ops:
            kn = know[op.eng]
            waits = []
            for i in op.deps:
                sid, val = ops[i].ev
                if kn.get(sid, 0) >= val:
                    continue
                waits.append((sid, val))
                for s, v in snap[(sid, val)].items():
                    if kn.get(s, 0) < v:
                        kn[s] = v
            op.waits = waits
            nw += len(waits)
            if op.sig:
                s = dict(kn)
                s[op.ev[0]] = op.ev[1]
                snap[op.ev] = s
        self.stats = dict(n_ops=len(ops), n_waits=nw, cnt=dict(cnt),
                          ndma={q: len(h) for q, h in dq_hist.items()})
        final_waits = []
        for q, h in dq_hist.items():
            for j in h[-KQ:]:
                final_waits.append(ops[j].ev)

        block = stack.enter_context(nc.Block())

        def emit(e, eng, extra=None):
            for op in ops:
                if op.eng != eng:
                    continue
                for sid, val in op.waits:
                    e.wait_ge(semobj[sid], val)
                ins = op.fn(e)
                if op.sig:
                    if op.dma:
                        ins.then_inc(semobj[op.ev[0]], 16)
                    else:
                        ins.then_inc(semobj[op.ev[0]], 1)
            if extra:
                for sid, val in extra:
                    e.wait_ge(semobj[sid], val)

        @block.tensor
        def _(e):
            emit(e, "pe")

        @block.scalar
        def _(e):
            emit(e, "act")

        @block.vector
        def _(e):
            emit(e, "dve")

        @block.gpsimd
        def _(e):
            emit(e, "pool")

        @block.sync
        def _(e):
            emit(e, "sp", final_waits)


from contextlib import ExitStack
from concourse.bass_utils import run_bass_kernel_spmd

D = 2048; NH = 32; NKV = 8; HD = 64; DI = 4096; NSH = 64; NST = 128; NG = 8
CONVD = 6144; DFF = 5632; EPS = 1e-6
OQ = 0; OK_ = 2048; OV = 2560; OZ = 3072; OX = 7168; OB = 11264; OC = 12288; ODT = 13312; OGA = 13376; OGS = 15424
INP = 17472
TP = 512
NS = 16
NEG = -30000.0
SB_BYTES = 212480


def _prod(s):
    r = 1
    for v in s:
        r *= v
    return r


class Arena(object):
    def __init__(self, nc, stack, name, nbytes):
        self.t = stack.enter_context(nc.sbuf_tensor(name, [128, nbytes // 4], F32))
        self.ap = self.t[:]
        self.cap = nbytes
        self.top = 0

    def alloc(self, free_shape, dtype, parts=128):
        n = _prod(free_shape)
        nb = n * (4 if dtype == F32 else 2)
        nb = (nb + 63) // 64 * 64
        off = self.top
        self.top += nb
        assert self.top <= self.cap, ("SBUF arena overflow", self.top, self.cap)
        v = self.ap[:, off // 4:(off + nb) // 4]
        if dtype != F32:
            v = v.bitcast(dtype)
        v = v[:, 0:n]
        if len(free_shape) == 2:
            v = v.rearrange("p (a b) -> p a b", a=free_shape[0])
        elif len(free_shape) == 3:
            v = v.rearrange("p (a b c) -> p a b c", a=free_shape[0], b=free_shape[1])
        return v

    def mark(self):
        return self.top

    def release(self, m):
        self.top = m


def build_program(dbg=None, opts=()):
    nc = bass.Bass("TRN2", target_bir_lowering=False)
    st = ExitStack()
    P = Prog(nc)

    def din(name, shape):
        return nc.dram_tensor(name, list(shape), F32, kind="ExternalInput").ap()

    def dout(name, shape):
        return nc.dram_tensor(name, list(shape), F32, kind="ExternalOutput").ap()

    x_all = din("x_all", [2048, D])
    x_smp = din("x_smp", [NS, D])
    ck = din("ck", [NS, 128, NKV, HD]); cv = din("cv", [NS, 128, NKV, HD])
    sconv = din("sconv", [NS, 3, CONVD]); sssm = din("sssm", [NS, NSH, HD, NST])
    w_in = din("w_in", [D, INP]); w_ab = din("w_ab", [D, D]); w_sb = din("w_sb", [DI, D])
    w_o = din("w_o", [D, D]); w_gu = din("w_gu", [D, 2 * DFF]); w_dn = din("w_dn", [DFF, D])
    nw_in = din("nw", [128, 4, 16])
    convw_in = din("convw", [128, 48, 4]); convb_in = din("convb", [128, 48])
    dtb_in = din("dtb", [8, 8]); alog_in = din("alog", [1, 64]); dsk_in = din("dsk", [128, 32])
    snw_in = din("snw", [128, 32]); sink_in = din("sinks", [1, 32])
    cos_in = din("cosT", [128, 4, TP + NS]); sin_in = din("sinT", [128, 4, TP + NS])
    cst_in = din("consts", [128, 6, 128])
    msk_in = din("masks", [128, 2, 256])
    flag_in = din("flag", [128, 1])

    y_out = dout("y_own", [1024, D]); ys_out = dout("y_smp", [NS, D])
    wk_out = dout("wk", [128, 512]); wv_out = dout("wv", [128, 512])
    cv_out = dout("convo", [3, CONVD]); ss_out = dout("ssmo", [DI, NST])
    wks_out = dout("wks", [NS, 128, 512]); wvs_out = dout("wvs", [NS, 128, 512])
    cvs_out = dout("convs", [NS, 3, CONVD]); sss_out = dout("ssms", [NS, NSH, HD, NST])
    scr_x = nc.dram_tensor("scr_x", [NS, CONVD], F32).ap()
    scr_dt = nc.dram_tensor("scr_dt", [NS, 64], F32).ap()
    scr_y = nc.dram_tensor("scr_y", [NS, DI], F32).ap()
    scr_q = nc.dram_tensor("scr_q", [NS, D], F32).ap()
    scr_k = nc.dram_tensor("scr_k", [NS, 512], F32).ap()
    scr_v = nc.dram_tensor("scr_v", [NS, 512], F32).ap()
    scr_o = nc.dram_tensor("scr_o", [NS, D], F32).ap()
    dbg_out = {}
    if dbg:
        for k, shp in dbg.items():
            dbg_out[k] = dout("dbg_" + k, shp)

    A = Arena(nc, st, "arena", SB_BYTES)
    cst = A.alloc([6, 128], F32)
    ident = cst[:, 0, :]; triU = cst[:, 1, :]; ones_f = cst[:, 2, :]; rotR = cst[:, 3, :]; ssdmask = cst[:, 4, :]
    cst_b = A.alloc([6, 128], BF16)
    ident_b = cst_b[:, 0, :]; ones_b = cst_b[:, 2, :]; ssdmask_b = cst_b[:, 4, :]
    masks = A.alloc([2, 256], F32)
    nw = A.alloc([4, 16], F32)
    convw = A.alloc([48, 4], F32); convb = A.alloc([48], F32)
    dtb = A.alloc([8], F32); dsk = A.alloc([32], F32); snw = A.alloc([32], F32)
    negA = A.alloc([64], F32)
    sink8 = A.alloc([32], F32)
    flag = A.alloc([1], F32)
    cosT = A.alloc([544], F32); sinT = A.alloc([544], F32)
    scr = A.alloc([128], F32)
    kTc = A.alloc([4, 128], BF16); Vc = A.alloc([512], BF16)
    convc = A.alloc([48, 3], F32)
    hT = A.alloc([NG, 512], F32)
    NSLOT = 2
    wslots = [A.alloc([8192], BF16) for _ in range(NSLOT)]
    hnT = A.alloc([16, 544], BF16)
    base_mark = A.mark()

    ps = [st.enter_context(nc.psum_tensor("ps%d" % i, [128, 512], F32))[:] for i in range(8)]
    psn = [0]

    def PS():
        i = psn[0] % 8
        psn[0] += 1
        return ps[i], "ps%d" % i

    tiny = {
        "pe": lambda e: e.matmul(ps[7][0:1, 0:1], lhsT=cst_b[0:1, 0, 0:1], rhs=cst_b[0:1, 0, 0:1], start=True, stop=True),
        "act": lambda e: e.activation(out=scr[0:1, 0:1], in_=scr[0:1, 16:17], func=AF.Copy),
        "dve": lambda e: e.memset(scr[0:1, 32:33], 0.0),
        "pool": lambda e: e.memset(scr[0:1, 48:49], 0.0),
        "sp": lambda e: e.dma_start(out=scr[0:1, 64:72], in_=scr[0:1, 96:104]),
    }

    def barrier():
        P.barrier(tiny)

    P.dma("sp", cst, cst_in, writes=["cst"])
    P.dma("pool", cst_b, cst_in, writes=["cst_b"])
    P.dma("sp", masks, msk_in, writes=["masks"])
    P.dma("sp", nw, nw_in, writes=["nw"])
    P.dma("sp", convw, convw_in, writes=["convw"]); P.dma("sp", convb, convb_in, writes=["convb"])
    P.dma("sp", dtb[0:8, :], dtb_in, writes=["dtb"]); P.dma("sp", dsk, dsk_in, writes=["dsk"])
    P.dma("sp", snw, snw_in, writes=["snw"]); P.dma("sp", flag, flag_in, writes=["flag"])
    P.dma("sp", negA, alog_in.partition_broadcast(128), writes=["negA"])
    P.dma("sp", sink8, sink_in.partition_broadcast(128), writes=["sink8"])
    P.op("act", lambda e: e.activation(out=negA, in_=negA, func=AF.Exp), reads=["negA"], writes=["negA"])
    P.op("dve", lambda e: e.tensor_scalar(out=negA, in0=negA, scalar1=-1.0, scalar2=None, op0=ALU.mult), reads=["negA"], writes=["negA"])
    P.op("dve", lambda e: e.tensor_scalar(out=sink8, in0=sink8, scalar1=8.0, scalar2=None, op0=ALU.mult), reads=["sink8"], writes=["sink8"])
    P.op("dve", lambda e: e.memset(hT, 0.0), writes=["hT"])
    P.op("dve", lambda e: e.memset(scr, 0.0), writes=["scr"])
    P.op("dve", lambda e: e.memset(kTc, 0.0), writes=["kTc"])
    P.op("dve", lambda e: e.memset(Vc, 0.0), writes=["Vc"])
    P.op("dve", lambda e: e.memset(convc, 0.0), writes=["convc"])


    T = 544
    TASKS = []

    CUR = [TASKS]

    def task(loads, compute):
        CUR[0].append((loads, compute))

    pss_n = [0]

    def PSS():
        i = psn[0] % 7
        psn[0] += 1
        return ps[i][:, 0:16], "ps%d" % i

    def PS7():
        i = psn[0] % 7
        psn[0] += 1
        return ps[i], "ps%d" % i

    def ranges(Tn):
        return [(0, TP)] if Tn == TP else [(0, Tn // 2), (Tn // 2, Tn)]

    def op(eng, fn, reads, writes):
        return P.op(eng, fn, reads=reads, writes=writes)

    def ACT(out, in_, func, reads, writes, **kw):
        op("act", lambda e: e.activation(out=out, in_=in_, func=func, **kw), reads, writes)

    def TT(eng, out, in0, in1, o, reads, writes):
        op(eng, lambda e: e.tensor_tensor(out=out, in0=in0, in1=in1, op=o), reads, writes)

    def TS(eng, out, in0, s1, s2, o0, o1, reads, writes):
        if o1 is None:
            op(eng, lambda e: e.tensor_scalar(out=out, in0=in0, scalar1=s1, scalar2=None, op0=o0), reads, writes)
        else:
            op(eng, lambda e: e.tensor_scalar(out=out, in0=in0, scalar1=s1, scalar2=s2, op0=o0, op1=o1), reads, writes)

    def STT(eng, out, in0, sc, in1, o0, o1, reads, writes):
        op(eng, lambda e: e.scalar_tensor_tensor(out=out, in0=in0, scalar=sc, in1=in1, op0=o0, op1=o1), reads, writes)

    def MM(out, lhsT, rhs, start, stop, reads, writes):
        op("pe", lambda e: e.matmul(out, lhsT=lhsT, rhs=rhs, start=start, stop=stop), reads, writes)

    def TR(out, in_, idn, reads, writes):
        op("pe", lambda e: e.transpose(out, in_, idn), reads, writes)

    def CP(eng, out, in_, reads, writes):
        if eng == "act":
            ACT(out, in_, AF.Copy, reads, writes)
        else:
            op(eng, lambda e: e.tensor_copy(out=out, in_=in_), reads, writes)

    stg = A.alloc([8, 16], F32)
    stg_n = [0]

    def proj(M, lhsT_of, rhs_of, nk, Tn, rd):
        res = []
        for (c0, c1) in ranges(Tn):
            if c1 - c0 > 16:
                pt, kp = PS7()
            else:
                pt, kp = PSS()
            o = pt[0:M, 0:c1 - c0]
            for kc in range(nk):
                MM(o, lhsT_of(kc), rhs_of(kc, c0, c1), kc == 0, kc == nk - 1, rd, [kp])
            if c1 - c0 <= 16:
                i = stg_n[0] % 8
                stg_n[0] += 1
                so = stg[0:M, i, 0:c1 - c0]
                CP("act", so, o, [kp], ["stg%d" % i])
                res.append((so, "stg%d" % i, c0, c1))
            else:
                res.append((o, kp, c0, c1))
        return res

    def wview(slot, nk, cw):
        return slot[:, 0:nk * cw].rearrange("p (k c) -> p k c", k=nk)

    def wloads_plain(W, c0, cw, nk, col_off=0, tot=None, sfx_default="a"):
        tot = tot or cw
        def f(slot):
            v = wview(slot, nk, tot)
            src = W[:, c0:c0 + cw].rearrange("(k p) c -> p k c", p=128)
            out = []
            step = max(1, 4096 // (cw * 4) * 4) if cw * 4 < 1024 else 4
            step = min(nk, max(4, step))
            if cw >= 256 and col_off == 0 and tot == cw:
                hw = cw // 2
                for (ca, sfx) in ((0, "a"), (hw, "b")):
                    for k0 in range(0, nk, nk // 2):
                        out.append((v[:, k0:k0 + nk // 2, ca:ca + hw], src[:, k0:k0 + nk // 2, ca:ca + hw], sfx))
                return out
            for k0 in range(0, nk, step):
                k1 = min(nk, k0 + step)
                out.append((v[:, k0:k1, col_off:col_off + cw], src[:, k0:k1, :], sfx_default))
            return out
        return f

    def fm_rstd(src, ksrc, Tn, sq, rstd, nfeat):
        ACT(sq[:, :, 0:Tn], src[:, :, 0:Tn], AF.Square, [ksrc], ["sq"])
        res = proj(128, lambda kc: ones_b, lambda kc, c0, c1: sq[:, kc, c0:c1], 16, Tn, ["sq", "cst_b"])
        for (o, kp, c0, c1) in res:
            TS("dve", rstd[:, c0:c1], o, 1.0 / nfeat, EPS, ALU.mult, ALU.add, [kp], ["rstd"])
        ACT(rstd[:, 0:Tn], rstd[:, 0:Tn], AF.Ln, ["rstd"], ["rstd"])
        ACT(rstd[:, 0:Tn], rstd[:, 0:Tn], AF.Exp, ["rstd"], ["rstd"], scale=-0.5)

    def emit_pass(pi, full, x_row0, has_s, y_row0, first_block_mask, dbgtag=None, kv=True):
        Tn = TP + (NS if has_s else 0)
        last = (pi == 3)
        pm = A.mark()

        def s01(slot=None, sk=None):
            m = A.mark()
            xT = A.alloc([16, T], F32); sq = A.alloc([16, T], BF16); rstd = A.alloc([T], F32)
            xtok = [A.alloc([D], F32) for _ in range(2)]
            tiles = [(x_all[x_row0 + t * 128:x_row0 + (t + 1) * 128, :], 128, t * 128) for t in range(4)]
            if has_s:
                tiles.append((x_smp, NS, TP))
            P.dma("sp", cosT[:, 0:TP + NS], cos_in[:, pi, :], writes=["cosT"])
            P.dma("sp", sinT[:, 0:TP + NS], sin_in[:, pi, :], writes=["sinT"])
            for ti, (src, n, c0) in enumerate(tiles):
                xt = xtok[ti % 2]; kx = "xtok%d" % (ti % 2)
                P.dma("sp", xt[0:n, :], src, writes=[kx])
                for q in range(4):
                    pt, kp = PS7()
                    for jj in range(4):
                        c = q * 4 + jj
                        TR(pt[:, jj * 128:jj * 128 + n], xt[0:n, c * 128:(c + 1) * 128], ident[0:n, 0:n], [kx, "cst"], [kp])
                    CP("act", xT[:, q * 4:(q + 1) * 4, c0:c0 + n],
                       pt.rearrange("p (a b) -> p a b", a=4)[:, :, 0:n], [kp], ["xT"])
            fm_rstd(xT, "xT", Tn, sq, rstd, D)
            for kc in range(16):
                STT("dve", hnT[:, kc, 0:Tn], xT[:, kc, 0:Tn], nw[:, 0, kc:kc + 1], rstd[:, 0:Tn],
                    ALU.mult, ALU.mult, ["xT", "rstd", "nw"], ["hnT"])
            if dbgtag and "hnT" in dbg_out:
                hd = A.alloc([16, T], F32)
                CP("dve", hd, hnT, ["hnT"], ["hd"])
                P.dma("sp", dbg_out["hnT"], hd, reads=["hd"])
            barrier()
            A.release(m)
        task(None, s01)

        B = {}

        att_list = []; ssd_list = []
        CUR[0] = att_list

        def s2_alloc():
            barrier()
            A.release(B["m_ssd"])
            B["attnT"] = A.alloc([16, T], BF16)
            B["m_att"] = A.mark()
            B["qT"] = A.alloc([16, T], BF16)
            B["kT"] = A.alloc([4, 128 + T], BF16)
            B["kfl"] = A.alloc([4, T], F32)
            B["Vt"] = A.alloc([5, 512], BF16)
            B["Vlast"] = A.alloc([512], F32)
            B["vs_tok"] = A.alloc([512], F32)
            B["qsf"] = A.alloc([16, NS], F32)
            for nm in ("qf", "qr", "t1", "t2"):
                B[nm] = A.alloc([T], F32)
            CP("dve", B["kT"][:, :, 0:128], kTc, ["kTc"], ["kT"])
            CP("dve", B["Vt"][:, 0, :], Vc, ["Vc"], ["Vt0"])
        task(None, s2_alloc)

        def rope_chunk(res, kq):
            qf, qr, t1, t2 = B["qf"], B["qr"], B["t1"], B["t2"]
            for (o, kp, c0, c1) in res:
                CP("act", qf[:, c0:c1], o, [kp], ["qf"])
            if "no_rope" in opts:
                CP("dve", qr[:, 0:Tn], qf[:, 0:Tn], ["qf"], ["qr"])
                return
            for (c0, c1) in ranges(Tn):
                pt, kp2 = PS7() if c1 - c0 > 16 else PSS()
                o2 = pt[:, 0:c1 - c0]
                for a0 in range(c0, c1, 256):
                    a1 = min(c1, a0 + 256)
                    MM(o2[:, a0 - c0:a1 - c0], rotR, qf[:, a0:a1], True, True, ["qf", "cst"], [kp2])
                if c1 - c0 <= 16:
                    i = stg_n[0] % 8
                    stg_n[0] += 1
                    CP("act", stg[:, i, 0:c1 - c0], o2, [kp2], ["stg%d" % i])
                    TT("dve", t2[:, c0:c1], stg[:, i, 0:c1 - c0], sinT[:, c0:c1], ALU.mult, ["stg%d" % i, "sinT"], ["t2"])
                else:
                    TT("dve", t2[:, c0:c1], o2, sinT[:, c0:c1], ALU.mult, [kp2, "sinT"], ["t2"])
            TT("pool", t1[:, 0:Tn], qf[:, 0:Tn], cosT[:, 0:Tn], ALU.mult, ["qf", "cosT"], ["t1"])
            TT("pool", qr[:, 0:Tn], t1[:, 0:Tn], t2[:, 0:Tn], ALU.add, ["t1", "t2"], ["qr"])

        if full and "no_q" not in opts:
            for jb in range(4):
                def loads_q(slot, jb=jb):
                    v = wview(slot, 16, 512).rearrange("p k (i h d) -> p k i h d", i=4, h=2)
                    src = w_in[:, OQ + jb * 512:OQ + (jb + 1) * 512].rearrange("(k p) (h i d) -> p k i h d", p=128, h=2, i=4)
                    out = []
                    for i in range(4):
                        for h in range(2):
                            out.append((v[:, :, i, h, :], src[:, :, i, h, :], "a" if i < 2 else "b"))
                    return out

                def comp_q(slot, sk, jb=jb):
                    wv_ = wview(slot, 16, 512)
                    for i in range(4):
                        c = 4 * jb + i
                        res = proj(128, lambda kc: wv_[:, kc, i * 128:(i + 1) * 128], lambda kc, c0, c1: hnT[:, kc, c0:c1], 16, Tn, ["hnT", sk + ("a" if i < 2 else "b")])
                        rope_chunk(res, "q")
                        CP("act", B["qT"][:, c, 0:Tn], B["qr"][:, 0:Tn], ["qr"], ["qT"])
                        if has_s:
                            CP("dve", B["qsf"][:, c, :], B["qr"][:, TP:Tn], ["qr"], ["qsf"])
                task(loads_q, comp_q)

        def comp_k(slot, sk):
            wv_ = wview(slot, 16, 512)
            for c in range(4):
                res = proj(128, lambda kc: wv_[:, kc, c * 128:(c + 1) * 128], lambda kc, c0, c1: hnT[:, kc, c0:c1], 16, Tn, ["hnT", sk + ("a" if c < 2 else "b")])
                rope_chunk(res, "k")
                CP("act", B["kT"][:, c, 128:128 + Tn], B["qr"][:, 0:Tn], ["qr"], ["kT"])
                CP("dve", B["kfl"][:, c, 0:Tn], B["qr"][:, 0:Tn], ["qr"], ["kfl"])
        if "no_k" not in opts:
            task(wloads_plain(w_in, OK_, 512, 16), comp_k)

        def comp_v(slot, sk):
            wv_ = wview(slot, 16, 512)
            for tt in range(4):
                pt, kp = PS7()
                for kc in range(16):
                    MM(pt, hnT[:, kc, tt * 128:(tt + 1) * 128], wv_[:, kc, :], kc == 0, kc == 15, ["hnT", sk + "a", sk + "b"], [kp])
                CP("act", B["Vt"][:, 1 + tt, :], pt, [kp], ["Vt%d" % (1 + tt)])
                if tt == 3 and "no_vlast" not in opts:
                    CP("act", B["Vlast"], pt, [kp], ["Vlast"])
            if has_s:
                pt, kp = PS7()
                for kc in range(16):
                    MM(pt[0:NS, :], hnT[:, kc, TP:Tn], wv_[:, kc, :], kc == 0, kc == 15, ["hnT", sk + "a", sk + "b"], [kp])
                CP("act", B["vs_tok"][0:NS, :], pt[0:NS, :], [kp], ["vs_tok"])
        if "no_v" not in opts:
            task(wloads_plain(w_in, OV, 512, 16), comp_v)

        def s3(slot=None, sk=None):
            m = A.mark()
            s_sb = A.alloc([2, 256], F32); Pb = A.alloc([2, 256], BF16); PT = A.alloc([2, 2, 128], BF16)
            rmx = A.alloc([2], F32); ngm = A.alloc([2], F32); rsm = A.alloc([2], F32); es = A.alloc([2], F32)
            atok = A.alloc([D], BF16)
            qT, kT, Vt = B["qT"], B["kT"], B["Vt"]
            for blk in range(4):
                mk = masks[:, 1, :] if (blk == 0 and first_block_mask) else masks[:, 0, :]
                for hp in range(16):
                    pS, kS = PS7()
                    hs = (2 * hp, 2 * hp + 1)
                    for u, h in enumerate(hs):
                        kvh = h // 4; g = h % 4; jj = kvh // 2; half = kvh % 2
                        cq = 4 * jj + g
                        MM(pS[:, u * 256:(u + 1) * 256], qT[half * 64:(half + 1) * 64, cq, blk * 128:(blk + 1) * 128],
                           kT[half * 64:(half + 1) * 64, jj, blk * 128:blk * 128 + 256], True, True, ["qT", "kT"], [kS])
                    TT("dve", s_sb, pS.rearrange("p (a b) -> p a b", a=2), mk.unsqueeze(1).to_broadcast([128, 2, 256]), ALU.add,
                       [kS, "masks"], ["s_sb"])
                    op("dve", lambda e: e.tensor_reduce(out=rmx, in_=s_sb, axis=AX.X, op=ALU.max), ["s_sb"], ["rmx"])
                    TT("dve", rmx, rmx, sink8[:, 2 * hp:2 * hp + 2], ALU.max, ["rmx", "sink8"], ["rmx"])
                    TS("dve", ngm, rmx, -0.125, None, ALU.mult, None, ["rmx"], ["ngm"])
                    for u in range(2):
                        ACT(Pb[:, u, :], s_sb[:, u, :], AF.Exp, ["s_sb", "ngm"], ["Pb", "rsm"], scale=0.125, bias=ngm[:, u:u + 1], accum_out=rsm[:, u:u + 1])
                    TT("dve", es, sink8[:, 2 * hp:2 * hp + 2], rmx, ALU.subtract, ["rmx", "sink8"], ["es"])
                    ACT(es, es, AF.Exp, ["es"], ["es"], scale=0.125)
                    TT("dve", es, es, rsm, ALU.add, ["es", "rsm"], ["es"])
                    op("dve", lambda e: e.reciprocal(out=es, in_=es), ["es"], ["es"])
                    pT, kT_ = PS7()
                    pTb = pT.bitcast(BF16)
                    for u in range(2):
                        for kb in range(2):
                            TR(pTb[:, (u * 2 + kb) * 128:(u * 2 + kb + 1) * 128], Pb[:, u, kb * 128:(kb + 1) * 128], ident_b, ["Pb", "cst_b"], [kT_])
                    CP("act", PT, pTb[:, 0:512].rearrange("p (a b c) -> p a b c", a=2, b=2), [kT_], ["PT"])
                    pO, kO = PS7()
                    for u, h in enumerate(hs):
                        kvh = h // 4
                        for kb in range(2):
                            MM(pO[:, u * 64:(u + 1) * 64], PT[:, u, kb, :], Vt[:, blk + kb, kvh * 64:(kvh + 1) * 64], kb == 0, kb == 1,
                               ["PT", "Vt%d" % (blk + kb)], [kO])
                    for u, h in enumerate(hs):
                        ACT(atok[:, h * 64:(h + 1) * 64], pO[:, u * 64:(u + 1) * 64], AF.Copy, [kO, "es"], ["atok"], scale=es[:, u:u + 1])
                for q2 in range(2):
                    pT, kT_ = PS7()
                    pTb = pT.bitcast(BF16)
                    for c8 in range(8):
                        c = q2 * 8 + c8
                        TR(pTb[:, c8 * 128:(c8 + 1) * 128], atok[:, c * 128:(c + 1) * 128], ident_b, ["atok", "cst_b"], [kT_])
                    CP("act", B["attnT"][:, q2 * 8:(q2 + 1) * 8, blk * 128:(blk + 1) * 128], pTb.rearrange("p (a b) -> p a b", a=8), [kT_], ["attnT"])
            A.release(m)
            if dbgtag and "attnT" in dbg_out:
                hd = A.alloc([8, T], F32)
                for hh in range(2):
                    CP("dve", hd, B["attnT"][:, hh * 8:(hh + 1) * 8, :], ["attnT"], ["hd"])
                    P.dma("sp", dbg_out["attnT"][:, hh * 8:(hh + 1) * 8, :], hd[:, :, 0:TP + NS], reads=["hd"], writes=["hdo"])
        if full and "no_s3" not in opts:
            task(None, s3)

        def s3_carry(slot=None, sk=None):
            CP("dve", kTc, B["kT"][:, :, TP:TP + 128], ["kT"], ["kTc"])
            CP("dve", Vc, B["Vt"][:, 4, :], ["Vt4"], ["Vc"])
            if last:
                P.dma("sp", wv_out, B["Vlast"], reads=["Vlast"], writes=["wvo"])
                pt, kp = PS7()
                for c in range(4):
                    TR(pt[:, c * 128:(c + 1) * 128], B["kfl"][:, c, TP - 128:TP], ident, ["kfl", "cst"], [kp])
                CP("act", B["qf"][:, 0:512], pt, [kp], ["qf"])
                P.dma("sp", wk_out, B["qf"][:, 0:512], reads=["qf"], writes=["wko"])
        task(None, s3_carry)

        def smp_attn(slot=None, sk=None):
            qtok = A.alloc([D], F32); ktok = A.alloc([512], F32)
            for q4 in range(4):
                pt, kp = PS7()
                for jj in range(4):
                    TR(pt[0:NS, jj * 128:(jj + 1) * 128], B["qsf"][:, q4 * 4 + jj, :], ident, ["qsf", "cst"], [kp])
                for jj in range(4):
                    CP("act", qtok[0:NS, (8 * q4 + jj) * 64:(8 * q4 + jj + 1) * 64], pt[0:NS, jj * 128:jj * 128 + 64], [kp], ["qtok"])
                    CP("act", qtok[0:NS, (8 * q4 + 4 + jj) * 64:(8 * q4 + 5 + jj) * 64], pt[0:NS, jj * 128 + 64:(jj + 1) * 128], [kp], ["qtok"])
            pt, kp = PS7()
            for c in range(4):
                TR(pt[0:NS, c * 128:(c + 1) * 128], B["kfl"][:, c, TP:Tn], ident, ["kfl", "cst"], [kp])
            CP("act", ktok[0:NS, :], pt[0:NS, :], [kp], ["ktok"])
            P.dma("sp", scr_q, qtok[0:NS, :], reads=["qtok"], writes=["scr_q"])
            P.dma("sp", scr_k, ktok[0:NS, :], reads=["ktok"], writes=["scr_k"])
            P.dma("sp", scr_v, B["vs_tok"][0:NS, :], reads=["vs_tok"], writes=["scr_v"])
            P.dma("sp", wks_out[:, 127, :], ktok[0:NS, :], reads=["ktok"], writes=["wks1"])
            P.dma("sp", wvs_out[:, 127, :], B["vs_tok"][0:NS, :], reads=["vs_tok"], writes=["wvs1"])
            P.dma("sp", wks_out[:, 0:127, :], ck[:, 1:128, :, :].rearrange("b s k d -> b s (k d)"), writes=["wks0"])
            P.dma("sp", wvs_out[:, 0:127, :], cv[:, 1:128, :, :].rearrange("b s k d -> b s (k d)"), writes=["wvs0"])
            barrier()
            A.release(B["m_att"])
            qb = A.alloc([256], F32); kb_ = A.alloc([64], F32); vb = A.alloc([64], F32); sk8 = A.alloc([4], F32)
            P.dma("sp", qb, scr_q.rearrange("b (k f) -> (b k) f", k=8), reads=["scr_q"], writes=["qb"])
            P.dma("sp", kb_, scr_k.rearrange("b (k f) -> (b k) f", k=8), reads=["scr_k"], writes=["kb_"])
            P.dma("sp", vb, scr_v.rearrange("b (k f) -> (b k) f", k=8), reads=["scr_v"], writes=["vb"])
            for b in range(NS):
                P.dma("pool", sk8[b * 8:(b + 1) * 8, :], sink_in[0:1, :].rearrange("o (k g) -> (o k) g", g=4), writes=["sk8"])
            TS("dve", sk8, sk8, 8.0, None, ALU.mult, None, ["sk8"], ["sk8"])
            cbuf = [A.alloc([64, 64], F32) for _ in range(2)]
            prod = A.alloc([64, 64], F32)
            sc = A.alloc([4, 128], F32); pp = A.alloc([4, 128], F32)
            rmx = A.alloc([4], F32); ngm = A.alloc([4], F32); rsm = A.alloc([4], F32); es = A.alloc([4], F32)
            oacc = A.alloc([4, 64], F32); opart = A.alloc([4, 64], F32)
            for hf in range(2):
                cb_ = cbuf[hf]; kc_ = "cbuf%d" % hf
                for b in range(NS):
                    P.dma("sp", cb_[b * 8:(b + 1) * 8, :, :], ck[b, hf * 64:(hf + 1) * 64, :, :].rearrange("s k d -> k s d"), writes=[kc_])
                if hf == 0:
                    CP("dve", cb_[:, 0, :], kb_, ["kb_", kc_], [kc_])
                for g in range(4):
                    TT("pool" if g % 2 == 0 else "dve", prod, cb_, qb[:, g * 64:(g + 1) * 64].unsqueeze(1).to_broadcast([128, 64, 64]), ALU.mult, [kc_, "qb"], ["sprod"])
                    op("dve", lambda e, g=g, hf=hf: e.tensor_reduce(out=sc[:, g, hf * 64:(hf + 1) * 64], in_=prod, axis=AX.X, op=ALU.add), ["sprod"], ["ssc"])
            op("dve", lambda e: e.tensor_reduce(out=rmx, in_=sc, axis=AX.X, op=ALU.max), ["ssc"], ["srmx"])
            TT("dve", rmx, rmx, sk8, ALU.max, ["srmx", "sk8"], ["srmx"])
            TS("dve", ngm, rmx, -0.125, None, ALU.mult, None, ["srmx"], ["sngm"])
            for g in range(4):
                ACT(pp[:, g, :], sc[:, g, :], AF.Exp, ["ssc", "sngm"], ["spp", "srsm"], scale=0.125, bias=ngm[:, g:g + 1], accum_out=rsm[:, g:g + 1])
            TT("dve", es, sk8, rmx, ALU.subtract, ["sk8", "srmx"], ["ses"])
            ACT(es, es, AF.Exp, ["ses"], ["ses"], scale=0.125)
            TT("dve", es, es, rsm, ALU.add, ["ses", "srsm"], ["ses"])
            op("dve", lambda e: e.reciprocal(out=es, in_=es), ["ses"], ["ses"])
            for hf in range(2):
                cb_ = cbuf[hf]; kc_ = "cbuf%d" % hf
                for b in range(NS):
                    P.dma("act", cb_[b * 8:(b + 1) * 8, :, :], cv[b, hf * 64:(hf + 1) * 64, :, :].rearrange("s k d -> k s d"), writes=[kc_])
                if hf == 0:
                    CP("dve", cb_[:, 0, :], vb, ["vb", kc_], [kc_])
                for g in range(4):
                    TT("pool" if g % 2 == 0 else "dve", prod, cb_, pp[:, g, hf * 64:(hf + 1) * 64].unsqueeze(2).to_broadcast([128, 64, 64]), ALU.mult, [kc_, "spp"], ["sprod"])
                    dst = oacc if hf == 0 else opart
                    op("dve", lambda e, g=g, dst=dst: e.tensor_reduce(out=dst[:, g, :], in_=prod.rearrange("p s d -> p d s"), axis=AX.X, op=ALU.add),
                       ["sprod"], ["soacc" if hf == 0 else "sopart"])
            TT("dve", oacc, oacc, opart, ALU.add, ["soacc", "sopart"], ["soacc"])
            TT("dve", oacc, oacc, es.unsqueeze(2).to_broadcast([128, 4, 64]), ALU.mult, ["soacc", "ses"], ["soacc"])
            P.dma("sp", scr_o.rearrange("b (k f) -> (b k) f", k=8), oacc.rearrange("p g d -> p (g d)"), reads=["soacc"], writes=["scr_o"])
            otok = A.alloc([D], F32)
            P.dma("sp", otok[0:NS, :], scr_o, reads=["scr_o"], writes=["otok"])
            pt, kp = PS7()
            ptb = pt
            for c in range(16):
                TR(pt[:, c * NS:(c + 1) * NS], otok[0:NS, c * 128:(c + 1) * 128], ident[0:NS, 0:NS], ["otok", "cst"], [kp])
            CP("act", B["attnT"][:, :, TP:Tn], pt[:, 0:16 * NS].rearrange("p (a b) -> p a b", a=16), [kp], ["attnT"])
        if has_s and full and "no_smp_attn" not in opts:
            task(None, smp_attn)

        CUR[0] = ssd_list

        def s4_alloc(slot=None, sk=None):
            B["m_pass"] = A.mark()
            B["ynT"] = A.alloc([32, T], BF16)
            if has_s:
                B["xpre_s"] = A.alloc([48, NS], F32); B["zs_s"] = A.alloc([32, NS], F32); B["dtT_s"] = A.alloc([8, NS], F32)
            B["m_ssd"] = A.mark()
            for nm in ("zs", "xs", "yT"):
                B[nm] = A.alloc([4, T], F32)
            B["xpre"] = A.alloc([4, 3 + T], F32)
            B["bcpre"] = A.alloc([2, 3 + T], F32)
            B["bcs"] = A.alloc([2, T], F32)
            B["BCT"] = A.alloc([2, T], BF16)
            B["dtT"] = A.alloc([T], F32)
            B["gsq"] = A.alloc([4, T], BF16)
            B["rstd2"] = A.alloc([T], F32)
            B["acc"] = A.alloc([T], F32)
            B["Xpad"] = [A.alloc([8, 128], BF16) for _ in range(2)]
            B["Xd"] = [A.alloc([512], BF16) for _ in range(2)]; B["Btok"] = [A.alloc([128], BF16) for _ in range(2)]
            for nm in ("dtk", "dA", "acum", "tot", "dec", "cd", "ndA", "nacum"):
                B[nm] = [A.alloc([8], F32) for _ in range(2)]
            B["dAtri"] = A.alloc([8, 128], F32)
            B["L"] = A.alloc([8, 128], F32); B["MT"] = A.alloc([8, 128], BF16)
            B["cbT"] = A.alloc([128], F32); B["Ebc"] = A.alloc([4, 128], F32)
            B["hTb"] = A.alloc([512], BF16); B["ytmp"] = A.alloc([4, 128], F32); B["htmp"] = A.alloc([512], F32)
            op("pool", lambda e: e.memset(B["Xpad"][0], 0.0), [], ["Xpad0"])
            op("pool", lambda e: e.memset(B["Xpad"][1], 0.0), [], ["Xpad1"])
        task(None, s4_alloc)

        def conv_silu(pre, c_in, ch, dst, kpre, kdst):
            acc = B["acc"]
            ACT(acc[:, 0:TP], pre[:, 0:TP], AF.Copy, [kpre, "convw"], ["acc"], scale=convw[:, ch, 0:1])
            for k in range(1, 4):
                STT("dve", acc[:, 0:TP], pre[:, k:k + TP], convw[:, ch, k:k + 1], acc[:, 0:TP], ALU.mult, ALU.add, [kpre, "acc", "convw"], ["acc"])
            ACT(dst[:, 0:TP], acc[:, 0:TP], AF.Silu, ["acc", "convb"], [kdst], bias=convb[:, ch:ch + 1])

        for g in range(NG):
            if full:
                def comp_z(slot, sk, g=g):
                    wv_ = wview(slot, 16, 512)
                    for i in range(4):
                        res = proj(128, lambda kc: wv_[:, kc, i * 128:(i + 1) * 128], lambda kc, c0, c1: hnT[:, kc, c0:c1], 16, Tn, ["hnT", sk + ("a" if i < 2 else "b")])
                        for (o, kp, c0, c1) in res:
                            ACT(B["zs"][:, i, c0:c1], o, AF.Silu, [kp], ["zs"])
                        if has_s:
                            CP("pool", B["zs_s"][:, 4 * g + i, :], B["zs"][:, i, TP:Tn], ["zs"], ["zs_s"])
                task(wloads_plain(w_in, OZ + g * 512, 512, 16), comp_z)

            def comp_x(slot, sk, g=g):
                wv_ = wview(slot, 16, 512)
                xpre = B["xpre"]
                CP("dve", xpre[:, :, 0:3], convc[:, 4 * g:4 * g + 4, :], ["convc"], ["xpre"])
                for i in range(4):
                    res = proj(128, lambda kc: wv_[:, kc, i * 128:(i + 1) * 128], lambda kc, c0, c1: hnT[:, kc, c0:c1], 16, Tn, ["hnT", sk + ("a" if i < 2 else "b")])
                    for (o, kp, c0, c1) in res:
                        CP("act", xpre[:, i, 3 + c0:3 + c1], o, [kp], ["xpre"])
                for i in range(4):
                    conv_silu(xpre[:, i, :], None, 4 * g + i, B["xs"][:, i, :], "xpre", "xs")
                CP("dve", convc[:, 4 * g:4 * g + 4, :], xpre[:, :, TP:TP + 3], ["xpre"], ["convc"])
                if has_s:
                    CP("pool", B["xpre_s"][:, 4 * g:4 * g + 4, :], xpre[:, :, 3 + TP:3 + Tn], ["xpre"], ["xpre_s"])
            task(wloads_plain(w_in, OX + g * 512, 512, 16), comp_x)

            def loads_bcdt(slot, g=g):
                f1 = wloads_plain(w_in, OB + g * 128, 128, 16, 0, 264, "a")(slot)
                f2 = wloads_plain(w_in, OC + g * 128, 128, 16, 128, 264, "b")(slot)
                f3 = wloads_plain(w_in, ODT + g * 8, 8, 16, 256, 264, "b")(slot)
                return f1 + f2 + f3

            def comp_ssd(slot, sk, g=g):
                wv_ = wview(slot, 16, 264)
                bcpre, bcs, BCT, dtT = B["bcpre"], B["bcs"], B["BCT"], B["dtT"]
                CP("dve", bcpre[:, 0, 0:3], convc[:, 32 + g, :], ["convc"], ["bcpre"])
                CP("dve", bcpre[:, 1, 0:3], convc[:, 40 + g, :], ["convc"], ["bcpre"])
                for i in range(2):
                    if i == 1 and not full and not kv:
                        continue
                    res = proj(128, lambda kc: wv_[:, kc, i * 128:(i + 1) * 128], lambda kc, c0, c1: hnT[:, kc, c0:c1], 16, Tn, ["hnT", sk + ("a" if i == 0 else "b")])
                    for (o, kp, c0, c1) in res:
                        CP("act", bcpre[:, i, 3 + c0:3 + c1], o, [kp], ["bcpre"])
                res = proj(8, lambda kc: wv_[:, kc, 256:264], lambda kc, c0, c1: hnT[:, kc, c0:c1], 16, Tn, ["hnT", sk + "b"])
                for (o, kp, c0, c1) in res:
                    ACT(dtT[0:8, c0:c1], o, AF.Exp, [kp, "dtb"], ["dtT"], bias=dtb[0:8, g:g + 1])
                ACT(dtT[0:8, 0:Tn], dtT[0:8, 0:Tn], AF.Ln, ["dtT"], ["dtT"], bias=1.0)
                for i in range(2 if full else 1):
                    conv_silu(bcpre[:, i, :], None, (32 if i == 0 else 40) + g, bcs[:, i, :], "bcpre", "bcs")
                CP("dve", convc[:, 32 + g, :], bcpre[:, 0, TP:TP + 3], ["bcpre"], ["convc"])
                if full or kv:
                    CP("dve", convc[:, 40 + g, :], bcpre[:, 1, TP:TP + 3], ["bcpre"], ["convc"])
                if full:
                    CP("act", BCT[:, :, 0:TP], bcs[:, :, 0:TP], ["bcs"], ["BCT"])
                if has_s:
                    CP("pool", B["xpre_s"][:, 32 + g, :], bcpre[:, 0, 3 + TP:3 + Tn], ["bcpre"], ["xpre_s"])
                    CP("pool", B["xpre_s"][:, 40 + g, :], bcpre[:, 1, 3 + TP:3 + Tn], ["bcpre"], ["xpre_s"])
                    CP("pool", B["dtT_s"][0:8, g, :], dtT[0:8, TP:Tn], ["dtT"], ["dtT_s"])
                hTg = hT[:, g, :]
                L, MT, cbT, Ebc, hTb, ytmp, htmp, dAtri = [B[n] for n in ("L", "MT", "cbT", "Ebc", "hTb", "ytmp", "htmp", "dAtri")]
                for c in range(4):
                    cs = slice(c * 128, (c + 1) * 128)
                    pr_ = c % 2
                    dtk, dA, acum, tot, dec, cd, ndA, nacum = [B[n][pr_] for n in ("dtk", "dA", "acum", "tot", "dec", "cd", "ndA", "nacum")]
                    Xpad, Xd, Btok = B["Xpad"][pr_], B["Xd"][pr_], B["Btok"][pr_]
                    pt, kp = PS7()
                    TR(pt[:, 0:8], dtT[0:8, cs], ident[0:8, 0:8], ["dtT", "cst"], [kp])
                    CP("act", dtk, pt[:, 0:8], [kp], ["dtk%d" % pr_])
                    TT("dve", dA, dtk, negA[:, g * 8:(g + 1) * 8], ALU.mult, ["dtk%d" % pr_, "negA"], ["dA%d" % pr_])
                    pa, kpa = PS7()
                    MM(pa[:, 0:8], triU, dA, True, True, ["dA%d" % pr_, "cst"], [kpa])
                    MM(pa[:, 8:16], ones_f, dA, True, True, ["dA%d" % pr_, "cst"], [kpa])
                    CP("act", acum, pa[:, 0:8], [kpa], ["acum%d" % pr_])
                    CP("act", tot, pa[:, 8:16], [kpa], ["tot%d" % pr_])
                    TT("dve", dec, tot, acum, ALU.subtract, ["tot%d" % pr_, "acum%d" % pr_], ["dec%d" % pr_])
                    ACT(dec, dec, AF.Exp, ["dec%d" % pr_], ["dec%d" % pr_])
                    TT("dve", dec, dec, dtk, ALU.mult, ["dec%d" % pr_, "dtk%d" % pr_], ["dec%d" % pr_])
                    ACT(cd, tot, AF.Exp, ["tot%d" % pr_], ["cd%d" % pr_])
                    px, kpx = PS7()
                    for i in range(4):
                        TR(px[:, i * 128:(i + 1) * 128], B["xs"][:, i, cs], ident, ["xs", "cst"], [kpx])
                    px3 = px.rearrange("p (r d) -> p r d", r=8)
                    TT("dve", Xd.rearrange("p (r d) -> p r d", r=8), px3, dec.unsqueeze(2).to_broadcast([128, 8, 64]), ALU.mult, [kpx, "dec%d" % pr_], ["Xd%d" % pr_])
                    if full:
                        for par in range(2):
                            TT("dve", Xpad[:, par::2, par * 64:(par + 1) * 64], px3[:, par::2, :],
                               dtk[:, par::2].unsqueeze(2).to_broadcast([128, 4, 64]), ALU.mult, [kpx, "dtk%d" % pr_], ["Xpad%d" % pr_])
                    pb, kpb = PS7()
                    TR(pb[:, 0:128], bcs[:, 0, cs], ident, ["bcs", "cst"], [kpb])
                    CP("act", Btok, pb[:, 0:128], [kpb], ["Btok%d" % pr_])
                    if full:
                        TT("pool", dAtri, triU.unsqueeze(1).to_broadcast([128, 8, 128]), dA.unsqueeze(2).to_broadcast([128, 8, 128]), ALU.mult,
                           ["dA%d" % pr_, "cst"], ["dAtri"])
                        pA = []
                        for hf in range(2):
                            p_, k_ = PS7()
                            for q4 in range(2):
                                r0 = hf * 4 + q4 * 2
                                MM(p_[:, q4 * 256:(q4 + 1) * 256], ones_f, dAtri[:, r0:r0 + 2, :].rearrange("p a b -> p (a b)"), True, True, ["dAtri", "cst"], [k_])
                            pA.append((p_, k_))
                        TS("dve", nacum, acum, -1.0, None, ALU.mult, None, ["acum%d" % pr_], ["nacum%d" % pr_])
                        pc, kpc = PS7()
                        MM(pc[:, 0:128], BCT[:, 0, cs], BCT[:, 1, cs], True, True, ["BCT"], [kpc])
                        TT("dve", cbT, pc[:, 0:128], ssdmask, ALU.mult, [kpc, "cst"], ["cbT"])
                        for r in range(8):
                            p_, k_ = pA[r // 4]
                            a_ = p_[:, (r % 4) * 128:(r % 4 + 1) * 128]
                            STT("dve", L[:, r, :], a_, nacum[:, r:r + 1], ssdmask, ALU.add, ALU.mult, [k_, "nacum%d" % pr_, "cst"], ["L"])
                        ACT(L, L, AF.Exp, ["L"], ["L"])
                        TT("pool", MT, L, cbT.unsqueeze(1).to_broadcast([128, 8, 128]), ALU.mult, ["L", "cbT"], ["MT"])
                        for jx in range(4):
                            for par in range(2):
                                r = 2 * jx + par
                                p_, k_ = pA[r // 4]
                                CP("dve", Ebc[par * 64:(par + 1) * 64, jx, :], p_[par * 64:(par + 1) * 64, (r % 4) * 128:(r % 4 + 1) * 128], [k_], ["Ebc"])
                        ACT(Ebc, Ebc, AF.Exp, ["Ebc"], ["Ebc"])
                        CP("act", hTb, hTg, ["hT"], ["hTb"])
                        po, kpo = PS7()
                        for jx in range(4):
                            MM(po[:, jx * 128:(jx + 1) * 128], hTb[:, jx * 128:(jx + 1) * 128], BCT[:, 1, cs], True, True, ["hTb", "BCT"], [kpo])
                        TT("dve", ytmp, po.rearrange("p (a b) -> p a b", a=4), Ebc, ALU.mult, [kpo, "Ebc"], ["ytmp"])
                        pd, kpd = PS7()
                        for jx in range(4):
                            for par in range(2):
                                r = 2 * jx + par
                                MM(pd[:, jx * 128:(jx + 1) * 128], Xpad[:, r, :], MT[:, r, :], par == 0, par == 1, ["Xpad%d" % pr_, "MT"], [kpd])
                        TT("dve", ytmp, pd.rearrange("p (a b) -> p a b", a=4), ytmp, ALU.add, [kpd, "ytmp"], ["ytmp"])
                        for jx in range(4):
                            STT("dve", B["yT"][:, jx, cs], B["xs"][:, jx, cs], dsk[:, 4 * g + jx:4 * g + jx + 1], ytmp[:, jx, :], ALU.mult, ALU.add,
                                ["xs", "ytmp", "dsk"], ["yT"])
                    pst, kps = PS7()
                    MM(pst, Btok, Xd, True, True, ["Btok%d" % pr_, "Xd%d" % pr_], [kps])
                    TT("dve", htmp.rearrange("p (r d) -> p r d", r=8), hTg.rearrange("p (r d) -> p r d", r=8),
                       cd.unsqueeze(2).to_broadcast([128, 8, 64]), ALU.mult, ["hT", "cd%d" % pr_], ["htmp"])
                    TT("dve", hTg, htmp, pst, ALU.add, ["htmp", kps], ["hT"])
                if full:
                    yT, zs, gsq, rstd2 = B["yT"], B["zs"], B["gsq"], B["rstd2"]
                    TT("pool", yT[:, :, 0:TP], yT[:, :, 0:TP], zs[:, :, 0:TP], ALU.mult, ["yT", "zs"], ["yT"])
                    ACT(gsq[:, :, 0:TP], yT[:, :, 0:TP], AF.Square, ["yT"], ["gsq"])
                    res = proj(128, lambda kc: ones_b, lambda kc, c0, c1: gsq[:, kc, c0:c1], 4, TP, ["gsq", "cst_b"])
                    for (o, kp, c0, c1) in res:
                        TS("dve", rstd2[:, c0:c1], o, 1.0 / 512, EPS, ALU.mult, ALU.add, [kp], ["rstd2"])
                    ACT(rstd2[:, 0:TP], rstd2[:, 0:TP], AF.Ln, ["rstd2"], ["rstd2"])
                    ACT(rstd2[:, 0:TP], rstd2[:, 0:TP], AF.Exp, ["rstd2"], ["rstd2"], scale=-0.5)
                    for jx in range(4):
                        STT("dve", B["ynT"][:, 4 * g + jx, 0:TP], yT[:, jx, 0:TP], snw[:, 4 * g + jx:4 * g + jx + 1], rstd2[:, 0:TP], ALU.mult, ALU.mult,
                            ["yT", "rstd2", "snw"], ["ynT"])
            task(loads_bcdt, comp_ssd)


        def smp_ssm(slot=None, sk=None):
            barrier()
            A.release(B["m_ssd"])
            xpre_s, zs_s, dtT_s = B["xpre_s"], B["zs_s"], B["dtT_s"]
            ST = A.alloc([48, 48], F32)
            sct = A.alloc([1536], F32)
            sc2 = sconv.rearrange("b k c -> (b k) c")
            for q in range(4):
                P.dma("sp", sct[0:48, :], sc2[:, q * 1536:(q + 1) * 1536], writes=["sct"])
                for j4 in range(3):
                    pt, kp = PS7()
                    for jj in range(4):
                        lc = j4 * 4 + jj
                        TR(pt[:, jj * 48:(jj + 1) * 48], sct[0:48, lc * 128:(lc + 1) * 128], ident[0:48, 0:48], ["sct", "cst"], [kp])
                    CP("act", ST[:, q * 12 + j4 * 4:q * 12 + j4 * 4 + 4, :], pt[:, 0:192].rearrange("p (a b) -> p a b", a=4), [kp], ["ST"])
            STv = ST.rearrange("p c (b k) -> p c b k", k=3)
            acc = A.alloc([48, NS], F32); tmp = A.alloc([48, NS], F32); xc_s = A.alloc([48, NS], F32)
            TT("pool", acc, STv[:, :, :, 0], convw[:, :, 0:1].to_broadcast([128, 48, NS]), ALU.mult, ["ST", "convw"], ["sacc"])
            for k in (1, 2):
                TT("pool", tmp, STv[:, :, :, k], convw[:, :, k:k + 1].to_broadcast([128, 48, NS]), ALU.mult, ["ST", "convw"], ["stmp"])
                TT("pool", acc, acc, tmp, ALU.add, ["sacc", "stmp"], ["sacc"])
            TT("pool", tmp, xpre_s, convw[:, :, 3:4].to_broadcast([128, 48, NS]), ALU.mult, ["xpre_s", "convw"], ["stmp"])
            TT("pool", acc, acc, tmp, ALU.add, ["sacc", "stmp"], ["sacc"])
            TT("pool", acc, acc, convb.unsqueeze(2).to_broadcast([128, 48, NS]), ALU.add, ["sacc", "convb"], ["sacc"])
            ACT(xc_s, acc, AF.Silu, ["sacc"], ["xc_s"])
            if "ssm_stop1" in opts:
                return
            P.dma("sp", cvs_out[:, 0:2, :], sconv[:, 1:3, :], writes=["cvs01"])
            m_tok = A.mark()
            tok = A.alloc([CONVD], F32)
            for (srcT, ksrc, dst, kd) in ((xpre_s, "xpre_s", cvs_out[:, 2, :], "cvs2"), (xc_s, "xc_s", scr_x, "scr_x")):
                for q in range(12):
                    pt, kp = PS7()
                    for jj in range(4):
                        TR(pt[0:NS, jj * 128:(jj + 1) * 128], srcT[:, q * 4 + jj, :], ident, [ksrc, "cst"], [kp])
                    CP("act", tok[0:NS, q * 512:(q + 1) * 512], pt[0:NS, :], [kp], ["stok"])
                P.dma("sp", dst, tok[0:NS, :], reads=["stok"], writes=[kd])
            dtt = A.alloc([64], F32)
            pt, kp = PS7()
            for g in range(NG):
                TR(pt[0:NS, g * 8:(g + 1) * 8], dtT_s[0:8, g, :], ident[0:8, 0:8], ["dtT_s", "cst"], [kp])
            CP("act", dtt[0:NS, :], pt[0:NS, 0:64], [kp], ["dtt"])
            P.dma("sp", scr_dt, dtt[0:NS, :], reads=["dtt"], writes=["scr_dt"])
            if "ssm_stop2" in opts:
                return
            barrier()
            A.release(m_tok)
            Xg = A.alloc([64], F32); Bg = A.alloc([128], F32); Cg = A.alloc([128], F32)
            dtg = A.alloc([1], F32); ag = A.alloc([1], F32); da = A.alloc([1], F32)
            yg = [A.alloc([64], F32) for _ in range(2)]
            hb = [A.alloc([8, 128], F32) for _ in range(4)]
            tms = [A.alloc([8, 128], F32) for _ in range(2)]; prs = [A.alloc([8, 128], F32) for _ in range(2)]
            for g in range(NG):
                P.dma("pool", Xg, scr_x[:, g * 512:(g + 1) * 512].rearrange("b (r p) -> b r p", r=8), reads=["scr_x"], writes=["Xg"])
                P.dma("pool", Bg, scr_x[:, OB - OX + g * 128:OB - OX + (g + 1) * 128].unsqueeze(1).to_broadcast([NS, 8, 128]), reads=["scr_x"], writes=["Bg"])
                P.dma("pool", Cg, scr_x[:, OC - OX + g * 128:OC - OX + (g + 1) * 128].unsqueeze(1).to_broadcast([NS, 8, 128]), reads=["scr_x"], writes=["Cg"])
                P.dma("pool", dtg, scr_dt[:, g * 8:(g + 1) * 8].unsqueeze(2), reads=["scr_dt"], writes=["dtg"])
                P.dma("pool", ag, alog_in[:, g * 8:(g + 1) * 8].unsqueeze(2).to_broadcast([NS, 8, 1]), writes=["ag"])
                ACT(ag, ag, AF.Exp, ["ag"], ["ag"])
                TT("dve", da, dtg, ag, ALU.mult, ["dtg", "ag"], ["da"])
                ACT(da, da, AF.Exp, ["da"], ["da"], scale=-1.0)
                TS("dve", Xg, Xg, dtg[:, 0:1], None, ALU.mult, None, ["Xg", "dtg"], ["Xg"])
                y_ = yg[g % 2]; ky = "yg%d" % (g % 2)
                for pc in range(8):
                    h = hb[pc % 4]; kh = "hb%d" % (pc % 4)
                    tm = tms[pc % 2]; pr = prs[pc % 2]; ktm = "tm%d" % (pc % 2); kpr = "pr%d" % (pc % 2)
                    hsrc = sssm[:, g * 8:(g + 1) * 8, pc * 8:(pc + 1) * 8, :].rearrange("b r p n -> b r (p n)")
                    hdst = sss_out[:, g * 8:(g + 1) * 8, pc * 8:(pc + 1) * 8, :].rearrange("b r p n -> b r (p n)")
                    if "ssm_noload" not in opts:
                        P.dma("sp", h.rearrange("q a b -> q (a b)"), hsrc, writes=[kh])
                    if "ssm_nocomp" in opts:
                        if "ssm_nostore" not in opts:
                            P.dma("act", hdst, h.rearrange("q a b -> q (a b)"), reads=[kh], writes=["ssso"])
                        continue
                    TT("pool", tm, Xg[:, pc * 8:(pc + 1) * 8].unsqueeze(2).to_broadcast([128, 8, 128]), Bg.unsqueeze(1).to_broadcast([128, 8, 128]), ALU.mult,
                       ["Xg", "Bg"], [ktm])
                    STT("dve", h, h, da[:, 0:1], tm, ALU.mult, ALU.add, [kh, "da", ktm], [kh])
                    if "ssm_nostore" not in opts:
                        P.dma("act", hdst, h.rearrange("q a b -> q (a b)"), reads=[kh], writes=["ssso"])
                    TT("pool" if pc % 2 == 0 else "dve", pr, h, Cg.unsqueeze(1).to_broadcast([128, 8, 128]), ALU.mult, [kh, "Cg"], [kpr])
                    op("dve", lambda e, y_=y_, pc=pc, pr=pr: e.tensor_reduce(out=y_[:, pc * 8:(pc + 1) * 8], in_=pr, axis=AX.X, op=ALU.add), [kpr], [ky])
                P.dma("sp", scr_y[:, g * 512:(g + 1) * 512].rearrange("b (r p) -> b r p", r=8), y_, reads=[ky], writes=["scr_y"])
            if "ssm_stop3" in opts:
                return
            ytk = A.alloc([DI], F32)
            P.dma("sp", ytk[0:NS, :], scr_y, reads=["scr_y"], writes=["ytk"])
            yTs = A.alloc([32, NS], F32); gq = A.alloc([32, 32], BF16)[:, :, 0:NS]; rs = A.alloc([8, NS], F32)
            for q in range(8):
                pt, kp = PS7()
                for jj in range(4):
                    TR(pt[:, jj * NS:(jj + 1) * NS], ytk[0:NS, (q * 4 + jj) * 128:(q * 4 + jj + 1) * 128], ident[0:NS, 0:NS], ["ytk", "cst"], [kp])
                CP("act", yTs[:, q * 4:(q + 1) * 4, :], pt[:, 0:4 * NS].rearrange("p (a b) -> p a b", a=4), [kp], ["yTs"])
            if "ssm_stop4" in opts:
                return
            TT("pool", tmp[:, 0:32, :], xc_s[:, 0:32, :], dsk.unsqueeze(2).to_broadcast([128, 32, NS]), ALU.mult, ["xc_s", "dsk"], ["stmp"])
            TT("pool", yTs, yTs, tmp[:, 0:32, :], ALU.add, ["yTs", "stmp"], ["yTs"])
            TT("pool", yTs, yTs, zs_s, ALU.mult, ["yTs", "zs_s"], ["yTs"])
            if "ssm_stop5" in opts:
                return
            ACT(gq, yTs, AF.Square, ["yTs"], ["gq"])
            if "ssm_stop6" in opts:
                return
            for g in range(NG):
                pt_, kp = PS7()
                o = pt_[:, 0:NS]
                for jx in range(4):
                    MM(o, ones_b, gq[:, 4 * g + jx, :], jx == 0, jx == 3, ["gq", "cst_b"], [kp])
                ACT(rs[:, g, :], o, AF.Copy, [kp], ["rs"], scale=1.0 / 512)
            TS("dve", rs, rs, EPS, None, ALU.add, None, ["rs"], ["rs"])
            ACT(rs, rs, AF.Ln, ["rs"], ["rs"])
            ACT(rs, rs, AF.Exp, ["rs"], ["rs"], scale=-0.5)
            if "ssm_stop7" in opts:
                return
            for c in range(32):
                STT("dve", B["ynT"][:, c, TP:Tn], yTs[:, c, :], snw[:, c:c + 1], rs[:, c // 4, :], ALU.mult, ALU.mult, ["yTs", "rs", "snw"], ["ynT"])
        if has_s and "no_smp_ssm" not in opts:
            task(None, smp_ssm)

        CUR[0] = TASKS
        TASKS.extend(ssd_list)
        if full or kv:
            TASKS.extend(att_list)
        if not full:
            def p_end(slot=None, sk=None):
                barrier()
                A.release(B["m_pass"])
            task(None, p_end)
            return

        def s5_alloc(slot=None, sk=None):
            barrier()
            A.release(B["m_att"])
            B["mergedT"] = A.alloc([16, T], BF16)
            B["sg"] = A.alloc([2, T], F32)
            B["mt"] = A.alloc([2, T], F32)
        task(None, s5_alloc)

        for cb in range(4):
            st5 = {}

            def mk_acc(name, nk, cw, sub):
                def comp(slot, sk, cb=cb):
                    wv_ = wview(slot, nk, cw)
                    src = {"ab": B["attnT"], "sb": B["ynT"], "ga": hnT, "gs": hnT}[name[0:2]]
                    ksrc = {"ab": "attnT", "sb": "ynT", "ga": "hnT", "gs": "hnT"}[name[0:2]]
                    return wv_, src, ksrc
                return comp

            def comp_merge_blk(slots, cb=cb):
                pass

        for cb in range(4):
            def comp_ga(slot, sk, cb=cb):
                wv_ = wview(slot, 16, 512)
                for i in range(4):
                    res = proj(128, lambda kc: wv_[:, kc, i * 128:(i + 1) * 128], lambda kc, c0, c1: hnT[:, kc, c0:c1], 16, Tn, ["hnT", sk + ("a" if i < 2 else "b")])
                    for (o, kp, c0, c1) in res:
                        ACT(B["sgA"][:, i, c0:c1], o, AF.Sigmoid, [kp], ["sgA"])
            def comp_gs(slot, sk, cb=cb):
                wv_ = wview(slot, 16, 512)
                for i in range(4):
                    res = proj(128, lambda kc: wv_[:, kc, i * 128:(i + 1) * 128], lambda kc, c0, c1: hnT[:, kc, c0:c1], 16, Tn, ["hnT", sk + ("a" if i < 2 else "b")])
                    for (o, kp, c0, c1) in res:
                        ACT(B["sgS"][:, i, c0:c1], o, AF.Sigmoid, [kp], ["sgS"])
            def comp_ab(slot, sk, cb=cb):
                wv_ = wview(slot, 16, 512)
                for i in range(4):
                    res = proj(128, lambda kc: wv_[:, kc, i * 128:(i + 1) * 128], lambda kc, c0, c1: B["attnT"][:, kc, c0:c1], 16, Tn, ["attnT", sk + ("a" if i < 2 else "b")])
                    for (o, kp, c0, c1) in res:
                        TT("dve", B["mtmp"][:, i, c0:c1], o, B["sgA"][:, i, c0:c1], ALU.mult, [kp, "sgA"], ["mtmp"])
            def comp_sb(slot, sk, cb=cb, half=0):
                pass
            if cb == 0:
                def s5b(slot=None, sk=None):
                    B["sgA"] = A.alloc([4, T], F32); B["sgS"] = A.alloc([4, T], F32); B["mtmp"] = A.alloc([4, T], F32)
                task(None, s5b)
            task(wloads_plain(w_in, OGA + cb * 512, 512, 16), comp_ga)
            task(wloads_plain(w_in, OGS + cb * 512, 512, 16), comp_gs)
            task(wloads_plain(w_ab, cb * 512, 512, 16), comp_ab)
            for hf in range(2):
                def comp_sbh(slot, sk, cb=cb, hf=hf):
                    wv_ = wview(slot, 32, 256)
                    for i2 in range(2):
                        i = hf * 2 + i2
                        res = proj(128, lambda kc: wv_[:, kc, i2 * 128:(i2 + 1) * 128], lambda kc, c0, c1: B["ynT"][:, kc, c0:c1], 32, Tn, ["ynT", sk + ("a" if i2 == 0 else "b")])
                        for (o, kp, c0, c1) in res:
                            TT("dve", B["sgS"][:, i, c0:c1], o, B["sgS"][:, i, c0:c1], ALU.mult, [kp, "sgS"], ["sgS"])
                        TT("pool", B["mergedT"][:, cb * 4 + i, 0:Tn], B["sgS"][:, i, 0:Tn], B["mtmp"][:, i, 0:Tn], ALU.add, ["sgS", "mtmp"], ["mergedT"])
                task(wloads_plain(w_sb, cb * 512 + hf * 256, 256, 32), comp_sbh)

        if has_s and "smp" in dbg_out:
            def dbg_smp(slot=None, sk=None):
                d1 = A.alloc([16, NS], F32); d2 = A.alloc([32, NS], F32)
                CP("dve", d1, B["attnT"][:, :, TP:Tn], ["attnT"], ["d1"])
                CP("dve", d2, B["ynT"][:, :, TP:Tn], ["ynT"], ["d2"])
                P.dma("sp", dbg_out["smp"][:, 0:16, :], d1, reads=["d1"], writes=["dbgo1"])
                P.dma("sp", dbg_out["smp"][:, 16:48, :], d2, reads=["d2"], writes=["dbgo2"])
            task(None, dbg_smp)

        def s6_alloc(slot=None, sk=None):
            barrier()
            A.release(B["m_pass"])
            B["mixT"] = A.alloc([16, T], F32)
            B["xT"] = A.alloc([16, T], F32)
            B["rstd"] = A.alloc([T], F32)
            B["m_ffn"] = A.mark()
            CP("dve", hnT[:, :, 0:Tn], B["mergedT"][:, :, 0:Tn], ["mergedT"], ["hnT"])
            barrier()
        task(None, s6_alloc)

        def reload_x(slot=None, sk=None):
            xtok = B["xtok2"] = [A.alloc([D], F32) for _ in range(2)]
            tiles = [(x_all[x_row0 + t * 128:x_row0 + (t + 1) * 128, :], 128, t * 128) for t in range(4)]
            if has_s:
                tiles.append((x_smp, NS, TP))
            for ti, (src, n, c0) in enumerate(tiles):
                xt = xtok[ti % 2]; kx = "xtokb%d" % (ti % 2)
                P.dma("sp", xt[0:n, :], src, writes=[kx])
                for q in range(4):
                    pt, kp = PS7()
                    for jj in range(4):
                        c = q * 4 + jj
                        TR(pt[:, jj * 128:jj * 128 + n], xt[0:n, c * 128:(c + 1) * 128], ident[0:n, 0:n], [kx, "cst"], [kp])
                    CP("act", B["xT"][:, q * 4:(q + 1) * 4, c0:c0 + n], pt.rearrange("p (a b) -> p a b", a=4)[:, :, 0:n], [kp], ["xT2"])
        task(None, reload_x)

        for cb in range(4):
            def comp_o(slot, sk, cb=cb):
                wv_ = wview(slot, 16, 512)
                for i in range(4):
                    res = proj(128, lambda kc: wv_[:, kc, i * 128:(i + 1) * 128], lambda kc, c0, c1: hnT[:, kc, c0:c1], 16, Tn, ["hnT", sk + ("a" if i < 2 else "b")])
                    for (o, kp, c0, c1) in res:
                        CP("act", B["mixT"][:, cb * 4 + i, c0:c1], o, [kp], ["mixT"])
            task(wloads_plain(w_o, cb * 512, 512, 16), comp_o)

        def add_norm(widx, srcname, sqbuf):
            fm_rstd(B[srcname], srcname, Tn, sqbuf, B["rstd"], D)
            for kc in range(16):
                STT("dve", B[srcname][:, kc, 0:Tn], B[srcname][:, kc, 0:Tn], nw[:, widx, kc:kc + 1], B["rstd"][:, 0:Tn], ALU.mult, ALU.mult,
                    [srcname, "rstd", "nw"], [srcname])
            TT("pool", B["xT"][:, :, 0:Tn], B["xT"][:, :, 0:Tn], B[srcname][:, :, 0:Tn], ALU.add, ["xT2", srcname], ["xT2"])

        def s6b(slot=None, sk=None):
            barrier()
            A.release(B["m_ffn"])
            B["actT"] = A.alloc([44, T], BF16)
            sq = B["actT"][:, 0:16, :]
            add_norm(1, "mixT", sq)
            fm_rstd(B["xT"], "xT2", Tn, sq, B["rstd"], D)
            for kc in range(16):
                STT("dve", hnT[:, kc, 0:Tn], B["xT"][:, kc, 0:Tn], nw[:, 2, kc:kc + 1], B["rstd"][:, 0:Tn], ALU.mult, ALU.mult,
                    ["xT2", "rstd", "nw"], ["hnT"])
            barrier()
            B["sgu"] = A.alloc([4, T], F32)
        task(None, s6b)

        for fb in range(11):
            def comp_g(slot, sk, fb=fb):
                wv_ = wview(slot, 16, 512)
                B["pend"] = []
                for i in range(4):
                    res = proj(128, lambda kc: wv_[:, kc, i * 128:(i + 1) * 128], lambda kc, c0, c1: hnT[:, kc, c0:c1], 16, Tn, ["hnT", sk + ("a" if i < 2 else "b")])
                    for (o, kp, c0, c1) in res:
                        ACT(B["sgu"][:, i, c0:c1], o, AF.Silu, [kp], ["sgu"])
            def comp_u(slot, sk, fb=fb):
                wv_ = wview(slot, 16, 512)
                for i in range(4):
                    res = proj(128, lambda kc: wv_[:, kc, i * 128:(i + 1) * 128], lambda kc, c0, c1: hnT[:, kc, c0:c1], 16, Tn, ["hnT", sk + ("a" if i < 2 else "b")])
                    for (o, kp, c0, c1) in res:
                        TT("dve", B["actT"][:, fb * 4 + i, c0:c1], o, B["sgu"][:, i, c0:c1], ALU.mult, [kp, "sgu"], ["actT"])
            task(wloads_plain(w_gu, fb * 512, 512, 16), comp_g)
            task(wloads_plain(w_gu, DFF + fb * 512, 512, 16), comp_u)
        for cbk in range(16):
            def comp_d(slot, sk, cbk=cbk):
                wv_ = wview(slot, 44, 128)
                res = proj(128, lambda kc: wv_[:, kc, :], lambda kc, c0, c1: B["actT"][:, kc, c0:c1], 44, Tn, ["actT", sk + "a"])
                for (o, kp, c0, c1) in res:
                    CP("act", B["mixT"][:, cbk, c0:c1], o, [kp], ["mixT"])
            task(wloads_plain(w_dn, cbk * 128, 128, 44), comp_d)

        def s8(slot=None, sk=None):
            barrier()
            A.release(B["m_ffn"])
            sq = A.alloc([16, T], BF16)
            add_norm(3, "mixT", sq)
            ytok = [A.alloc([D], F32) for _ in range(2)]
            tiles = [(y_out[y_row0 + t * 128:y_row0 + (t + 1) * 128, :], 128, t * 128) for t in range(4)]
            if has_s:
                tiles.append((ys_out, NS, TP))
            for ti, (dst, n, c0) in enumerate(tiles):
                yt = ytok[ti % 2]; ky = "ytok%d" % (ti % 2)
                for q in range(4):
                    pt, kp = PS7()
                    for jj in range(4):
                        c = q * 4 + jj
                        TR(pt[0:n, jj * 128:(jj + 1) * 128], B["xT"][:, c, c0:c0 + n], ident, ["xT2", "cst"], [kp])
                    CP("act", yt[0:n, q * 512:(q + 1) * 512], pt[0:n, :], [kp], [ky])
                P.dma("sp", dst, yt[0:n, :], reads=[ky], writes=["yout"])
            barrier()
            A.release(pm)
        task(None, s8)

    if "only_p3" not in opts:
        emit_pass(0, False, 0, False, 0, False, kv=False)
        emit_pass(1, False, 512, False, 0, False)

    def apply_flag(slot=None, sk=None):
        TS("dve", hT, hT, flag[:, 0:1], None, ALU.mult, None, ["hT", "flag"], ["hT"])
    task(None, apply_flag)
    if "only_p3" not in opts:
        emit_pass(2, True, 1024, False, 0, True)
    emit_pass(3, True, 1536, "no_s" not in opts, 512, False)

    def final_out(slot=None, sk=None):
        m = A.mark()
        ctok = A.alloc([CONVD], F32)
        for q in range(12):
            pt, kp = PS7()
            for jj in range(4):
                c = q * 4 + jj
                TR(pt[0:3, jj * 128:(jj + 1) * 128], convc[:, c, :], ident, ["convc", "cst"], [kp])
            CP("act", ctok[0:3, q * 512:(q + 1) * 512], pt[0:3, :], [kp], ["ctok"])
        P.dma("sp", cv_out, ctok[0:3, :], reads=["ctok"], writes=["cvo"])
        hto = [A.alloc([512], F32) for _ in range(2)]
        for g in range(NG):
            pt, kp = PS7()
            for jx in range(4):
                TR(pt[:, jx * 128:(jx + 1) * 128], hT[:, g, jx * 128:(jx + 1) * 128], ident, ["hT", "cst"], [kp])
            CP("act", hto[g % 2], pt, [kp], ["hto%d" % (g % 2)])
            P.dma("sp", ss_out[g * 512:(g + 1) * 512, :].rearrange("(j p) n -> p j n", p=128),
                  hto[g % 2].rearrange("p (j n) -> p j n", j=4), reads=["hto%d" % (g % 2)], writes=["sso"])
        A.release(m)
    task(None, final_out)

    wt = [i for i, (l, c) in enumerate(TASKS) if l is not None]
    slot_of = {ti: (n % NSLOT) for n, ti in enumerate(wt)}

    def do_load(ti):
        sl = slot_of[ti]
        for (o, i_, sfx) in TASKS[ti][0](wslots[sl]):
            P.dma("pool", o, i_, writes=["ws%d%s" % (sl, sfx)])

    nxt = 0
    if wt:
        do_load(wt[0]); nxt = 1
    for ti, (l, c) in enumerate(TASKS):
        if l is not None:
            if nxt < len(wt):
                do_load(wt[nxt]); nxt += 1
            sl = slot_of[ti]
            c(wslots[sl], "ws%d" % sl)
        else:
            c()
    P.finalize(st)
    return nc, P


_CACHE = {}


def _host_consts():
    ident = np.eye(128, dtype=np.float32)
    tri = np.triu(np.ones((128, 128), np.float32))
    R = np.zeros((128, 128), np.float32)
    for m in range(128):
        if m % 64 < 32:
            R[m + 32, m] = -1.0
        else:
            R[m - 32, m] = 1.0
    return np.ascontiguousarray(np.stack([ident, tri, np.ones((128, 128), np.float32), R, tri, ident], 1))


def _rope_tables(s0):
    inv = (10000.0 ** (-np.arange(32, dtype=np.float32) / 32)).astype(np.float32)
    cosT = np.zeros((128, 4, TP + NS), np.float32)
    sinT = np.zeros((128, 4, TP + NS), np.float32)
    for pi in range(4):
        pos = (s0 - 1024 + pi * 512 + np.arange(512)).astype(np.float32)
        pos = np.concatenate([pos, np.full((NS,), 16384.0, np.float32)])
        ang = pos[None, :] * inv[:, None]
        c = np.cos(ang).astype(np.float32); sn = np.sin(ang).astype(np.float32)
        idx = np.arange(128) % 32
        cosT[:, pi, :] = c[idx]; sinT[:, pi, :] = sn[idx]
    return cosT, sinT


def kernel(x_prompt, x_sample, cache_win_k, cache_win_v, state_conv, state_ssm,
           norm_mix_pre, norm_mix_post, w_in, attn_sinks, w_attn_branch, conv_w, conv_b,
           dt_bias, a_log, d_skip, ssm_norm, w_ssm_branch, w_out,
           norm_ffn_pre, norm_ffn_post, w_gate_up, w_down):
    f = lambda a: np.ascontiguousarray(np.asarray(a, dtype=np.float32))
    if "nc" not in _CACHE:
        _CACHE["nc"] = build_program()[0]
    nc = _CACHE["nc"]
    xp = f(x_prompt); xs = f(x_sample)
    nws = np.stack([f(norm_mix_pre)[0], f(norm_mix_post)[0], f(norm_ffn_pre)[0], f(norm_ffn_post)[0]], 0)
    nw_l = np.ascontiguousarray(nws.reshape(4, 16, 128).transpose(2, 0, 1))
    cw = f(conv_w)[0]
    convw_l = np.ascontiguousarray(cw.reshape(4, 48, 128).transpose(2, 1, 0))
    convb_l = np.ascontiguousarray(f(conv_b)[0].reshape(48, 128).T)
    dtb_l = np.ascontiguousarray(f(dt_bias)[0].reshape(8, 8).T)
    dsk_l = np.ascontiguousarray(np.repeat(f(d_skip)[0], 64).reshape(32, 128).T)
    snw_l = np.ascontiguousarray(f(ssm_norm)[0].reshape(32, 128).T)
    consts = _host_consts()
    ii = np.arange(128)[:, None]; jj = np.arange(128)[None, :]
    mprev = np.where(jj > ii, 0.0, NEG).astype(np.float32); mcur = np.where(jj <= ii, 0.0, NEG).astype(np.float32)
    m_std = np.concatenate([mprev, mcur], 1)
    m_none = np.concatenate([np.full((128, 128), NEG, np.float32), mcur], 1)
    shared = {"w_in": f(w_in)[0], "w_ab": f(w_attn_branch)[0], "w_sb": f(w_ssm_branch)[0], "w_o": f(w_out)[0],
              "w_gu": f(w_gate_up)[0], "w_dn": f(w_down)[0], "nw": nw_l, "convw": convw_l, "convb": convb_l,
              "dtb": dtb_l, "alog": f(a_log), "dsk": dsk_l, "snw": snw_l, "sinks": f(attn_sinks), "consts": consts}
    in_maps = []
    for c in range(8):
        b = c // 2; hf = c % 2
        xa = np.zeros((2048, D), np.float32)
        if hf == 1:
            xa[0:1024] = xp[b, 0:1024]
        xa[1024:2048] = xp[b, hf * 1024:(hf + 1) * 1024]
        cosT, sinT = _rope_tables(hf * 1024)
        m = dict(shared)
        m.update({"x_all": xa, "x_smp": np.ascontiguousarray(xs[c * NS:(c + 1) * NS, 0, :]),
                  "ck": np.ascontiguousarray(f(cache_win_k)[0, c * NS:(c + 1) * NS]), "cv": np.ascontiguousarray(f(cache_win_v)[0, c * NS:(c + 1) * NS]),
                  "sconv": np.ascontiguousarray(f(state_conv)[0, c * NS:(c + 1) * NS]), "sssm": np.ascontiguousarray(f(state_ssm)[0, c * NS:(c + 1) * NS]),
                  "cosT": cosT, "sinT": sinT,
                  "masks": np.ascontiguousarray(np.stack([m_std, m_std if hf == 1 else m_none], 1)),
                  "flag": np.full((128, 1), float(hf), np.float32)})
        in_maps.append(m)
    res = run_bass_kernel_spmd(nc, in_maps, core_ids=list(range(8))).results
    y_p = np.zeros((4, 2048, D), np.float32)
    for c in range(8):
        y_p[c // 2, (c % 2) * 1024:(c % 2 + 1) * 1024] = res[c]["y_own"]
    y_s = np.concatenate([res[c]["y_smp"] for c in range(8)], 0).reshape(128, 1, D)
    odd = [1, 3, 5, 7]
    wk = np.stack([res[c]["wk"].reshape(128, 8, 64) for c in odd], 0)[None]
    wv = np.stack([res[c]["wv"].reshape(128, 8, 64) for c in odd], 0)[None]
    cvp = np.stack([res[c]["convo"] for c in odd], 0)[None]
    ssp = np.stack([res[c]["ssmo"].reshape(64, 64, 128) for c in odd], 0)[None]
    wks = np.concatenate([res[c]["wks"].reshape(NS, 128, 8, 64) for c in range(8)], 0)[None]
    wvs = np.concatenate([res[c]["wvs"].reshape(NS, 128, 8, 64) for c in range(8)], 0)[None]
    cvs = np.concatenate([res[c]["convs"] for c in range(8)], 0)[None]
    sss = np.concatenate([res[c]["ssms"] for c in range(8)], 0)[None]
    return (y_p, y_s, wk, wv, cvp, ssp, wks, wvs, cvs, sss)
```
